# Optimizing a Trainium2 kernel written in Bass

```python
import math
import jax, jax.numpy as jnp
from jax import lax
import numpy as np

D_MODEL = 2048
BATCH = 4
SEQ = 2048
DEPTH = 2
DEC_BATCH = 128
DEC_SEQ = 4
PAST_LEN = 16384
PAGE_SIZE = 128

N_MIXERS = 2
N_CONV_LAYERS = (DEPTH + 1) // 2
N_SSM_LAYERS = DEPTH // 2
N_META = 16
D_CONV = D_MODEL
CONV_WIDTH = 3
SSM_GROUP_WIDTH = 16
SSM_GROUPS = D_MODEL // SSM_GROUP_WIDTH
SSM_STATE = 64
D_FF = 4 * D_MODEL
NORM_EPS = 1e-6

kernel_name = "hybrid_shortconv_s5_decoder_step"


def rmsnorm(x, g):
    x32 = x.astype(jnp.float32)
    y = x32 * lax.rsqrt(jnp.mean(x32 * x32, axis=-1, keepdims=True) + NORM_EPS)
    return (y * g.astype(jnp.float32)).astype(x.dtype)


def short_conv_mixer(u, conv_past, w_in, w_conv, w_out):
    L = u.shape[1]
    proj = u @ w_in
    b_gate, c_gate, h = jnp.split(proj, 3, axis=-1)
    v = (c_gate * h).astype(conv_past.dtype)
    vp = jnp.concatenate([conv_past, v], axis=1)
    conv = (w_conv[0] * vp[:, 0:L] + w_conv[1] * vp[:, 1:L + 1]
            + w_conv[2] * vp[:, 2:L + 2])
    out = (b_gate * conv) @ w_out
    new_past = vp[:, L:]
    return out.astype(u.dtype), new_past


def _scan_combine(e1, e2):
    a1r, a1i, b1r, b1i = e1
    a2r, a2i, b2r, b2i = e2
    ar = a2r * a1r - a2i * a1i
    ai = a2r * a1i + a2i * a1r
    br = a2r * b1r - a2i * b1i + b2r
    bi = a2r * b1i + a2i * b1r + b2i
    return (ar, ai, br, bi)


def s5_mixer(u, h0_re, h0_im, lam_re, lam_im, log_step, b_re, b_im, c_re, c_im, d_skip, w_a, w_b):
    bsz, L, _ = u.shape
    u32 = u.astype(jnp.float32).reshape(bsz, L, SSM_GROUPS, SSM_GROUP_WIDTH)
    dt = jnp.exp(log_step.astype(jnp.float32))[:, None]
    lr = lam_re.astype(jnp.float32)
    li = lam_im.astype(jnp.float32)
    mag = jnp.exp(lr * dt)
    abar_re = mag * jnp.cos(li * dt)
    abar_im = mag * jnp.sin(li * dt)
    nr = abar_re - 1.0
    ni = abar_im
    den = lr * lr + li * li
    q_re = (nr * lr + ni * li) / den
    q_im = (ni * lr - nr * li) / den
    br32 = b_re.astype(jnp.float32)
    bi32 = b_im.astype(jnp.float32)
    bbar_re = q_re[..., None] * br32 - q_im[..., None] * bi32
    bbar_im = q_re[..., None] * bi32 + q_im[..., None] * br32
    bu_re = jnp.einsum('btgc,gpc->btgp', u32, bbar_re)
    bu_im = jnp.einsum('btgc,gpc->btgp', u32, bbar_im)
    a_re = jnp.broadcast_to(abar_re, (1, L, SSM_GROUPS, SSM_STATE))
    a_im = jnp.broadcast_to(abar_im, (1, L, SSM_GROUPS, SSM_STATE))
    cum_ar, cum_ai, cum_br, cum_bi = lax.associative_scan(
        _scan_combine, (a_re, a_im, bu_re, bu_im), axis=1)
    h0r = h0_re.astype(jnp.float32)[:, None]
    h0i = h0_im.astype(jnp.float32)[:, None]
    h_re = cum_ar * h0r - cum_ai * h0i + cum_br
    h_im = cum_ar * h0i + cum_ai * h0r + cum_bi
    y = (jnp.einsum('gcp,btgp->btgc', c_re.astype(jnp.float32), h_re)
         - jnp.einsum('gcp,btgp->btgc', c_im.astype(jnp.float32), h_im)
         + d_skip.astype(jnp.float32).reshape(SSM_GROUPS, SSM_GROUP_WIDTH) * u32)
    y = y.reshape(bsz, L, D_MODEL)
    g = jax.nn.gelu(y)
    out = (g @ w_a.astype(jnp.float32)) * jax.nn.sigmoid(g @ w_b.astype(jnp.float32))
    return out.astype(u.dtype), h_re[:, -1].astype(h0_re.dtype), h_im[:, -1].astype(h0_im.dtype)


def sq_relu_mlp(x, w_up, w_down):
    h = jax.nn.relu(x @ w_up)
    return (h * h) @ w_down


def trunk(h, conv_past, ssm_re0, ssm_im0, norm_mixer, norm_mlp, norm_final,
          conv_w_in, conv_w, conv_w_out,
          ssm_lambda_re, ssm_lambda_im, ssm_log_step, ssm_b_re, ssm_b_im,
          ssm_c_re, ssm_c_im, ssm_d, ssm_glu_w_a, ssm_glu_w_b,
          mlp_w_up, mlp_w_down):
    new_conv, new_re, new_im = [], [], []
    for i in range(DEPTH):
        j = i // N_MIXERS
        hn = rmsnorm(h, norm_mixer[i])
        if i % N_MIXERS == 0:
            m, st = short_conv_mixer(hn, conv_past[j], conv_w_in[j], conv_w[j], conv_w_out[j])
            new_conv.append(st)
        else:
            m, sr, si = s5_mixer(hn, ssm_re0[j], ssm_im0[j], ssm_lambda_re[j], ssm_lambda_im[j],
                                 ssm_log_step[j], ssm_b_re[j], ssm_b_im[j], ssm_c_re[j], ssm_c_im[j],
                                 ssm_d[j], ssm_glu_w_a[j], ssm_glu_w_b[j])
            new_re.append(sr)
            new_im.append(si)
        h = h + m
        h = h + sq_relu_mlp(rmsnorm(h, norm_mlp[i]), mlp_w_up[i], mlp_w_down[i])
    h = rmsnorm(h, norm_final)
    return h, jnp.stack(new_conv), jnp.stack(new_re), jnp.stack(new_im)


def setup_inputs(seed: int = 0) -> dict:
    key = jax.random.key(seed)
    ks = jax.random.split(key, 32)
    f32 = jnp.float32
    nrm = lambda k, s, sc: (jax.random.normal(k, s, f32) * sc)
    x_prompt = nrm(ks[0], (BATCH, SEQ, D_MODEL), 1.0)
    x_sample = nrm(ks[1], (DEC_BATCH, DEC_SEQ, D_MODEL), 1.0)
    state_conv = nrm(ks[2], (N_CONV_LAYERS, DEC_BATCH, CONV_WIDTH - 1, D_CONV), 1.0)
    state_ssm_re = nrm(ks[3], (N_SSM_LAYERS, DEC_BATCH, SSM_GROUPS, SSM_STATE), 0.3)
    state_ssm_im = nrm(ks[4], (N_SSM_LAYERS, DEC_BATCH, SSM_GROUPS, SSM_STATE), 0.3)
    meta_tokens = nrm(ks[5], (N_META, D_MODEL), 1.0)
    norm_mixer = 1.0 + nrm(ks[6], (DEPTH, D_MODEL), 0.02)
    norm_mlp = 1.0 + nrm(ks[7], (DEPTH, D_MODEL), 0.02)
    norm_final = 1.0 + nrm(ks[8], (D_MODEL,), 0.02)
    conv_w_in = nrm(ks[9], (N_CONV_LAYERS, D_MODEL, 3 * D_CONV), D_MODEL ** -0.5)
    conv_w = nrm(ks[10], (N_CONV_LAYERS, CONV_WIDTH, D_CONV), CONV_WIDTH ** -0.5)
    conv_w_out = nrm(ks[11], (N_CONV_LAYERS, D_CONV, D_MODEL), D_CONV ** -0.5)
    n_idx = jnp.arange(SSM_STATE, dtype=f32)
    ssm_lambda_re = -0.5 + nrm(ks[12], (N_SSM_LAYERS, SSM_GROUPS, SSM_STATE), 0.01)
    ssm_lambda_im = math.pi * n_idx + nrm(ks[13], (N_SSM_LAYERS, SSM_GROUPS, SSM_STATE), 0.01)
    ssm_log_step = jax.random.uniform(ks[14], (N_SSM_LAYERS, SSM_GROUPS), f32,
                                      minval=math.log(1e-3), maxval=math.log(1e-1))
    bsc = (2.0 * SSM_GROUP_WIDTH) ** -0.5
    ssm_b_re = nrm(ks[15], (N_SSM_LAYERS, SSM_GROUPS, SSM_STATE, SSM_GROUP_WIDTH), bsc)
    ssm_b_im = nrm(ks[16], (N_SSM_LAYERS, SSM_GROUPS, SSM_STATE, SSM_GROUP_WIDTH), bsc)
    csc = 0.5
    ssm_c_re = nrm(ks[17], (N_SSM_LAYERS, SSM_GROUPS, SSM_GROUP_WIDTH, SSM_STATE), csc)
    ssm_c_im = nrm(ks[18], (N_SSM_LAYERS, SSM_GROUPS, SSM_GROUP_WIDTH, SSM_STATE), csc)
    ssm_d = nrm(ks[19], (N_SSM_LAYERS, D_MODEL), 1.0)
    ssm_glu_w_a = nrm(ks[20], (N_SSM_LAYERS, D_MODEL, D_MODEL), D_MODEL ** -0.5)
    ssm_glu_w_b = nrm(ks[21], (N_SSM_LAYERS, D_MODEL, D_MODEL), D_MODEL ** -0.5)
    mlp_w_up = nrm(ks[22], (DEPTH, D_MODEL, D_FF), D_MODEL ** -0.5)
    mlp_w_down = nrm(ks[23], (DEPTH, D_FF, D_MODEL), D_FF ** -0.5)
    return {
        "x_prompt": x_prompt, "x_sample": x_sample,
        "state_conv": state_conv, "state_ssm_re": state_ssm_re, "state_ssm_im": state_ssm_im,
        "meta_tokens": meta_tokens,
        "norm_mixer": norm_mixer, "norm_mlp": norm_mlp, "norm_final": norm_final,
        "conv_w_in": conv_w_in, "conv_w": conv_w, "conv_w_out": conv_w_out,
        "ssm_lambda_re": ssm_lambda_re, "ssm_lambda_im": ssm_lambda_im, "ssm_log_step": ssm_log_step,
        "ssm_b_re": ssm_b_re, "ssm_b_im": ssm_b_im, "ssm_c_re": ssm_c_re, "ssm_c_im": ssm_c_im,
        "ssm_d": ssm_d, "ssm_glu_w_a": ssm_glu_w_a, "ssm_glu_w_b": ssm_glu_w_b,
        "mlp_w_up": mlp_w_up, "mlp_w_down": mlp_w_down,
    }


def reference(x_prompt, x_sample, state_conv, state_ssm_re, state_ssm_im, meta_tokens,
              norm_mixer, norm_mlp, norm_final, conv_w_in, conv_w, conv_w_out,
              ssm_lambda_re, ssm_lambda_im, ssm_log_step, ssm_b_re, ssm_b_im,
              ssm_c_re, ssm_c_im, ssm_d, ssm_glu_w_a, ssm_glu_w_b, mlp_w_up, mlp_w_down):
    bp = x_prompt.shape[0]
    meta = jnp.broadcast_to(meta_tokens.astype(x_prompt.dtype)[None], (bp, N_META, D_MODEL))
    hp = jnp.concatenate([meta, x_prompt], axis=1)
    conv0 = jnp.zeros((N_CONV_LAYERS, bp, CONV_WIDTH - 1, D_CONV), state_conv.dtype)
    ssm0_re = jnp.zeros((N_SSM_LAYERS, bp, SSM_GROUPS, SSM_STATE), state_ssm_re.dtype)
    ssm0_im = jnp.zeros((N_SSM_LAYERS, bp, SSM_GROUPS, SSM_STATE), state_ssm_im.dtype)
    yp, conv_p, re_p, im_p = trunk(hp, conv0, ssm0_re, ssm0_im, norm_mixer, norm_mlp, norm_final,
                                   conv_w_in, conv_w, conv_w_out,
                                   ssm_lambda_re, ssm_lambda_im, ssm_log_step, ssm_b_re, ssm_b_im,
                                   ssm_c_re, ssm_c_im, ssm_d, ssm_glu_w_a, ssm_glu_w_b,
                                   mlp_w_up, mlp_w_down)
    y_prompt = yp[:, N_META:]
    y_sample, conv_s, re_s, im_s = trunk(x_sample, state_conv, state_ssm_re, state_ssm_im,
                                         norm_mixer, norm_mlp, norm_final,
                                         conv_w_in, conv_w, conv_w_out,
                                         ssm_lambda_re, ssm_lambda_im, ssm_log_step, ssm_b_re, ssm_b_im,
                                         ssm_c_re, ssm_c_im, ssm_d, ssm_glu_w_a, ssm_glu_w_b,
                                         mlp_w_up, mlp_w_down)
    return (y_prompt, y_sample, conv_p, re_p, im_p, conv_s, re_s, im_s)
```

```python
import math
import os
import numpy as np
import concourse.bass as bass
import concourse.mybir as mybir
from concourse.bass_utils import run_bass_kernel_spmd

F32 = mybir.dt.float32
BF16 = mybir.dt.bfloat16
I32 = mybir.dt.int32
AF = mybir.ActivationFunctionType
ALU = mybir.AluOpType

D = 2048
NCH = 16
NPT = 516
NTILE_HALF = 2
HALF = NPT * NTILE_HALF
NSS = 16
NSAMP = NSS * 4
TTMAX = NPT + NSAMP
VW = 2 + NPT + NSS * 6
TC = 16
NW = 4
EPS = 1e-6
NTOK = 2 * HALF + NSAMP
ENGS = ["pe", "act", "dve", "pool", "sp"]


SEM_LIMIT = 1000


class Sem:
    def __init__(self, name, owner=None):
        self.name = name
        self.h = None
        self.count = 0
        self.owner = owner


class Prog:
    def __init__(self):
        self.prog = {e: [] for e in ENGS}
        self.waited = {e: {} for e in ENGS}
        self.esem = {e: Sem("e_" + e, e) for e in ENGS}
        self.nrot = 0
        self.sems = list(self.esem.values())
        self.lastw = {}
        self.readers = {}
        self.guard = {}

    def new_sem(self, name):
        s = Sem(name)
        self.sems.append(s)
        return s

    def _wait(self, eng, ev):
        if ev is None:
            return
        sem, val = ev
        if eng == "pe" and sem.owner == "pe":
            return
        w = self.waited[eng]
        if w.get(sem.name, 0) >= val:
            return
        w[sem.name] = val
        self.prog[eng].append(lambda h, sem=sem, val=val: h.wait_ge(sem.h, val))

    def emit(self, eng, fn, reads=(), writes=(), sem=None, inc=1, signal=True, extra=()):
        for ev in extra:
            self._wait(eng, ev)
        for b in reads:
            self._wait(eng, self.lastw.get(b))
            for ev in self.guard.get(b[0], ()):
                self._wait(eng, ev)
        for b in writes:
            self._wait(eng, self.lastw.get(b))
            for ev in self.readers.get(b, ()):
                self._wait(eng, ev)
            for ev in self.guard.get(b[0], ()):
                self._wait(eng, ev)
        if sem is None:
            if self.esem[eng].count >= SEM_LIMIT:
                self.nrot += 1
                ns = Sem(f"e_{eng}_{self.nrot}", eng)
                self.sems.append(ns)
                self.esem[eng] = ns
            s = self.esem[eng]
        else:
            s = sem
        if signal:
            s.count += inc
            ev = (s, s.count)
            self.prog[eng].append(lambda h, fn=fn, s=s, inc=inc: fn(h).then_inc(s.h, inc))
        else:
            ev = (s, s.count + inc)
            self.prog[eng].append(lambda h, fn=fn: fn(h))
        for b in reads:
            self.readers.setdefault(b, []).append(ev)
        for b in writes:
            self.lastw[b] = ev
            self.readers[b] = []
        return ev

    def all_events(self):
        return [(s, s.count) for s in self.sems if s.count > 0]

    def barrier(self):
        evs = self.all_events()
        for e in ENGS:
            for ev in evs:
                self._wait(e, ev)


def build(stage=99):
    nc = bass.Bass("TRN2", target_bir_lowering=False)
    P = Prog()

    def din(name, shape, dt=F32):
        return nc.dram_tensor(name, list(shape), dt, kind="ExternalInput").ap()

    def dout(name, shape, dt=F32):
        return nc.dram_tensor(name, list(shape), dt, kind="ExternalOutput").ap()

    xin = din("xin", [D, NTOK])
    convs_in = din("convs_in", [128, NCH, NSS, 2])
    ssms_in = din("ssms_in", [128, NSS, 2, 64])
    gam = din("gam", [128, 5, NCH])
    convw = din("convw", [128, NCH, 3])
    w_in = din("w_in", [48, 128, 2048])
    w_out = din("w_out", [16, 128, 2048])
    w_up = din("w_up", [2, 64, 128, 2048])
    w_dn = din("w_dn", [2, 64, 128, 2048])
    w_ga = din("w_ga", [16, 128, 2048])
    w_gb = din("w_gb", [16, 128, 2048])
    lam_s = din("lam_s", [128, 3, 64])
    lam_t = din("lam_t", [128, 3, 128])
    dscr = nc.dram_tensor("dscr", [64, 4, 128], F32, kind="Internal").ap()
    bp = din("bp", [128, 2, 2048])
    cp = din("cp", [128, 2, 64 * 32])
    dd = din("dd", [128, NCH * 128])

    y_out = dout("y_out", [D, HALF + NSAMP])
    convp_out = dout("convp_out", [128, NCH, 2])
    ssmp_out = dout("ssmp_out", [128, 2, 64])
    convs_out = dout("convs_out", [128, NCH, NSS, 2])
    ssms_out = dout("ssms_out", [128, NSS, 2, 64])

    xin_v = xin.rearrange("(c p) t -> p c t", p=128)
    yout_v = y_out.rearrange("(c p) t -> p c t", p=128)

    RM = 9600
    from contextlib import ExitStack
    with ExitStack() as es:
        def sb(name, shape, dt=F32):
            return es.enter_context(nc.sbuf_tensor(name, list(shape), dt))

        h = sb("h", [128, NCH, TTMAX])
        xn = sb("xn", [128, NCH, TTMAX], BF16)
        wsl = sb("wsl", [128, NW, 2048], BF16)
        rmix = sb("rmix", [128, RM])
        rstd = sb("rstd", [128, TTMAX])
        sqb = sb("sqb", [128, 2, TTMAX], BF16)
        ones = sb("ones", [128, 128], BF16)
        gam_t = sb("gam_t", [128, 5, NCH])
        convw_t = sb("convw_t", [128, NCH, 3])
        convs_t = sb("convs_t", [128, NCH, NSS, 2])
        carry_v = sb("carry_v", [128, NCH, 2])
        xcarry = sb("xcarry", [128, 2, 64])
        ssms_t = sb("ssms_t", [128, NSS, 2, 64])
        m1 = sb("m1", [128, 2, 64])
        m2 = sb("m2", [128, 2, 64])
        bbt = sb("bbt", [128, 64, 2, 128], BF16)
        abt = sb("abt", [128, 64, 2, 128], BF16)
        cpad = sb("cpad", [128, 2, 64 * 32], BF16)
        ddg = sb("ddg", [128, NCH, 128], BF16)
        ps = es.enter_context(nc.psum_tensor("ps", [128, 8, 512], F32))

        def rm_f32(off, n):
            return rmix[:, off:off + n]

        def rm_bf16(off, n):
            return rmix[:, off:off + n].bitcast(BF16)

        vbuf = [rm_f32(0, VW), rm_f32(VW, VW)]
        tmpc = rm_f32(2 * VW, TTMAX)
        convt = rm_f32(2 * VW + TTMAX, VW)
        g0 = rm_bf16(3 * VW + TTMAX, NCH * TTMAX // 2).rearrange("p (c t) -> p c t", c=NCH)
        HQW = NCH * TTMAX // 2
        hq = [rm_bf16(0, HQW).rearrange("p (c t) -> p c t", c=NCH),
              rm_bf16(HQW, HQW).rearrange("p (c t) -> p c t", c=NCH)]
        relu_t = rm_bf16(2 * HQW, TTMAX // 2 + 2)
        glu_t = [rm_f32(0, TTMAX), rm_f32(TTMAX, TTMAX)]
        BUW = 128 * TC
        v4 = lambda off, t_: rm_f32(off, 128 * t_).rearrange("p (r g t) -> p r g t", r=2, g=64)
        bu = [v4(0, TC), v4(BUW, TC)]
        TRW = 128 * (TC + 2)
        traj = v4(2 * BUW, TC + 2)
        trajb = rm_bf16(2 * BUW + TRW, BUW // 2).rearrange("p (r g t) -> p r g t", r=2, g=64)
        o1 = 2 * BUW + TRW + BUW // 2
        t1b = v4(o1, 2)
        t2b = v4(o1 + 256, 2)
        c1b = rm_f32(o1 + 512, 128).rearrange("p (r g) -> p r g", r=2)
        c2b = rm_f32(o1 + 640, 128).rearrange("p (r g) -> p r g", r=2)
        o2 = o1 + 768
        GW = 16 * TC
        gel = [rm_f32(o2 + i * GW, GW) for i in range(3)]
        assert o2 + 3 * GW <= RM and 2 * HQW + TTMAX // 2 + 2 <= RM and 3 * VW + TTMAX + NCH * TTMAX // 2 <= RM
        yfin = rm_f32(0, NCH * TTMAX).rearrange("p (c t) -> p c t", c=NCH) if NCH * TTMAX <= RM else None
        assert yfin is not None

        RMN = ("w", "tmpw", "cc", "g0", "vbuf", "tmpc", "convt", "hq", "relu", "glu", "bu", "traj", "trajb", "tt", "gel", "yfin")

        def rm_switch():
            evs = P.all_events()
            for n in RMN:
                P.guard[n] = evs

        wsem = [P.new_sem(f"w{i}") for i in range(NW)]
        wcount = [0]

        def load_w(src_ap):
            k = wcount[0]
            wcount[0] += 1
            slot = k % NW
            if wsem[slot].count >= SEM_LIMIT:
                wsem[slot] = P.new_sem(f"w{slot}_{k}")
            P.emit("pool", lambda hh, slot=slot, src_ap=src_ap: hh.dma_start(out=wsl[:, slot, :], in_=src_ap),
                   writes=[("w", slot)], sem=wsem[slot], inc=16)
            return slot

        def wv(slot):
            return wsl[:, slot, :].rearrange("p (k m) -> p k m", k=16)

        ldx = P.new_sem("ldx")
        st = P.new_sem("st")

        def load_small(dst, src, key):
            P.emit("sp", lambda hh: hh.dma_start(out=dst, in_=src), writes=[key], sem=P.new_sem("ld_" + key[0]), inc=16)

        load_small(gam_t[:], gam, ("gam",))
        load_small(convw_t[:], convw, ("convw",))
        load_small(convs_t[:], convs_in, ("convs_t",))
        load_small(ssms_t[:], ssms_in, ("ssms_t",))
        P.emit("dve", lambda hh: hh.memset(ones[:], 1.0), writes=[("ones",)])
        P.emit("dve", lambda hh: hh.memset(carry_v[:], 0.0), writes=[("carry_v",)])
        P.emit("dve", lambda hh: hh.memset(xcarry[:], 0.0), writes=[("xcarry",)])
        P.emit("pool", lambda hh: hh.dma_start(out=cpad[:, 0, :], in_=cp[:, 0, :]), writes=[("cpad", 0)], sem=P.new_sem("ldp0"), inc=16)
        P.emit("pool", lambda hh: hh.dma_start(out=cpad[:, 1, :], in_=cp[:, 1, :]), writes=[("cpad", 1)], sem=P.new_sem("ldp1"), inc=16)
        P.emit("pool", lambda hh: hh.dma_start(out=ddg[:].rearrange("p c m -> p (c m)"), in_=dd), writes=[("ddg",)], sem=P.new_sem("ldp2"), inc=16)
        P.emit("act", lambda hh: hh.activation(out=cpad[:, 1, :], in_=cpad[:, 1, :], func=AF.Copy, scale=-1.0),
               reads=[("cpad", 1)], writes=[("cpad", 1)])

        TWO_PI = 2.0 * math.pi

        rm_bump = [0]

        def rm_alloc(n):
            o = rm_bump[0]
            rm_bump[0] += n
            assert rm_bump[0] <= RM
            return rmix[:, o:o + n]

        def cplx_setup(tag, F, want_q):
            T = {}

            def t(n):
                T[n] = rm_alloc(F)
                return T[n]
            lam_t = rm_alloc(3 * F).rearrange("p (a f) -> p a f", a=3)
            lam_sem = P.new_sem("ld_lam_" + tag)
            lr, li, ls = lam_t[:, 0, :], lam_t[:, 1, :], lam_t[:, 2, :]
            K = lambda n: (tag, n)
            dt_ = t("dt"); mag = t("mag"); th = t("th"); kf = t("kf")
            r = t("r"); msk = t("msk"); sn = t("sn"); cs = t("cs"); are = t("are"); aim = t("aim")
            if want_q:
                den = t("den"); nr = t("nr"); qre = t("qre"); qim = t("qim"); tq = t("tq")
            fact = [1.0]
            for i_ in range(1, 20):
                fact.append(fact[-1] * i_)
            EXPC = [1.0 / fact[i_] for i_ in range(11)]
            SINC = [((-1.0) ** i_) / fact[2 * i_ + 1] for i_ in range(8)]
            COSC = [((-1.0) ** i_) / fact[2 * i_] for i_ in range(9)]

            def poly(dst, w, coeffs, kd, kw):
                n_ = len(coeffs) - 1
                P.emit("dve", lambda hh: hh.tensor_scalar(out=dst, in0=w, scalar1=coeffs[n_], scalar2=None, op0=ALU.mult), reads=[K(kw)], writes=[K(kd)])
                for i_ in range(n_ - 1, 0, -1):
                    P.emit("dve", lambda hh, c_=coeffs[i_]: hh.scalar_tensor_tensor(out=dst, in0=dst, scalar=c_, in1=w, op0=ALU.add, op1=ALU.mult),
                           reads=[K(kd), K(kw)], writes=[K(kd)])
                P.emit("dve", lambda hh: hh.tensor_scalar(out=dst, in0=dst, scalar1=coeffs[0], scalar2=None, op0=ALU.add), reads=[K(kd)], writes=[K(kd)])

            def tt(out, a_, b_, op, ko, ka, kb):
                P.emit("dve", lambda hh: hh.tensor_tensor(out=out, in0=a_, in1=b_, op=op), reads=[K(ka), K(kb)], writes=[K(ko)])

            def run(lam_ap):
                P.emit("sp", lambda hh: hh.dma_start(out=lam_t, in_=lam_ap), writes=[K("lam")], sem=lam_sem, inc=16)
                P.emit("dve", lambda hh: hh.tensor_scalar(out=kf, in0=ls, scalar1=1.0 / 16.0, scalar2=None, op0=ALU.mult), reads=[K("lam")], writes=[K("kf")])
                poly(dt_, kf, EXPC, "dt", "kf")
                for _ in range(4):
                    tt(dt_, dt_, dt_, ALU.mult, "dt", "dt", "dt")
                tt(kf, lr, dt_, ALU.mult, "kf", "lam", "dt")
                poly(mag, kf, EXPC[:9], "mag", "kf")
                tt(th, li, dt_, ALU.mult, "th", "lam", "dt")
                P.emit("dve", lambda hh: hh.tensor_scalar(out=th, in0=th, scalar1=1.0 / 16.0, scalar2=None, op0=ALU.mult), reads=[K("th")], writes=[K("th")])
                tt(r, th, th, ALU.mult, "r", "th", "th")
                poly(sn, r, SINC, "sn", "r")
                tt(sn, sn, th, ALU.mult, "sn", "sn", "th")
                poly(cs, r, COSC, "cs", "r")
                for _ in range(4):
                    tt(msk, cs, sn, ALU.mult, "msk", "cs", "sn")
                    tt(cs, cs, cs, ALU.mult, "cs", "cs", "cs")
                    tt(sn, sn, sn, ALU.mult, "sn", "sn", "sn")
                    tt(cs, cs, sn, ALU.subtract, "cs", "cs", "sn")
                    P.emit("dve", lambda hh: hh.tensor_scalar(out=sn, in0=msk, scalar1=2.0, scalar2=None, op0=ALU.mult), reads=[K("msk")], writes=[K("sn")])
                P.emit("dve", lambda hh: hh.tensor_tensor(out=are, in0=mag, in1=cs, op=ALU.mult), reads=[K("mag"), K("cs")], writes=[K("are")])
                P.emit("dve", lambda hh: hh.tensor_tensor(out=aim, in0=mag, in1=sn, op=ALU.mult), reads=[K("mag"), K("sn")], writes=[K("aim")])
                if want_q:
                    P.emit("dve", lambda hh: hh.tensor_tensor(out=den, in0=lr, in1=lr, op=ALU.mult), reads=[K("lam")], writes=[K("den")])
                    P.emit("dve", lambda hh: hh.tensor_tensor(out=tq, in0=li, in1=li, op=ALU.mult), reads=[K("lam")], writes=[K("tq")])
                    P.emit("dve", lambda hh: hh.tensor_tensor(out=den, in0=den, in1=tq, op=ALU.add), reads=[K("den"), K("tq")], writes=[K("den")])
                    P.emit("dve", lambda hh: hh.reciprocal(out=den, in_=den), reads=[K("den")], writes=[K("den")])
                    P.emit("dve", lambda hh: hh.tensor_scalar(out=nr, in0=are, scalar1=-1.0, scalar2=None, op0=ALU.add), reads=[K("are")], writes=[K("nr")])
                    P.emit("dve", lambda hh: hh.tensor_tensor(out=qre, in0=nr, in1=lr, op=ALU.mult), reads=[K("nr"), K("lam")], writes=[K("qre")])
                    P.emit("dve", lambda hh: hh.tensor_tensor(out=tq, in0=aim, in1=li, op=ALU.mult), reads=[K("aim"), K("lam")], writes=[K("tq")])
                    P.emit("dve", lambda hh: hh.tensor_tensor(out=qre, in0=qre, in1=tq, op=ALU.add), reads=[K("qre"), K("tq")], writes=[K("qre")])
                    P.emit("dve", lambda hh: hh.tensor_tensor(out=qre, in0=qre, in1=den, op=ALU.mult), reads=[K("qre"), K("den")], writes=[K("qre")])
                    P.emit("dve", lambda hh: hh.tensor_tensor(out=qim, in0=aim, in1=lr, op=ALU.mult), reads=[K("aim"), K("lam")], writes=[K("qim")])
                    P.emit("dve", lambda hh: hh.tensor_tensor(out=tq, in0=nr, in1=li, op=ALU.mult), reads=[K("nr"), K("lam")], writes=[K("tq")])
                    P.emit("dve", lambda hh: hh.tensor_tensor(out=qim, in0=qim, in1=tq, op=ALU.subtract), reads=[K("qim"), K("tq")], writes=[K("qim")])
                    P.emit("dve", lambda hh: hh.tensor_tensor(out=qim, in0=qim, in1=den, op=ALU.mult), reads=[K("qim"), K("den")], writes=[K("qim")])
            return T, run

        Ts, run_s = cplx_setup("ss", 64, False)
        run_s(lam_s)
        K = lambda n: ("ss", n)
        P.emit("dve", lambda hh: hh.tensor_copy(out=m1[:, 0, :], in_=Ts["are"]), reads=[K("are")], writes=[("m1",)])
        P.emit("dve", lambda hh: hh.tensor_copy(out=m1[:, 1, :], in_=Ts["are"]), reads=[K("are")], writes=[("m1",)])
        P.emit("dve", lambda hh: hh.tensor_copy(out=m2[:, 1, :], in_=Ts["aim"]), reads=[K("aim")], writes=[("m2",)])
        P.emit("dve", lambda hh: hh.tensor_scalar(out=m2[:, 0, :], in0=Ts["aim"], scalar1=-1.0, scalar2=None, op0=ALU.mult), reads=[K("aim")], writes=[("m2",)])
        m1x = sb("m1x", [128, 2, 64, 2])
        m2x = sb("m2x", [128, 2, 64, 2])
        a2re = rm_alloc(64)
        a2im = rm_alloc(64)
        a2t = rm_alloc(64)
        P.emit("dve", lambda hh: hh.tensor_tensor(out=a2re, in0=Ts["are"], in1=Ts["are"], op=ALU.mult), reads=[K("are")], writes=[("a2", 0)])
        P.emit("dve", lambda hh: hh.tensor_tensor(out=a2t, in0=Ts["aim"], in1=Ts["aim"], op=ALU.mult), reads=[K("aim")], writes=[("a2", 2)])
        P.emit("dve", lambda hh: hh.tensor_tensor(out=a2re, in0=a2re, in1=a2t, op=ALU.subtract), reads=[("a2", 0), ("a2", 2)], writes=[("a2", 0)])
        P.emit("dve", lambda hh: hh.tensor_tensor(out=a2im, in0=Ts["are"], in1=Ts["aim"], op=ALU.mult), reads=[K("are"), K("aim")], writes=[("a2", 1)])
        for ri_ in range(2):
            for ph_ in range(2):
                P.emit("dve", lambda hh, ri_=ri_, ph_=ph_: hh.tensor_copy(out=m1x[:, ri_, :, ph_], in_=a2re), reads=[("a2", 0)], writes=[("m1x",)])
                P.emit("dve", lambda hh, ri_=ri_, ph_=ph_: hh.tensor_scalar(out=m2x[:, ri_, :, ph_], in0=a2im, scalar1=(-2.0 if ri_ == 0 else 2.0), scalar2=None, op0=ALU.mult),
                       reads=[("a2", 1)], writes=[("m2x",)])
        FB = 256
        NQB = FB // 128
        bbk = rm_alloc(NQB * 2 * 128).rearrange("p (c r m) -> p c r m", c=NQB, r=2)
        abk = rm_alloc(NQB * 2 * 128).rearrange("p (c r m) -> p c r m", c=NQB, r=2)
        P.emit("dve", lambda hh: hh.memset(bbt[:].rearrange("p a r m -> p (a r m)"), 0.0), writes=[("bbtp", 0), ("bbtp", 1)])
        P.emit("dve", lambda hh: hh.memset(abt[:].rearrange("p a r m -> p (a r m)"), 0.0), writes=[("abtp", 0), ("abtp", 1)])
        Tt_, run_t = cplx_setup("st", 128, True)
        run_t(lam_t)
        t4 = rm_alloc(4 * 128).rearrange("p (k m) -> p k m", k=4)
        for k_, nm_ in enumerate(("are", "aim", "qre", "qim")):
            P.emit("dve", lambda hh, k_=k_, nm_=nm_: hh.tensor_copy(out=t4[:, k_, :], in_=Tt_[nm_]), reads=[("st", nm_)], writes=[("t4",)])
        P.emit("sp", lambda hh: hh.dma_start(out=dscr, in_=t4[0:64, :, :]), reads=[("t4",)], writes=[("dscr",)], sem=P.new_sem("st_dscr"), inc=16)
        cm = rm_alloc(4 * FB).rearrange("p (k f) -> p k f", k=4)
        cm_sem = P.new_sem("ld_cm")
        dscr_v = dscr.rearrange("(q f) k m -> f k q m", f=4)
        bp_t = rm_alloc(2 * FB).rearrange("p (a f) -> p a f", a=2)
        bp_sem = P.new_sem("ld_bp")
        K = lambda n: ("sc", n)
        u1 = rm_alloc(FB); u2 = rm_alloc(FB)
        qre, qim = cm[:, 2, :], cm[:, 3, :]
        arc, aic = cm[:, 0, :], cm[:, 1, :]
        v3 = lambda a: a.rearrange("p (c m) -> p c m", m=128)
        bbt_v = bbt[:].rearrange("p (q f) r m -> p q f r m", f=4)
        abt_v = abt[:].rearrange("p (q f) r m -> p q f r m", f=4)
        for blk in range(2048 // FB):
            f0 = blk * FB
            q0 = f0 // 128
            for g4 in range(4):
                for k_ in range(4):
                    P.emit("sp", lambda hh, g4=g4, q0=q0, k_=k_: hh.dma_start(
                        out=cm[32 * g4:32 * g4 + 32, k_, :].rearrange("p (q m) -> p q m", m=128),
                        in_=dscr_v[g4, k_, q0:q0 + NQB, :].partition_broadcast(32)),
                        reads=[("dscr",)], writes=[("sc", "are"), ("sc", "aim"), ("sc", "qre"), ("sc", "qim")], sem=cm_sem, inc=16)
            P.emit("sp", lambda hh, f0=f0: hh.dma_start(out=bp_t, in_=bp[:, :, f0:f0 + FB]), writes=[("bp_t",)], sem=bp_sem, inc=16)
            b_re, b_im = bbk[:, :, 0, :], bbk[:, :, 1, :]
            a_re, a_im = abk[:, :, 0, :], abk[:, :, 1, :]
            P.emit("dve", lambda hh: hh.tensor_tensor(out=u1, in0=qre, in1=bp_t[:, 0, :], op=ALU.mult), reads=[K("qre"), ("bp_t",), K("r")], writes=[K("r")])
            P.emit("dve", lambda hh: hh.tensor_tensor(out=u2, in0=qim, in1=bp_t[:, 1, :], op=ALU.mult), reads=[K("qim"), ("bp_t",), K("msk")], writes=[K("msk")])
            P.emit("dve", lambda hh: hh.tensor_tensor(out=b_re, in0=v3(u1), in1=v3(u2), op=ALU.subtract), reads=[K("r"), K("msk")], writes=[("bbk", 0)])
            P.emit("dve", lambda hh: hh.tensor_tensor(out=u1, in0=qre, in1=bp_t[:, 1, :], op=ALU.mult), reads=[K("qre"), ("bp_t",), K("r")], writes=[K("r")])
            P.emit("dve", lambda hh: hh.tensor_tensor(out=u2, in0=qim, in1=bp_t[:, 0, :], op=ALU.mult), reads=[K("qim"), ("bp_t",), K("msk")], writes=[K("msk")])
            P.emit("dve", lambda hh: hh.tensor_tensor(out=b_im, in0=v3(u1), in1=v3(u2), op=ALU.add), reads=[K("r"), K("msk")], writes=[("bbk", 1)])
            P.emit("dve", lambda hh: hh.tensor_tensor(out=v3(u1), in0=v3(arc), in1=b_re, op=ALU.mult), reads=[K("are"), ("bbk", 0), K("r")], writes=[K("r")])
            P.emit("dve", lambda hh: hh.tensor_tensor(out=v3(u2), in0=v3(aic), in1=b_im, op=ALU.mult), reads=[K("aim"), ("bbk", 1), K("msk")], writes=[K("msk")])
            P.emit("dve", lambda hh: hh.tensor_tensor(out=a_re, in0=v3(u1), in1=v3(u2), op=ALU.subtract), reads=[K("r"), K("msk")], writes=[("abk", 0)])
            P.emit("dve", lambda hh: hh.tensor_tensor(out=v3(u1), in0=v3(arc), in1=b_im, op=ALU.mult), reads=[K("are"), ("bbk", 1), K("r")], writes=[K("r")])
            P.emit("dve", lambda hh: hh.tensor_tensor(out=v3(u2), in0=v3(aic), in1=b_re, op=ALU.mult), reads=[K("aim"), ("bbk", 0), K("msk")], writes=[K("msk")])
            P.emit("dve", lambda hh: hh.tensor_tensor(out=a_im, in0=v3(u1), in1=v3(u2), op=ALU.add), reads=[K("r"), K("msk")], writes=[("abk", 1)])
            for g4 in range(4):
                for ri in range(2):
                    P.emit("dve", lambda hh, g4=g4, ri=ri, q0=q0: hh.tensor_copy(out=bbt_v[32 * g4:32 * g4 + 32, q0:q0 + NQB, g4, ri, :], in_=bbk[32 * g4:32 * g4 + 32, :, ri, :]),
                           reads=[("bbk", ri)], writes=[("bbtp", ri)])
                    P.emit("dve", lambda hh, g4=g4, ri=ri, q0=q0: hh.tensor_copy(out=abt_v[32 * g4:32 * g4 + 32, q0:q0 + NQB, g4, ri, :], in_=abk[32 * g4:32 * g4 + 32, :, ri, :]),
                           reads=[("abk", ri)], writes=[("abtp", ri)])
        P.barrier()

        bank_rr = [0]

        def next_banks(n):
            s = bank_rr[0] % 2
            bank_rr[0] += 1
            return [3 * s + i for i in range(n)]

        def mm_group(banks, subs, slots_rhs, extra_reads=()):
            total = sum(x[4] for x in slots_rhs)
            idx = 0
            last = None
            for (lf, wkey, rf, rkeys, nk) in slots_rhs:
                for k in range(nk):
                    for si, (c0, n) in enumerate(subs):
                        is_last = (idx == total - 1) and (si == len(subs) - 1)
                        last = P.emit("pe", lambda hh, b=banks[si], n=n, c0=c0, k=k, lf=lf, rf=rf, st_=(idx == 0), sp_=(idx == total - 1):
                                      hh.matmul(ps[:, b, 0:n], lhsT=lf(k), rhs=rf(k, c0, n), start=st_, stop=sp_),
                                      reads=[wkey] + list(rkeys(k)) + list(extra_reads), writes=[("ps", banks[si])], signal=is_last)
                    idx += 1
            return last

        def rmsnorm(subs, TT, gi, dst_fn, dst_key):
            bnk = [6, 7, 6][:len(subs)]
            for c in range(NCH):
                sl = c % 2
                P.emit("act", lambda hh, c=c, sl=sl: hh.activation(out=sqb[:, sl, 0:TT], in_=h[:, c, 0:TT], func=AF.Square),
                       reads=[("h", c)], writes=[("sqb", sl)])
                for si, (c0, n) in enumerate(subs):
                    bb = 6 + (si % 2) if len(subs) <= 2 else [6, 7, 5][si]
                    P.emit("pe", lambda hh, bb=bb, c=c, sl=sl, c0=c0, n=n: hh.matmul(ps[:, bb, 0:n], lhsT=ones[:], rhs=sqb[:, sl, c0:c0 + n],
                                                                                 start=(c == 0), stop=(c == NCH - 1)),
                           reads=[("ones",), ("sqb", sl)], writes=[("ps", bb)], signal=True)
            for si, (c0, n) in enumerate(subs):
                bb = 6 + (si % 2) if len(subs) <= 2 else [6, 7, 5][si]
                P.emit("act", lambda hh, bb=bb, c0=c0, n=n: hh.activation(out=rstd[:, c0:c0 + n], in_=ps[:, bb, 0:n], func=AF.Sqrt, bias=eps_t[:, 0:1], scale=1.0 / D),
                       reads=[("ps", bb), ("eps",)], writes=[("rstd", si)])
                P.emit("dve", lambda hh, c0=c0, n=n: hh.reciprocal(out=rstd[:, c0:c0 + n], in_=rstd[:, c0:c0 + n]),
                       reads=[("rstd", si)], writes=[("rstd", si)])
            for c in range(NCH):
                P.emit("dve", lambda hh, c=c: hh.scalar_tensor_tensor(out=dst_fn(c), in0=h[:, c, 0:TT], scalar=gam_t[:, gi, c:c + 1], in1=rstd[:, 0:TT],
                                                                    op0=ALU.mult, op1=ALU.mult),
                       reads=[("h", c), ("gam",)] + [("rstd", si) for si in range(len(subs))], writes=[(dst_key, c)])

        eps_t = sb("eps_t", [128, 1])
        P.emit("dve", lambda hh: hh.memset(eps_t[:], EPS), writes=[("eps",)])

        def h_add_psum(m, banks, subs):
            for si, (c0, n) in enumerate(subs):
                P.emit("dve", lambda hh, m=m, b=banks[si], c0=c0, n=n: hh.tensor_tensor(out=h[:, m, c0:c0 + n], in0=h[:, m, c0:c0 + n], in1=ps[:, b, 0:n], op=ALU.add),
                       reads=[("ps", banks[si]), ("h", m)], writes=[("h", m)])

        def mlp(layer, subs, TT):
            def up(qd):
                hb = hq[qd % 2]
                for m16 in range(16):
                    f = qd * 16 + m16
                    slot = load_w(w_up[layer, f])
                    banks = next_banks(len(subs))
                    mm_group(banks, subs, [(lambda k, slot=slot: wv(slot)[:, k, :], ("w", slot),
                                            lambda k, c0, n: xn[:, k, c0:c0 + n], lambda k: [("xn", k)], 16)])
                    for si, (c0, n) in enumerate(subs):
                        P.emit("act", lambda hh, b=banks[si], c0=c0, n=n: hh.activation(out=relu_t[:, 0:n], in_=ps[:, b, 0:n], func=AF.Relu),
                               reads=[("ps", banks[si])], writes=[("relu",)])
                        P.emit("dve", lambda hh, hb=hb, m16=m16, c0=c0, n=n: hh.tensor_tensor(out=hb[:, m16, c0:c0 + n], in0=relu_t[:, 0:n], in1=relu_t[:, 0:n], op=ALU.mult),
                               reads=[("relu",)], writes=[("hq", qd % 2, m16)])

            def down(qd):
                hb = hq[qd % 2]
                for m in range(16):
                    slot = load_w(w_dn[layer, qd * 16 + m])
                    banks = next_banks(len(subs))
                    mm_group(banks, subs, [(lambda k, slot=slot: wv(slot)[:, k, :], ("w", slot),
                                            lambda k, c0, n, hb=hb: hb[:, k, c0:c0 + n], lambda k, qd=qd: [("hq", qd % 2, k)], 16)])
                    h_add_psum(m, banks, subs)
            up(0)
            for qd in range(1, 4):
                up(qd)
                down(qd - 1)
            down(3)

        def conv_mixer(subs, TT, has_samp, tile_is_last_b):
            npv = 2 + NPT + (NSS * 6 if has_samp else 0)
            W = npv - 2
            for m in range(NCH):
                vb = vbuf[m % 2]
                vk = ("vbuf", m % 2)
                s_c = load_w(w_in[16 + m])
                s_h = load_w(w_in[32 + m])
                s_b = load_w(w_in[m])
                rk = lambda k: [("xn", k)]
                rf = lambda k, c0, n: xn[:, k, c0:c0 + n]
                bk_c = next_banks(len(subs))
                mm_group(bk_c, subs, [(lambda k, s=s_c: wv(s)[:, k, :], ("w", s_c), rf, rk, 16)])
                bk_h = next_banks(len(subs))
                mm_group(bk_h, subs, [(lambda k, s=s_h: wv(s)[:, k, :], ("w", s_h), rf, rk, 16)])
                P.emit("dve", lambda hh, vb=vb, m=m: hh.tensor_copy(out=vb[:, 0:2], in_=carry_v[:, m, :]), reads=[("carry_v",)], writes=[vk])
                if has_samp:
                    vs = vb[:, 2 + NPT:2 + NPT + NSS * 6].rearrange("p (s t) -> p s t", t=6)
                    P.emit("dve", lambda hh, vs=vs, m=m: hh.tensor_copy(out=vs[:, :, 0:2], in_=convs_t[:, m, :, :]), reads=[("convs_t",)], writes=[vk])
                for si, (c0, n) in enumerate(subs):
                    P.emit("act", lambda hh, b=bk_c[si], c0=c0, n=n: hh.activation(out=tmpc[:, c0:c0 + n], in_=ps[:, b, 0:n], func=AF.Copy),
                           reads=[("ps", bk_c[si])], writes=[("tmpc", si)])
                    if c0 < NPT:
                        P.emit("dve", lambda hh, vb=vb, b=bk_h[si], c0=c0, n=n: hh.tensor_tensor(out=vb[:, 2 + c0:2 + c0 + n], in0=tmpc[:, c0:c0 + n], in1=ps[:, b, 0:n], op=ALU.mult),
                               reads=[("tmpc", si), ("ps", bk_h[si])], writes=[vk])
                    else:
                        vs = vb[:, 2 + NPT:2 + NPT + NSS * 6].rearrange("p (s t) -> p s t", t=6)
                        P.emit("dve", lambda hh, vs=vs, b=bk_h[si], c0=c0, n=n: hh.tensor_tensor(
                            out=vs[:, :, 2:6], in0=tmpc[:, c0:c0 + n].rearrange("p (s t) -> p s t", t=4),
                            in1=ps[:, b, 0:n].rearrange("p (s t) -> p s t", t=4), op=ALU.mult),
                            reads=[("tmpc", si), ("ps", bk_h[si])], writes=[vk])
                bk_b = next_banks(len(subs))
                mm_group(bk_b, subs, [(lambda k, s=s_b: wv(s)[:, k, :], ("w", s_b), rf, rk, 16)])
                P.emit("dve", lambda hh, vb=vb, m=m: hh.tensor_copy(out=carry_v[:, m, :], in_=vb[:, NPT:NPT + 2]), reads=[vk], writes=[("carry_v",)])
                if has_samp:
                    vs = vb[:, 2 + NPT:2 + NPT + NSS * 6].rearrange("p (s t) -> p s t", t=6)
                    P.emit("dve", lambda hh, vs=vs, m=m: hh.tensor_copy(out=convs_t[:, m, :, :], in_=vs[:, :, 4:6]), reads=[vk], writes=[("convs_t",)])
                P.emit("dve", lambda hh, vb=vb, m=m: hh.tensor_scalar(out=convt[:, 0:W], in0=vb[:, 0:W], scalar1=convw_t[:, m, 0:1], scalar2=None, op0=ALU.mult),
                       reads=[vk, ("convw",)], writes=[("convt",)])
                P.emit("dve", lambda hh, vb=vb, m=m: hh.scalar_tensor_tensor(out=convt[:, 0:W], in0=vb[:, 1:W + 1], scalar=convw_t[:, m, 1:2], in1=convt[:, 0:W], op0=ALU.mult, op1=ALU.add),
                       reads=[vk, ("convw",), ("convt",)], writes=[("convt",)])
                P.emit("dve", lambda hh, vb=vb, m=m: hh.scalar_tensor_tensor(out=convt[:, 0:W], in0=vb[:, 2:W + 2], scalar=convw_t[:, m, 2:3], in1=convt[:, 0:W], op0=ALU.mult, op1=ALU.add),
                       reads=[vk, ("convw",), ("convt",)], writes=[("convt",)])
                for si, (c0, n) in enumerate(subs):
                    if c0 < NPT:
                        P.emit("dve", lambda hh, m=m, b=bk_b[si], c0=c0, n=n: hh.tensor_tensor(out=g0[:, m, c0:c0 + n], in0=convt[:, c0:c0 + n], in1=ps[:, b, 0:n], op=ALU.mult),
                               reads=[("convt",), ("ps", bk_b[si])], writes=[("g0", m)])
                    else:
                        cs_ = convt[:, NPT + 2:NPT + 2 + NSS * 6].rearrange("p (s t) -> p s t", t=6)
                        P.emit("dve", lambda hh, m=m, cs_=cs_, b=bk_b[si], c0=c0, n=n: hh.tensor_tensor(
                            out=g0[:, m, c0:c0 + n].rearrange("p (s t) -> p s t", t=4), in0=cs_[:, :, 0:4],
                            in1=ps[:, b, 0:n].rearrange("p (s t) -> p s t", t=4), op=ALU.mult),
                            reads=[("convt",), ("ps", bk_b[si])], writes=[("g0", m)])
            for m in range(NCH):
                slot = load_w(w_out[m])
                banks = next_banks(len(subs))
                mm_group(banks, subs, [(lambda k, slot=slot: wv(slot)[:, k, :], ("w", slot),
                                        lambda k, c0, n: g0[:, k, c0:c0 + n], lambda k: [("g0", k)], 16)])
                h_add_psum(m, banks, subs)

        def ssm_bproj(t0, n, ck, par, seg=None):
            for ri in range(2):
                for g16 in range(4):
                    bb = 6 + ((ri * 4 + g16) % 2)
                    for j in range(16):
                        gp = g16 * 16 + j
                        q = gp // 4
                        P.emit("pe", lambda hh, bb=bb, j=j, q=q, gp=gp, ri=ri: hh.matmul(
                            ps[:, bb, j * TC:j * TC + n], lhsT=bbt[:, gp, ri, :], rhs=xn[:, q, t0:t0 + n],
                            start=True, stop=False),
                            reads=[("bbtp", ri), ("xn", q), ("xnc", q, ck)], writes=[("ps", bb)], signal=False)
                        if seg is None:
                            P.emit("pe", lambda hh, bb=bb, j=j, q=q, gp=gp, ri=ri: hh.matmul(
                                ps[:, bb, j * TC + 1:j * TC + n], lhsT=abt[:, gp, ri, :], rhs=xn[:, q, t0:t0 + n - 1],
                                start=False, stop=True, skip_group_check=True),
                                reads=[("abtp", ri), ("xn", q), ("xnc", q, ck)], writes=[("ps", bb)], signal=(j == 15))
                        else:
                            P.emit("pe", lambda hh, bb=bb, j=j, q=q, gp=gp, ri=ri: hh.matmul(
                                ps[:, bb, j * TC:j * TC + n].rearrange("p (s t) -> p s t", t=seg)[:, :, 1:seg], lhsT=abt[:, gp, ri, :],
                                rhs=xn[:, q, t0:t0 + n].rearrange("p (s t) -> p s t", t=seg)[:, :, 0:seg - 1],
                                start=False, stop=True, skip_group_check=True),
                                reads=[("abtp", ri), ("xn", q), ("xnc", q, ck)], writes=[("ps", bb)], signal=(j == 15))
                    P.emit("act", lambda hh, bb=bb, ri=ri, g16=g16: hh.activation(
                        out=bu[par][:, ri, g16 * 16:g16 * 16 + 16, 0:n], in_=ps[:, bb, 0:16 * TC].rearrange("p (j t) -> p j t", j=16)[:, :, 0:n], func=AF.Copy),
                        reads=[("ps", bb)], writes=[("bu", par, ri, g16)])

        def ssm_recur(n, par, init_ap, init_key, boff=0):
            assert n % 2 == 0
            bk_ = [("bu", par, ri, g16) for ri in range(2) for g16 in range(4)]
            P.emit("dve", lambda hh: hh.memset(traj[:, :, :, 0], 0.0), writes=[("traj",)])
            P.emit("dve", lambda hh: hh.tensor_copy(out=traj[:, :, :, 1], in_=init_ap), reads=[init_key], writes=[("traj",)])
            P.emit("dve", lambda hh: hh.tensor_tensor(out=c1b[:], in0=m1[:], in1=init_ap, op=ALU.mult), reads=[("m1",), init_key], writes=[("cc", 1)])
            P.emit("dve", lambda hh: hh.tensor_tensor(out=c2b[:], in0=m2[:], in1=init_ap[:, ::-1, :], op=ALU.mult), reads=[("m2",), init_key], writes=[("cc", 2)])
            P.emit("dve", lambda hh: hh.tensor_tensor(out=c1b[:], in0=c1b[:], in1=c2b[:], op=ALU.add), reads=[("cc", 1), ("cc", 2)], writes=[("cc", 1)])
            P.emit("dve", lambda hh: hh.tensor_tensor(out=bu[par][:, :, :, boff], in0=c1b[:], in1=bu[par][:, :, :, boff], op=ALU.add),
                   reads=[("cc", 1)] + bk_, writes=[("w", par, "h")])
            for c in range(0, n, 2):
                P.emit("dve", lambda hh, c=c: hh.tensor_tensor(out=t1b[:], in0=m1x[:], in1=traj[:, :, :, c:c + 2], op=ALU.mult), reads=[("m1x",), ("traj",)], writes=[("tt", 1)])
                P.emit("dve", lambda hh, c=c: hh.tensor_tensor(out=t2b[:], in0=m2x[:], in1=traj[:, ::-1, :, c:c + 2], op=ALU.mult), reads=[("m2x",), ("traj",)], writes=[("tt", 2)])
                P.emit("dve", lambda hh, c=c: hh.tensor_tensor(out=t1b[:], in0=t1b[:], in1=bu[par][:, :, :, boff + c:boff + c + 2], op=ALU.add),
                       reads=[("tt", 1), ("w", par, "h")] + (bk_ if (c == 0 or c == n - 2) else []), writes=[("tt", 1)])
                P.emit("dve", lambda hh, c=c: hh.tensor_tensor(out=traj[:, :, :, c + 2:c + 4], in0=t1b[:], in1=t2b[:], op=ALU.add),
                       reads=[("tt", 1), ("tt", 2)], writes=[("traj",)])

        def ssm_trajb(n, boff=0):
            P.emit("act", lambda hh: hh.activation(out=trajb[:, :, :, boff:boff + n], in_=traj[:, :, :, 2:n + 2], func=AF.Copy), reads=[("traj",)], writes=[("trajb",)])

        def ssm_cproj(t0, n, ck, yb):
            for q in range(16):
                P.emit("pe", lambda hh, q=q: hh.matmul(
                    ps[:, yb, q * TC:q * TC + n], lhsT=ddg[:, q, :], rhs=xn[:, q, t0:t0 + n], start=True, stop=False),
                    reads=[("ddg",), ("xn", q), ("xnc", q, ck)], writes=[("ps", yb)], signal=False)
                for g4 in range(4):
                    gp = q * 4 + g4
                    for ri in range(2):
                        lastmm = (g4 == 3 and ri == 1)
                        P.emit("pe", lambda hh, q=q, gp=gp, g4=g4, ri=ri, lastmm=lastmm: hh.matmul(
                            ps[32 * g4:32 * g4 + 32, yb, q * TC:q * TC + n], lhsT=cpad[:, ri, gp * 32:(gp + 1) * 32], rhs=trajb[:, ri, gp, 0:n],
                            start=False, stop=(ri == 1), tile_position=(0, 32 * g4), skip_group_check=True),
                            reads=[("cpad", ri), ("trajb",)], writes=[("ps", yb)], signal=(q == 15 and lastmm))

        def ssm_gelu(t0, n, ck, yb):
            yv = ps[:, yb, 0:16 * TC]
            P.emit("act", lambda hh: hh.activation(out=gel[0][:], in_=yv, func=AF.Square), reads=[("ps", yb)], writes=[("gel", 0)])
            P.emit("dve", lambda hh: hh.tensor_scalar(out=gel[0][:], in0=gel[0][:], scalar1=0.044715, scalar2=1.0, op0=ALU.mult, op1=ALU.add),
                   reads=[("gel", 0)], writes=[("gel", 0)])
            P.emit("dve", lambda hh: hh.tensor_tensor(out=gel[1][:], in0=gel[0][:], in1=yv, op=ALU.mult), reads=[("gel", 0), ("ps", yb)], writes=[("gel", 1)])
            P.emit("act", lambda hh: hh.activation(out=gel[2][:], in_=gel[1][:], func=AF.Sigmoid, scale=1.5957691216057308), reads=[("gel", 1)], writes=[("gel", 2)])
            P.emit("dve", lambda hh: hh.tensor_tensor(
                out=xn[:, :, t0:t0 + n], in0=gel[2][:].rearrange("p (j t) -> p j t", j=16)[:, :, 0:n],
                in1=yv.rearrange("p (j t) -> p j t", j=16)[:, :, 0:n], op=ALU.mult),
                reads=[("gel", 2), ("ps", yb)], writes=[("xnc", q, ck) for q in range(16)])
            cks.add(ck)

        cks = set()

        def ssm_mixer(subs, TT, has_samp, full, is_last_b):
            cks.clear()
            chunks = []
            t0 = 0
            for n in [16] * 32 + [4]:
                chunks.append((t0, n, "p", None))
                t0 += n
            assert t0 == NPT
            if has_samp:
                for g_ in range(NSS // 4):
                    chunks.append((NPT + 16 * g_, 16, "s", g_))
            nprompt = 33
            ssm_bproj(chunks[0][0], chunks[0][1], 0, 0)
            pend = None
            segof = lambda kind: (4 if kind == "s" else None)
            for ci, (t0, n, kind, s_) in enumerate(chunks):
                par = ci % 2
                if ci + 1 < len(chunks):
                    ssm_bproj(chunks[ci + 1][0], chunks[ci + 1][1], ci + 1, (ci + 1) % 2, segof(chunks[ci + 1][2]))
                if kind == "p":
                    if ci > 0:
                        P.emit("dve", lambda hh, pn=chunks[ci - 1][1]: hh.tensor_copy(out=xcarry[:], in_=traj[:, :, :, pn + 1]), reads=[("traj",)], writes=[("xcarry",)])
                    ssm_recur(n, par, xcarry[:], ("xcarry",))
                    if ci == nprompt - 1:
                        P.emit("dve", lambda hh, n=n: hh.tensor_copy(out=xcarry[:], in_=traj[:, :, :, n + 1]), reads=[("traj",)], writes=[("xcarry",)])
                    if full:
                        ssm_trajb(n)
                else:
                    for l_ in range(4):
                        sq_ = 4 * s_ + l_
                        ssm_recur(4, par, ssms_t[:, sq_, :, :], ("ssms_t",), boff=4 * l_)
                        P.emit("dve", lambda hh, sq_=sq_: hh.tensor_copy(out=ssms_t[:, sq_, :, :], in_=traj[:, :, :, 5]), reads=[("traj",)], writes=[("ssms_t",)])
                        if full:
                            ssm_trajb(4, boff=4 * l_)
                if full:
                    yb = 3 + (ci % 2)
                    ssm_cproj(t0, n, ci, yb)
                    if pend is not None:
                        ssm_gelu(*pend)
                    pend = (t0, n, ci, yb)
            if full and pend is not None:
                ssm_gelu(*pend)
            if not full:
                return
            for m in range(NCH):
                sa = load_w(w_ga[m])
                sb_ = load_w(w_gb[m])
                ckl = sorted(cks)
                rk = lambda k, ckl=ckl: [("xn", k)] + [("xnc", k, c_) for c_ in ckl]
                rf = lambda k, c0, n: xn[:, k, c0:c0 + n]
                bka = next_banks(len(subs))
                mm_group(bka, subs, [(lambda k, s=sa: wv(s)[:, k, :], ("w", sa), rf, rk, 16)])
                bkb = next_banks(len(subs))
                mm_group(bkb, subs, [(lambda k, s=sb_: wv(s)[:, k, :], ("w", sb_), rf, rk, 16)])
                for si, (c0, n) in enumerate(subs):
                    P.emit("act", lambda hh, b=bkb[si], c0=c0, n=n: hh.activation(out=glu_t[0][:, c0:c0 + n], in_=ps[:, b, 0:n], func=AF.Sigmoid),
                           reads=[("ps", bkb[si])], writes=[("glu", 0, si)])
                    P.emit("dve", lambda hh, b=bka[si], c0=c0, n=n: hh.tensor_tensor(out=glu_t[1][:, c0:c0 + n], in0=glu_t[0][:, c0:c0 + n], in1=ps[:, b, 0:n], op=ALU.mult),
                           reads=[("glu", 0, si), ("ps", bka[si])], writes=[("glu", 1, si)])
                    P.emit("dve", lambda hh, m=m, c0=c0, n=n: hh.tensor_tensor(out=h[:, m, c0:c0 + n], in0=h[:, m, c0:c0 + n], in1=glu_t[1][:, c0:c0 + n], op=ALU.add),
                           reads=[("glu", 1, si), ("h", m)], writes=[("h", m)])

        tiles = []
        for ti in range(NTILE_HALF):
            tiles.append(dict(col=ti * NPT, full=False, samp=False, last=False, ocol=None))
        for ti in range(NTILE_HALF):
            lastb = ti == NTILE_HALF - 1
            tiles.append(dict(col=HALF + ti * NPT, full=True, samp=lastb, last=lastb, ocol=ti * NPT))

        for tinfo in tiles:
            full, samp = tinfo["full"], tinfo["samp"]
            TT = NPT + (NSAMP if samp else 0)
            subs = [(0, NPT // 2), (NPT // 2, NPT // 2)] + ([(NPT, NSAMP)] if samp else [])
            for c in range(NCH):
                pass
            P.emit("sp", lambda hh, col=tinfo["col"]: hh.dma_start(out=h[:, :, 0:NPT], in_=xin_v[:, :, col:col + NPT]),
                   writes=[("h", c) for c in range(NCH)], sem=ldx, inc=16)
            if samp:
                P.emit("sp", lambda hh: hh.dma_start(out=h[:, :, NPT:NPT + NSAMP], in_=xin_v[:, :, 2 * HALF:2 * HALF + NSAMP]),
                       writes=[("h", c) for c in range(NCH)], sem=ldx, inc=16)
            rm_switch()
            rmsnorm(subs, TT, 0, lambda c, TT=TT: xn[:, c, 0:TT], "xn")
            conv_mixer(subs, TT, samp, tinfo["last"])
            if stage >= 2:
                rm_switch()
                rmsnorm(subs, TT, 1, lambda c, TT=TT: xn[:, c, 0:TT], "xn")
                mlp(0, subs, TT)
            if stage >= 3:
                rm_switch()
                rmsnorm(subs, TT, 2, lambda c, TT=TT: xn[:, c, 0:TT], "xn")
                ssm_mixer(subs, TT, samp, full and stage >= 4, tinfo["last"])
            if full and stage >= 5:
                rm_switch()
                rmsnorm(subs, TT, 3, lambda c, TT=TT: xn[:, c, 0:TT], "xn")
                mlp(1, subs, TT)
            if full:
                rm_switch()
                if stage >= 6:
                    rmsnorm(subs, TT, 4, lambda c, TT=TT: yfin[:, c, 0:TT], "yfin")
                else:
                    for c in range(NCH):
                        P.emit("dve", lambda hh, c=c, TT=TT: hh.tensor_copy(out=yfin[:, c, 0:TT], in_=h[:, c, 0:TT]), reads=[("h", c)], writes=[("yfin", c)])
                oc = tinfo["ocol"]
                P.emit("sp", lambda hh, oc=oc: hh.dma_start(out=yout_v[:, :, oc:oc + NPT], in_=yfin[:, :, 0:NPT]),
                       reads=[("yfin", c) for c in range(NCH)], sem=st, inc=16)
                if samp:
                    P.emit("sp", lambda hh: hh.dma_start(out=yout_v[:, :, HALF:HALF + NSAMP], in_=yfin[:, :, NPT:NPT + NSAMP]),
                           reads=[("yfin", c) for c in range(NCH)], sem=st, inc=16)
        P.emit("sp", lambda hh: hh.dma_start(out=convp_out, in_=carry_v[:]), reads=[("carry_v",)], sem=st, inc=16)
        P.emit("sp", lambda hh: hh.dma_start(out=ssmp_out, in_=xcarry[:]), reads=[("xcarry",)], sem=st, inc=16)
        P.emit("sp", lambda hh: hh.dma_start(out=convs_out, in_=convs_t[:]), reads=[("convs_t",)], sem=st, inc=16)
        P.emit("sp", lambda hh: hh.dma_start(out=ssms_out, in_=ssms_t[:]), reads=[("ssms_t",)], sem=st, inc=16)
        P.barrier()

        with ExitStack() as es3:
            for s in P.sems:
                s.h = es3.enter_context(nc.semaphore(s.name))
            block = es3.enter_context(nc.Block())

            @block.tensor
            def _(e):
                for f in P.prog["pe"]:
                    f(e)

            @block.scalar
            def _(e):
                for f in P.prog["act"]:
                    f(e)

            @block.vector
            def _(e):
                for f in P.prog["dve"]:
                    f(e)

            @block.gpsimd
            def _(e):
                for f in P.prog["pool"]:
                    f(e)

            @block.sync
            def _(e):
                for f in P.prog["sp"]:
                    f(e)
    return nc


def _slabs(W):
    K, M = W.shape
    a = W.reshape(K // 128, 128, M // 128, 128)
    return np.ascontiguousarray(a.transpose(2, 1, 0, 3)).reshape(M // 128, 128, (K // 128) * 128)


def _fm(x):
    return np.ascontiguousarray(x.T)


def _prep_shared(inp):
    f = lambda k: np.asarray(inp[k], dtype=np.float32)
    sh = {}
    sh["w_in"] = _slabs(f("conv_w_in")[0])
    sh["w_out"] = _slabs(f("conv_w_out")[0])
    sh["w_up"] = np.stack([_slabs(f("mlp_w_up")[l]) for l in range(2)])
    dn = []
    for l in range(2):
        Wd = f("mlp_w_down")[l]
        q = [_slabs(Wd[qd * 2048:(qd + 1) * 2048]) for qd in range(4)]
        dn.append(np.concatenate(q, axis=0))
    sh["w_dn"] = np.stack(dn)
    sh["w_ga"] = _slabs(f("ssm_glu_w_a")[0])
    sh["w_gb"] = _slabs(f("ssm_glu_w_b")[0])
    gam = np.stack([f("norm_mixer")[0], f("norm_mlp")[0], f("norm_mixer")[1], f("norm_mlp")[1], f("norm_final")])
    sh["gam"] = np.ascontiguousarray(gam.reshape(5, 16, 128).transpose(2, 0, 1))
    sh["convw"] = np.ascontiguousarray(f("conv_w")[0].reshape(3, 16, 128).transpose(2, 1, 0))
    lre, lim, ls = f("ssm_lambda_re")[0], f("ssm_lambda_im")[0], f("ssm_log_step")[0]
    lsb = np.broadcast_to(ls[:, None], (128, 64))
    sm = lambda a: np.ascontiguousarray(a.reshape(64, 2, 64).transpose(1, 2, 0).reshape(128, 64))
    sh["lam_s"] = np.ascontiguousarray(np.stack([sm(lre), sm(lim), sm(lsb)], axis=1))
    def cmaj(a):
        t = a.reshape(16, 4, 2, 64)
        t = np.broadcast_to(t[:, :, None, None, :, :], (16, 4, 2, 16, 2, 64))
        return np.ascontiguousarray(t.transpose(1, 2, 3, 0, 4, 5)).reshape(128, 2048)
    tm = lambda a: np.tile(a.reshape(64, 128), (2, 1))
    sh["lam_t"] = np.ascontiguousarray(np.stack([tm(lre), tm(lim), tm(lsb)], axis=1))
    def bmaj(b):
        t = b.reshape(16, 4, 2, 64, 16)
        out = np.zeros((4, 2, 16, 16, 2, 64), np.float32)
        for j in range(2):
            out[:, j, :, :, j, :] = t[:, :, j].transpose(1, 3, 0, 2)
        return out.reshape(128, 2048)
    sh["bp"] = np.ascontiguousarray(np.stack([bmaj(f("ssm_b_re")[0]), bmaj(f("ssm_b_im")[0])], axis=1))
    def cmajp(c):
        t = c.reshape(64, 2, 16, 64)
        out = np.zeros((2, 64, 64, 2, 16), np.float32)
        for j in range(2):
            out[j, :, :, j, :] = t[:, j].transpose(2, 0, 1)
        return out.reshape(128, 64 * 32)
    sh["cp"] = np.ascontiguousarray(np.stack([cmajp(f("ssm_c_re")[0]), cmajp(f("ssm_c_im")[0])], axis=1))
    dv = f("ssm_d")[0].reshape(16, 128)
    ddm = np.zeros((128, 16, 128), np.float32)
    for q in range(16):
        ddm[np.arange(128), q, np.arange(128)] = dv[q]
    sh["dd"] = ddm.reshape(128, 2048)
    return sh


_NC_CACHE = {}


def kernel(**inp):
    sh = _prep_shared(inp)
    xp = np.asarray(inp["x_prompt"], np.float32)
    xs = np.asarray(inp["x_sample"], np.float32)
    meta = np.asarray(inp["meta_tokens"], np.float32)
    sc = np.asarray(inp["state_conv"], np.float32)[0]
    sre = np.asarray(inp["state_ssm_re"], np.float32)[0]
    sim = np.asarray(inp["state_ssm_im"], np.float32)[0]
    in_maps = []
    for c in range(8):
        i, r = c // 2, c % 2
        S = np.concatenate([meta, xp[i]], axis=0)
        A = np.zeros((HALF, D), np.float32) if r == 0 else S[:HALF]
        B = S[:HALF] if r == 0 else S[HALF:]
        smp = xs[NSS * c:NSS * (c + 1)].reshape(NSAMP, D)
        m = dict(sh)
        m["xin"] = _fm(np.concatenate([A, B, smp], axis=0))
        cs = sc[NSS * c:NSS * (c + 1)]
        m["convs_in"] = np.ascontiguousarray(cs.reshape(NSS, 2, 16, 128).transpose(3, 2, 0, 1))
        def st(a):
            return a.reshape(NSS, 64, 2, 64).transpose(2, 3, 0, 1).reshape(128, NSS, 64)
        m["ssms_in"] = np.ascontiguousarray(np.stack([st(sre[NSS * c:NSS * (c + 1)]), st(sim[NSS * c:NSS * (c + 1)])], axis=2))
        in_maps.append(m)
    if "nc" not in _NC_CACHE:
        _NC_CACHE["nc"] = build()
    res = run_bass_kernel_spmd(_NC_CACHE["nc"], in_maps, core_ids=list(range(8)))
    R = res.results
    y_prompt = np.zeros((4, 2048, D), np.float32)
    y_sample = np.zeros((128, 4, D), np.float32)
    conv_p = np.zeros((1, 4, 2, D), np.float32)
    re_p = np.zeros((1, 4, 128, 64), np.float32)
    im_p = np.zeros((1, 4, 128, 64), np.float32)
    conv_s = np.zeros((1, 128, 2, D), np.float32)
    re_s = np.zeros((1, 128, 128, 64), np.float32)
    im_s = np.zeros((1, 128, 128, 64), np.float32)
    for c in range(8):
        i, r = c // 2, c % 2
        yo = np.asarray(R[c]["y_out"]).T
        if r == 0:
            y_prompt[i, 0:HALF - 16] = yo[16:HALF]
        else:
            y_prompt[i, HALF - 16:] = yo[0:HALF]
        y_sample[NSS * c:NSS * (c + 1)] = yo[HALF:].reshape(NSS, 4, D)
        cso = np.asarray(R[c]["convs_out"])
        conv_s[0, NSS * c:NSS * (c + 1)] = cso.transpose(2, 3, 1, 0).reshape(NSS, 2, D)
        sso = np.asarray(R[c]["ssms_out"])
        t = sso.reshape(2, 64, NSS, 2, 64).transpose(2, 3, 4, 0, 1).reshape(NSS, 2, 128, 64)
        re_s[0, NSS * c:NSS * (c + 1)] = t[:, 0]
        im_s[0, NSS * c:NSS * (c + 1)] = t[:, 1]
        if r == 1:
            cpo = np.asarray(R[c]["convp_out"])
            conv_p[0, i] = cpo.transpose(2, 1, 0).reshape(2, D)
            spo = np.asarray(R[c]["ssmp_out"])
            t = spo.reshape(2, 64, 2, 64).transpose(2, 3, 0, 1).reshape(2, 128, 64)
            re_p[0, i] = t[0]
            im_p[0, i] = t[1]
    return (y_prompt, y_sample, conv_p, re_p, im_p, conv_s, re_s, im_s)
```

```python
import math
import os
import numpy as np
import concourse.bass as bass
import concourse.mybir as mybir
from concourse.bass_utils import run_bass_kernel_spmd

F32 = mybir.dt.float32
BF16 = mybir.dt.bfloat16
I32 = mybir.dt.int32
AF = mybir.ActivationFunctionType
ALU = mybir.AluOpType

D = 2048
NCH = 16
NPT = 516
NTILE_HALF = 2
HALF = NPT * NTILE_HALF
NSS = 16
NSAMP = NSS * 4
TTMAX = NPT + NSAMP
VW = 2 + NPT + NSS * 6
TC = 16
NW = 4
EPS = 1e-6
NTOK = 2 * HALF + NSAMP
ENGS = ["pe", "act", "dve", "pool", "sp"]


SEM_LIMIT = 1000


class Sem:
    def __init__(self, name, owner=None):
        self.name = name
        self.h = None
        self.count = 0
        self.owner = owner


class Prog:
    def __init__(self):
        self.prog = {e: [] for e in ENGS}
        self.waited = {e: {} for e in ENGS}
        self.esem = {e: Sem("e_" + e, e) for e in ENGS}
        self.nrot = 0
        self.sems = list(self.esem.values())
        self.lastw = {}
        self.readers = {}
        self.guard = {}

    def new_sem(self, name):
        s = Sem(name)
        self.sems.append(s)
        return s

    def _wait(self, eng, ev):
        if ev is None:
            return
        sem, val = ev
        if eng == "pe" and sem.owner == "pe":
            return
        w = self.waited[eng]
        if w.get(sem.name, 0) >= val:
            return
        w[sem.name] = val
        self.prog[eng].append(lambda h, sem=sem, val=val: h.wait_ge(sem.h, val))

    def emit(self, eng, fn, reads=(), writes=(), sem=None, inc=1, signal=True, extra=()):
        for ev in extra:
            self._wait(eng, ev)
        for b in reads:
            self._wait(eng, self.lastw.get(b))
            for ev in self.guard.get(b[0], ()):
                self._wait(eng, ev)
        for b in writes:
            self._wait(eng, self.lastw.get(b))
            for ev in self.readers.get(b, ()):
                self._wait(eng, ev)
            for ev in self.guard.get(b[0], ()):
                self._wait(eng, ev)
        if sem is None:
            if self.esem[eng].count >= SEM_LIMIT:
                self.nrot += 1
                ns = Sem(f"e_{eng}_{self.nrot}", eng)
                self.sems.append(ns)
                self.esem[eng] = ns
            s = self.esem[eng]
        else:
            s = sem
        if signal:
            s.count += inc
            ev = (s, s.count)
            self.prog[eng].append(lambda h, fn=fn, s=s, inc=inc: fn(h).then_inc(s.h, inc))
        else:
            ev = (s, s.count + inc)
            self.prog[eng].append(lambda h, fn=fn: fn(h))
        for b in reads:
            self.readers.setdefault(b, []).append(ev)
        for b in writes:
            self.lastw[b] = ev
            self.readers[b] = []
        return ev

    def all_events(self):
        return [(s, s.count) for s in self.sems if s.count > 0]

    def barrier(self):
        evs = self.all_events()
        for e in ENGS:
            for ev in evs:
                self._wait(e, ev)


def build(stage=99):
    nc = bass.Bass("TRN2", target_bir_lowering=False)
    P = Prog()

    def din(name, shape, dt=F32):
        return nc.dram_tensor(name, list(shape), dt, kind="ExternalInput").ap()

    def dout(name, shape, dt=F32):
        return nc.dram_tensor(name, list(shape), dt, kind="ExternalOutput").ap()

    xin = din("xin", [D, NTOK])
    convs_in = din("convs_in", [128, NCH, NSS, 2])
    ssms_in = din("ssms_in", [128, NSS, 2, 64])
    gam = din("gam", [128, 5, NCH])
    convw = din("convw", [128, NCH, 3])
    w_in = din("w_in", [48, 128, 2048])
    w_out = din("w_out", [16, 128, 2048])
    w_up = din("w_up", [2, 64, 128, 2048])
    w_dn = din("w_dn", [2, 64, 128, 2048])
    w_ga = din("w_ga", [16, 128, 2048])
    w_gb = din("w_gb", [16, 128, 2048])
    lam_s = din("lam_s", [128, 3, 64])
    lam_t = din("lam_t", [128, 3, 128])
    dscr = nc.dram_tensor("dscr", [64, 4, 128], F32, kind="Internal").ap()
    bp = din("bp", [128, 2, 2048])
    cp = din("cp", [128, 2, 64 * 32])
    dd = din("dd", [128, NCH * 128])

    y_out = dout("y_out", [D, HALF + NSAMP])
    convp_out = dout("convp_out", [128, NCH, 2])
    ssmp_out = dout("ssmp_out", [128, 2, 64])
    convs_out = dout("convs_out", [128, NCH, NSS, 2])
    ssms_out = dout("ssms_out", [128, NSS, 2, 64])

    xin_v = xin.rearrange("(c p) t -> p c t", p=128)
    yout_v = y_out.rearrange("(c p) t -> p c t", p=128)

    RM = 9600
    from contextlib import ExitStack
    with ExitStack() as es:
        def sb(name, shape, dt=F32):
            return es.enter_context(nc.sbuf_tensor(name, list(shape), dt))

        h = sb("h", [128, NCH, TTMAX])
        xn = sb("xn", [128, NCH, TTMAX], BF16)
        wsl = sb("wsl", [128, NW, 2048], BF16)
        rmix = sb("rmix", [128, RM])
        rstd = sb("rstd", [128, TTMAX])
        sqb = sb("sqb", [128, 2, TTMAX], BF16)
        ones = sb("ones", [128, 128], BF16)
        gam_t = sb("gam_t", [128, 5, NCH])
        convw_t = sb("convw_t", [128, NCH, 3])
        convs_t = sb("convs_t", [128, NCH, NSS, 2])
        carry_v = sb("carry_v", [128, NCH, 2])
        xcarry = sb("xcarry", [128, 2, 64])
        ssms_t = sb("ssms_t", [128, NSS, 2, 64])
        m1 = sb("m1", [128, 2, 64])
        m2 = sb("m2", [128, 2, 64])
        bbt = sb("bbt", [128, 64, 2, 128], BF16)
        abt = sb("abt", [128, 64, 2, 128], BF16)
        cpad = sb("cpad", [128, 2, 64 * 32], BF16)
        ddg = sb("ddg", [128, NCH, 128], BF16)
        ps = es.enter_context(nc.psum_tensor("ps", [128, 8, 512], F32))

        def rm_f32(off, n):
            return rmix[:, off:off + n]

        def rm_bf16(off, n):
            return rmix[:, off:off + n].bitcast(BF16)

        vbuf = [rm_f32(0, VW), rm_f32(VW, VW)]
        tmpc = rm_f32(2 * VW, TTMAX)
        convt = rm_f32(2 * VW + TTMAX, VW)
        g0 = rm_bf16(3 * VW + TTMAX, NCH * TTMAX // 2).rearrange("p (c t) -> p c t", c=NCH)
        HQW = NCH * TTMAX // 2
        hq = [rm_bf16(0, HQW).rearrange("p (c t) -> p c t", c=NCH),
              rm_bf16(HQW, HQW).rearrange("p (c t) -> p c t", c=NCH)]
        relu_t = rm_bf16(2 * HQW, TTMAX // 2 + 2)
        glu_t = [rm_f32(0, TTMAX), rm_f32(TTMAX, TTMAX)]
        BUW = 128 * TC
        v4 = lambda off, t_: rm_f32(off, 128 * t_).rearrange("p (r g t) -> p r g t", r=2, g=64)
        bu = [v4(0, TC), v4(BUW, TC)]
        TRW = 128 * (TC + 2)
        traj = v4(2 * BUW, TC + 2)
        trajb = rm_bf16(2 * BUW + TRW, BUW // 2).rearrange("p (r g t) -> p r g t", r=2, g=64)
        o1 = 2 * BUW + TRW + BUW // 2
        t1b = v4(o1, 2)
        t2b = v4(o1 + 256, 2)
        c1b = rm_f32(o1 + 512, 128).rearrange("p (r g) -> p r g", r=2)
        c2b = rm_f32(o1 + 640, 128).rearrange("p (r g) -> p r g", r=2)
        o2 = o1 + 768
        GW = 16 * TC
        gel = [rm_f32(o2 + i * GW, GW) for i in range(3)]
        assert o2 + 3 * GW <= RM and 2 * HQW + TTMAX // 2 + 2 <= RM and 3 * VW + TTMAX + NCH * TTMAX // 2 <= RM
        yfin = rm_f32(0, NCH * TTMAX).rearrange("p (c t) -> p c t", c=NCH) if NCH * TTMAX <= RM else None
        assert yfin is not None

        RMN = ("w", "tmpw", "cc", "g0", "vbuf", "tmpc", "convt", "hq", "relu", "glu", "bu", "traj", "trajb", "tt", "gel", "yfin")

        def rm_switch():
            evs = P.all_events()
            for n in RMN:
                P.guard[n] = evs

        wsem = [P.new_sem(f"w{i}") for i in range(NW)]
        wcount = [0]

        def load_w(src_ap):
            k = wcount[0]
            wcount[0] += 1
            slot = k % NW
            if wsem[slot].count >= SEM_LIMIT:
                wsem[slot] = P.new_sem(f"w{slot}_{k}")
            P.emit("pool", lambda hh, slot=slot, src_ap=src_ap: hh.dma_start(out=wsl[:, slot, :], in_=src_ap),
                   writes=[("w", slot)], sem=wsem[slot], inc=16)
            return slot

        def wv(slot):
            return wsl[:, slot, :].rearrange("p (k m) -> p k m", k=16)

        ldx = P.new_sem("ldx")
        st = P.new_sem("st")

        def load_small(dst, src, key):
            P.emit("sp", lambda hh: hh.dma_start(out=dst, in_=src), writes=[key], sem=P.new_sem("ld_" + key[0]), inc=16)

        load_small(gam_t[:], gam, ("gam",))
        load_small(convw_t[:], convw, ("convw",))
        load_small(convs_t[:], convs_in, ("convs_t",))
        load_small(ssms_t[:], ssms_in, ("ssms_t",))
        P.emit("dve", lambda hh: hh.memset(ones[:], 1.0), writes=[("ones",)])
        P.emit("dve", lambda hh: hh.memset(carry_v[:], 0.0), writes=[("carry_v",)])
        P.emit("dve", lambda hh: hh.memset(xcarry[:], 0.0), writes=[("xcarry",)])
        P.emit("pool", lambda hh: hh.dma_start(out=cpad[:, 0, :], in_=cp[:, 0, :]), writes=[("cpad", 0)], sem=P.new_sem("ldp0"), inc=16)
        P.emit("pool", lambda hh: hh.dma_start(out=cpad[:, 1, :], in_=cp[:, 1, :]), writes=[("cpad", 1)], sem=P.new_sem("ldp1"), inc=16)
        P.emit("pool", lambda hh: hh.dma_start(out=ddg[:].rearrange("p c m -> p (c m)"), in_=dd), writes=[("ddg",)], sem=P.new_sem("ldp2"), inc=16)
        P.emit("act", lambda hh: hh.activation(out=cpad[:, 1, :], in_=cpad[:, 1, :], func=AF.Copy, scale=-1.0),
               reads=[("cpad", 1)], writes=[("cpad", 1)])

        TWO_PI = 2.0 * math.pi

        rm_bump = [0]

        def rm_alloc(n):
            o = rm_bump[0]
            rm_bump[0] += n
            assert rm_bump[0] <= RM
            return rmix[:, o:o + n]

        def cplx_setup(tag, F, want_q):
            T = {}

            def t(n):
                T[n] = rm_alloc(F)
                return T[n]
            lam_t = rm_alloc(3 * F).rearrange("p (a f) -> p a f", a=3)
            lam_sem = P.new_sem("ld_lam_" + tag)
            lr, li, ls = lam_t[:, 0, :], lam_t[:, 1, :], lam_t[:, 2, :]
            K = lambda n: (tag, n)
            dt_ = t("dt"); mag = t("mag"); th = t("th"); kf = t("kf")
            r = t("r"); msk = t("msk"); sn = t("sn"); cs = t("cs"); are = t("are"); aim = t("aim")
            if want_q:
                den = t("den"); nr = t("nr"); qre = t("qre"); qim = t("qim"); tq = t("tq")
            fact = [1.0]
            for i_ in range(1, 20):
                fact.append(fact[-1] * i_)
            EXPC = [1.0 / fact[i_] for i_ in range(11)]
            SINC = [((-1.0) ** i_) / fact[2 * i_ + 1] for i_ in range(8)]
            COSC = [((-1.0) ** i_) / fact[2 * i_] for i_ in range(9)]

            def poly(dst, w, coeffs, kd, kw):
                n_ = len(coeffs) - 1
                P.emit("dve", lambda hh: hh.tensor_scalar(out=dst, in0=w, scalar1=coeffs[n_], scalar2=None, op0=ALU.mult), reads=[K(kw)], writes=[K(kd)])
                for i_ in range(n_ - 1, 0, -1):
                    P.emit("dve", lambda hh, c_=coeffs[i_]: hh.scalar_tensor_tensor(out=dst, in0=dst, scalar=c_, in1=w, op0=ALU.add, op1=ALU.mult),
                           reads=[K(kd), K(kw)], writes=[K(kd)])
                P.emit("dve", lambda hh: hh.tensor_scalar(out=dst, in0=dst, scalar1=coeffs[0], scalar2=None, op0=ALU.add), reads=[K(kd)], writes=[K(kd)])

            def tt(out, a_, b_, op, ko, ka, kb):
                P.emit("dve", lambda hh: hh.tensor_tensor(out=out, in0=a_, in1=b_, op=op), reads=[K(ka), K(kb)], writes=[K(ko)])

            def run(lam_ap):
                P.emit("sp", lambda hh: hh.dma_start(out=lam_t, in_=lam_ap), writes=[K("lam")], sem=lam_sem, inc=16)
                P.emit("dve", lambda hh: hh.tensor_scalar(out=kf, in0=ls, scalar1=1.0 / 16.0, scalar2=None, op0=ALU.mult), reads=[K("lam")], writes=[K("kf")])
                poly(dt_, kf, EXPC, "dt", "kf")
                for _ in range(4):
                    tt(dt_, dt_, dt_, ALU.mult, "dt", "dt", "dt")
                tt(kf, lr, dt_, ALU.mult, "kf", "lam", "dt")
                poly(mag, kf, EXPC[:9], "mag", "kf")
                tt(th, li, dt_, ALU.mult, "th", "lam", "dt")
                P.emit("dve", lambda hh: hh.tensor_scalar(out=th, in0=th, scalar1=1.0 / 16.0, scalar2=None, op0=ALU.mult), reads=[K("th")], writes=[K("th")])
                tt(r, th, th, ALU.mult, "r", "th", "th")
                poly(sn, r, SINC, "sn", "r")
                tt(sn, sn, th, ALU.mult, "sn", "sn", "th")
                poly(cs, r, COSC, "cs", "r")
                for _ in range(4):
                    tt(msk, cs, sn, ALU.mult, "msk", "cs", "sn")
                    tt(cs, cs, cs, ALU.mult, "cs", "cs", "cs")
                    tt(sn, sn, sn, ALU.mult, "sn", "sn", "sn")
                    tt(cs, cs, sn, ALU.subtract, "cs", "cs", "sn")
                    P.emit("dve", lambda hh: hh.tensor_scalar(out=sn, in0=msk, scalar1=2.0, scalar2=None, op0=ALU.mult), reads=[K("msk")], writes=[K("sn")])
                P.emit("dve", lambda hh: hh.tensor_tensor(out=are, in0=mag, in1=cs, op=ALU.mult), reads=[K("mag"), K("cs")], writes=[K("are")])
                P.emit("dve", lambda hh: hh.tensor_tensor(out=aim, in0=mag, in1=sn, op=ALU.mult), reads=[K("mag"), K("sn")], writes=[K("aim")])
                if want_q:
                    P.emit("dve", lambda hh: hh.tensor_tensor(out=den, in0=lr, in1=lr, op=ALU.mult), reads=[K("lam")], writes=[K("den")])
                    P.emit("dve", lambda hh: hh.tensor_tensor(out=tq, in0=li, in1=li, op=ALU.mult), reads=[K("lam")], writes=[K("tq")])
                    P.emit("dve", lambda hh: hh.tensor_tensor(out=den, in0=den, in1=tq, op=ALU.add), reads=[K("den"), K("tq")], writes=[K("den")])
                    P.emit("dve", lambda hh: hh.reciprocal(out=den, in_=den), reads=[K("den")], writes=[K("den")])
                    P.emit("dve", lambda hh: hh.tensor_scalar(out=nr, in0=are, scalar1=-1.0, scalar2=None, op0=ALU.add), reads=[K("are")], writes=[K("nr")])
                    P.emit("dve", lambda hh: hh.tensor_tensor(out=qre, in0=nr, in1=lr, op=ALU.mult), reads=[K("nr"), K("lam")], writes=[K("qre")])
                    P.emit("dve", lambda hh: hh.tensor_tensor(out=tq, in0=aim, in1=li, op=ALU.mult), reads=[K("aim"), K("lam")], writes=[K("tq")])
                    P.emit("dve", lambda hh: hh.tensor_tensor(out=qre, in0=qre, in1=tq, op=ALU.add), reads=[K("qre"), K("tq")], writes=[K("qre")])
                    P.emit("dve", lambda hh: hh.tensor_tensor(out=qre, in0=qre, in1=den, op=ALU.mult), reads=[K("qre"), K("den")], writes=[K("qre")])
                    P.emit("dve", lambda hh: hh.tensor_tensor(out=qim, in0=aim, in1=lr, op=ALU.mult), reads=[K("aim"), K("lam")], writes=[K("qim")])
                    P.emit("dve", lambda hh: hh.tensor_tensor(out=tq, in0=nr, in1=li, op=ALU.mult), reads=[K("nr"), K("lam")], writes=[K("tq")])
                    P.emit("dve", lambda hh: hh.tensor_tensor(out=qim, in0=qim, in1=tq, op=ALU.subtract), reads=[K("qim"), K("tq")], writes=[K("qim")])
                    P.emit("dve", lambda hh: hh.tensor_tensor(out=qim, in0=qim, in1=den, op=ALU.mult), reads=[K("qim"), K("den")], writes=[K("qim")])
            return T, run

        Ts, run_s = cplx_setup("ss", 64, False)
        run_s(lam_s)
        K = lambda n: ("ss", n)
        P.emit("dve", lambda hh: hh.tensor_copy(out=m1[:, 0, :], in_=Ts["are"]), reads=[K("are")], writes=[("m1",)])
        P.emit("dve", lambda hh: hh.tensor_copy(out=m1[:, 1, :], in_=Ts["are"]), reads=[K("are")], writes=[("m1",)])
        P.emit("dve", lambda hh: hh.tensor_copy(out=m2[:, 1, :], in_=Ts["aim"]), reads=[K("aim")], writes=[("m2",)])
        P.emit("dve", lambda hh: hh.tensor_scalar(out=m2[:, 0, :], in0=Ts["aim"], scalar1=-1.0, scalar2=None, op0=ALU.mult), reads=[K("aim")], writes=[("m2",)])
        m1x = sb("m1x", [128, 2, 64, 2])
        m2x = sb("m2x", [128, 2, 64, 2])
        a2re = rm_alloc(64)
        a2im = rm_alloc(64)
        a2t = rm_alloc(64)
        P.emit("dve", lambda hh: hh.tensor_tensor(out=a2re, in0=Ts["are"], in1=Ts["are"], op=ALU.mult), reads=[K("are")], writes=[("a2", 0)])
        P.emit("dve", lambda hh: hh.tensor_tensor(out=a2t, in0=Ts["aim"], in1=Ts["aim"], op=ALU.mult), reads=[K("aim")], writes=[("a2", 2)])
        P.emit("dve", lambda hh: hh.tensor_tensor(out=a2re, in0=a2re, in1=a2t, op=ALU.subtract), reads=[("a2", 0), ("a2", 2)], writes=[("a2", 0)])
        P.emit("dve", lambda hh: hh.tensor_tensor(out=a2im, in0=Ts["are"], in1=Ts["aim"], op=ALU.mult), reads=[K("are"), K("aim")], writes=[("a2", 1)])
        for ri_ in range(2):
            for ph_ in range(2):
                P.emit("dve", lambda hh, ri_=ri_, ph_=ph_: hh.tensor_copy(out=m1x[:, ri_, :, ph_], in_=a2re), reads=[("a2", 0)], writes=[("m1x",)])
                P.emit("dve", lambda hh, ri_=ri_, ph_=ph_: hh.tensor_scalar(out=m2x[:, ri_, :, ph_], in0=a2im, scalar1=(-2.0 if ri_ == 0 else 2.0), scalar2=None, op0=ALU.mult),
                       reads=[("a2", 1)], writes=[("m2x",)])
        FB = 256
        NQB = FB // 128
        bbk = rm_alloc(NQB * 2 * 128).rearrange("p (c r m) -> p c r m", c=NQB, r=2)
        abk = rm_alloc(NQB * 2 * 128).rearrange("p (c r m) -> p c r m", c=NQB, r=2)
        P.emit("dve", lambda hh: hh.memset(bbt[:].rearrange("p a r m -> p (a r m)"), 0.0), writes=[("bbtp", 0), ("bbtp", 1)])
        P.emit("dve", lambda hh: hh.memset(abt[:].rearrange("p a r m -> p (a r m)"), 0.0), writes=[("abtp", 0), ("abtp", 1)])
        Tt_, run_t = cplx_setup("st", 128, True)
        run_t(lam_t)
        t4 = rm_alloc(4 * 128).rearrange("p (k m) -> p k m", k=4)
        for k_, nm_ in enumerate(("are", "aim", "qre", "qim")):
            P.emit("dve", lambda hh, k_=k_, nm_=nm_: hh.tensor_copy(out=t4[:, k_, :], in_=Tt_[nm_]), reads=[("st", nm_)], writes=[("t4",)])
        P.emit("sp", lambda hh: hh.dma_start(out=dscr, in_=t4[0:64, :, :]), reads=[("t4",)], writes=[("dscr",)], sem=P.new_sem("st_dscr"), inc=16)
        cm = rm_alloc(4 * FB).rearrange("p (k f) -> p k f", k=4)
        cm_sem = P.new_sem("ld_cm")
        dscr_v = dscr.rearrange("(q f) k m -> f k q m", f=4)
        bp_t = rm_alloc(2 * FB).rearrange("p (a f) -> p a f", a=2)
        bp_sem = P.new_sem("ld_bp")
        K = lambda n: ("sc", n)
        u1 = rm_alloc(FB); u2 = rm_alloc(FB)
        qre, qim = cm[:, 2, :], cm[:, 3, :]
        arc, aic = cm[:, 0, :], cm[:, 1, :]
        v3 = lambda a: a.rearrange("p (c m) -> p c m", m=128)
        bbt_v = bbt[:].rearrange("p (q f) r m -> p q f r m", f=4)
        abt_v = abt[:].rearrange("p (q f) r m -> p q f r m", f=4)
        for blk in range(2048 // FB):
            f0 = blk * FB
            q0 = f0 // 128
            for g4 in range(4):
                for k_ in range(4):
                    P.emit("sp", lambda hh, g4=g4, q0=q0, k_=k_: hh.dma_start(
                        out=cm[32 * g4:32 * g4 + 32, k_, :].rearrange("p (q m) -> p q m", m=128),
                        in_=dscr_v[g4, k_, q0:q0 + NQB, :].partition_broadcast(32)),
                        reads=[("dscr",)], writes=[("sc", "are"), ("sc", "aim"), ("sc", "qre"), ("sc", "qim")], sem=cm_sem, inc=16)
            P.emit("sp", lambda hh, f0=f0: hh.dma_start(out=bp_t, in_=bp[:, :, f0:f0 + FB]), writes=[("bp_t",)], sem=bp_sem, inc=16)
            b_re, b_im = bbk[:, :, 0, :], bbk[:, :, 1, :]
            a_re, a_im = abk[:, :, 0, :], abk[:, :, 1, :]
            P.emit("dve", lambda hh: hh.tensor_tensor(out=u1, in0=qre, in1=bp_t[:, 0, :], op=ALU.mult), reads=[K("qre"), ("bp_t",), K("r")], writes=[K("r")])
            P.emit("dve", lambda hh: hh.tensor_tensor(out=u2, in0=qim, in1=bp_t[:, 1, :], op=ALU.mult), reads=[K("qim"), ("bp_t",), K("msk")], writes=[K("msk")])
            P.emit("dve", lambda hh: hh.tensor_tensor(out=b_re, in0=v3(u1), in1=v3(u2), op=ALU.subtract), reads=[K("r"), K("msk")], writes=[("bbk", 0)])
            P.emit("dve", lambda hh: hh.tensor_tensor(out=u1, in0=qre, in1=bp_t[:, 1, :], op=ALU.mult), reads=[K("qre"), ("bp_t",), K("r")], writes=[K("r")])
            P.emit("dve", lambda hh: hh.tensor_tensor(out=u2, in0=qim, in1=bp_t[:, 0, :], op=ALU.mult), reads=[K("qim"), ("bp_t",), K("msk")], writes=[K("msk")])
            P.emit("dve", lambda hh: hh.tensor_tensor(out=b_im, in0=v3(u1), in1=v3(u2), op=ALU.add), reads=[K("r"), K("msk")], writes=[("bbk", 1)])
            P.emit("dve", lambda hh: hh.tensor_tensor(out=v3(u1), in0=v3(arc), in1=b_re, op=ALU.mult), reads=[K("are"), ("bbk", 0), K("r")], writes=[K("r")])
            P.emit("dve", lambda hh: hh.tensor_tensor(out=v3(u2), in0=v3(aic), in1=b_im, op=ALU.mult), reads=[K("aim"), ("bbk", 1), K("msk")], writes=[K("msk")])
            P.emit("dve", lambda hh: hh.tensor_tensor(out=a_re, in0=v3(u1), in1=v3(u2), op=ALU.subtract), reads=[K("r"), K("msk")], writes=[("abk", 0)])
            P.emit("dve", lambda hh: hh.tensor_tensor(out=v3(u1), in0=v3(arc), in1=b_im, op=ALU.mult), reads=[K("are"), ("bbk", 1), K("r")], writes=[K("r")])
            P.emit("dve", lambda hh: hh.tensor_tensor(out=v3(u2), in0=v3(aic), in1=b_re, op=ALU.mult), reads=[K("aim"), ("bbk", 0), K("msk")], writes=[K("msk")])
            P.emit("dve", lambda hh: hh.tensor_tensor(out=a_im, in0=v3(u1), in1=v3(u2), op=ALU.add), reads=[K("r"), K("msk")], writes=[("abk", 1)])
            for g4 in range(4):
                for ri in range(2):
                    P.emit("dve", lambda hh, g4=g4, ri=ri, q0=q0: hh.tensor_copy(out=bbt_v[32 * g4:32 * g4 + 32, q0:q0 + NQB, g4, ri, :], in_=bbk[32 * g4:32 * g4 + 32, :, ri, :]),
                           reads=[("bbk", ri)], writes=[("bbtp", ri)])
                    P.emit("dve", lambda hh, g4=g4, ri=ri, q0=q0: hh.tensor_copy(out=abt_v[32 * g4:32 * g4 + 32, q0:q0 + NQB, g4, ri, :], in_=abk[32 * g4:32 * g4 + 32, :, ri, :]),
                           reads=[("abk", ri)], writes=[("abtp", ri)])
        P.barrier()

        bank_rr = [0]

        def next_banks(n):
            s = bank_rr[0] % 2
            bank_rr[0] += 1
            return [3 * s + i for i in range(n)]

        def mm_group(banks, subs, slots_rhs, extra_reads=()):
            total = sum(x[4] for x in slots_rhs)
            idx = 0
            last = None
            for (lf, wkey, rf, rkeys, nk) in slots_rhs:
                for k in range(nk):
                    for si, (c0, n) in enumerate(subs):
                        is_last = (idx == total - 1) and (si == len(subs) - 1)
                        last = P.emit("pe", lambda hh, b=banks[si], n=n, c0=c0, k=k, lf=lf, rf=rf, st_=(idx == 0), sp_=(idx == total - 1):
                                      hh.matmul(ps[:, b, 0:n], lhsT=lf(k), rhs=rf(k, c0, n), start=st_, stop=sp_),
                                      reads=[wkey] + list(rkeys(k)) + list(extra_reads), writes=[("ps", banks[si])], signal=is_last)
                    idx += 1
            return last

        def rmsnorm(subs, TT, gi, dst_fn, dst_key):
            bnk = [6, 7, 6][:len(subs)]
            for c in range(NCH):
                sl = c % 2
                P.emit("act", lambda hh, c=c, sl=sl: hh.activation(out=sqb[:, sl, 0:TT], in_=h[:, c, 0:TT], func=AF.Square),
                       reads=[("h", c)], writes=[("sqb", sl)])
                for si, (c0, n) in enumerate(subs):
                    bb = 6 + (si % 2) if len(subs) <= 2 else [6, 7, 5][si]
                    P.emit("pe", lambda hh, bb=bb, c=c, sl=sl, c0=c0, n=n: hh.matmul(ps[:, bb, 0:n], lhsT=ones[:], rhs=sqb[:, sl, c0:c0 + n],
                                                                                 start=(c == 0), stop=(c == NCH - 1)),
                           reads=[("ones",), ("sqb", sl)], writes=[("ps", bb)], signal=True)
            for si, (c0, n) in enumerate(subs):
                bb = 6 + (si % 2) if len(subs) <= 2 else [6, 7, 5][si]
                P.emit("act", lambda hh, bb=bb, c0=c0, n=n: hh.activation(out=rstd[:, c0:c0 + n], in_=ps[:, bb, 0:n], func=AF.Sqrt, bias=eps_t[:, 0:1], scale=1.0 / D),
                       reads=[("ps", bb), ("eps",)], writes=[("rstd", si)])
                P.emit("dve", lambda hh, c0=c0, n=n: hh.reciprocal(out=rstd[:, c0:c0 + n], in_=rstd[:, c0:c0 + n]),
                       reads=[("rstd", si)], writes=[("rstd", si)])
            for c in range(NCH):
                P.emit("dve", lambda hh, c=c: hh.scalar_tensor_tensor(out=dst_fn(c), in0=h[:, c, 0:TT], scalar=gam_t[:, gi, c:c + 1], in1=rstd[:, 0:TT],
                                                                    op0=ALU.mult, op1=ALU.mult),
                       reads=[("h", c), ("gam",)] + [("rstd", si) for si in range(len(subs))], writes=[(dst_key, c)])

        eps_t = sb("eps_t", [128, 1])
        P.emit("dve", lambda hh: hh.memset(eps_t[:], EPS), writes=[("eps",)])

        def h_add_psum(m, banks, subs):
            for si, (c0, n) in enumerate(subs):
                P.emit("dve", lambda hh, m=m, b=banks[si], c0=c0, n=n: hh.tensor_tensor(out=h[:, m, c0:c0 + n], in0=h[:, m, c0:c0 + n], in1=ps[:, b, 0:n], op=ALU.add),
                       reads=[("ps", banks[si]), ("h", m)], writes=[("h", m)])

        def mlp(layer, subs, TT):
            def up(qd):
                hb = hq[qd % 2]
                for m16 in range(16):
                    f = qd * 16 + m16
                    slot = load_w(w_up[layer, f])
                    banks = next_banks(len(subs))
                    mm_group(banks, subs, [(lambda k, slot=slot: wv(slot)[:, k, :], ("w", slot),
                                            lambda k, c0, n: xn[:, k, c0:c0 + n], lambda k: [("xn", k)], 16)])
                    for si, (c0, n) in enumerate(subs):
                        P.emit("act", lambda hh, b=banks[si], c0=c0, n=n: hh.activation(out=relu_t[:, 0:n], in_=ps[:, b, 0:n], func=AF.Relu),
                               reads=[("ps", banks[si])], writes=[("relu",)])
                        P.emit("dve", lambda hh, hb=hb, m16=m16, c0=c0, n=n: hh.tensor_tensor(out=hb[:, m16, c0:c0 + n], in0=relu_t[:, 0:n], in1=relu_t[:, 0:n], op=ALU.mult),
                               reads=[("relu",)], writes=[("hq", qd % 2, m16)])

            def down(qd):
                hb = hq[qd % 2]
                for m in range(16):
                    slot = load_w(w_dn[layer, qd * 16 + m])
                    banks = next_banks(len(subs))
                    mm_group(banks, subs, [(lambda k, slot=slot: wv(slot)[:, k, :], ("w", slot),
                                            lambda k, c0, n, hb=hb: hb[:, k, c0:c0 + n], lambda k, qd=qd: [("hq", qd % 2, k)], 16)])
                    h_add_psum(m, banks, subs)
            up(0)
            for qd in range(1, 4):
                up(qd)
                down(qd - 1)
            down(3)

        def conv_mixer(subs, TT, has_samp, tile_is_last_b):
            npv = 2 + NPT + (NSS * 6 if has_samp else 0)
            W = npv - 2
            for m in range(NCH):
                vb = vbuf[m % 2]
                vk = ("vbuf", m % 2)
                s_c = load_w(w_in[16 + m])
                s_h = load_w(w_in[32 + m])
                s_b = load_w(w_in[m])
                rk = lambda k: [("xn", k)]
                rf = lambda k, c0, n: xn[:, k, c0:c0 + n]
                bk_c = next_banks(len(subs))
                mm_group(bk_c, subs, [(lambda k, s=s_c: wv(s)[:, k, :], ("w", s_c), rf, rk, 16)])
                bk_h = next_banks(len(subs))
                mm_group(bk_h, subs, [(lambda k, s=s_h: wv(s)[:, k, :], ("w", s_h), rf, rk, 16)])
                P.emit("dve", lambda hh, vb=vb, m=m: hh.tensor_copy(out=vb[:, 0:2], in_=carry_v[:, m, :]), reads=[("carry_v",)], writes=[vk])
                if has_samp:
                    vs = vb[:, 2 + NPT:2 + NPT + NSS * 6].rearrange("p (s t) -> p s t", t=6)
                    P.emit("dve", lambda hh, vs=vs, m=m: hh.tensor_copy(out=vs[:, :, 0:2], in_=convs_t[:, m, :, :]), reads=[("convs_t",)], writes=[vk])
                for si, (c0, n) in enumerate(subs):
                    P.emit("act", lambda hh, b=bk_c[si], c0=c0, n=n: hh.activation(out=tmpc[:, c0:c0 + n], in_=ps[:, b, 0:n], func=AF.Copy),
                           reads=[("ps", bk_c[si])], writes=[("tmpc", si)])
                    if c0 < NPT:
                        P.emit("dve", lambda hh, vb=vb, b=bk_h[si], c0=c0, n=n: hh.tensor_tensor(out=vb[:, 2 + c0:2 + c0 + n], in0=tmpc[:, c0:c0 + n], in1=ps[:, b, 0:n], op=ALU.mult),
                               reads=[("tmpc", si), ("ps", bk_h[si])], writes=[vk])
                    else:
                        vs = vb[:, 2 + NPT:2 + NPT + NSS * 6].rearrange("p (s t) -> p s t", t=6)
                        P.emit("dve", lambda hh, vs=vs, b=bk_h[si], c0=c0, n=n: hh.tensor_tensor(
                            out=vs[:, :, 2:6], in0=tmpc[:, c0:c0 + n].rearrange("p (s t) -> p s t", t=4),
                            in1=ps[:, b, 0:n].rearrange("p (s t) -> p s t", t=4), op=ALU.mult),
                            reads=[("tmpc", si), ("ps", bk_h[si])], writes=[vk])
                bk_b = next_banks(len(subs))
                mm_group(bk_b, subs, [(lambda k, s=s_b: wv(s)[:, k, :], ("w", s_b), rf, rk, 16)])
                P.emit("dve", lambda hh, vb=vb, m=m: hh.tensor_copy(out=carry_v[:, m, :], in_=vb[:, NPT:NPT + 2]), reads=[vk], writes=[("carry_v",)])
                if has_samp:
                    vs = vb[:, 2 + NPT:2 + NPT + NSS * 6].rearrange("p (s t) -> p s t", t=6)
                    P.emit("dve", lambda hh, vs=vs, m=m: hh.tensor_copy(out=convs_t[:, m, :, :], in_=vs[:, :, 4:6]), reads=[vk], writes=[("convs_t",)])
                P.emit("dve", lambda hh, vb=vb, m=m: hh.tensor_scalar(out=convt[:, 0:W], in0=vb[:, 0:W], scalar1=convw_t[:, m, 0:1], scalar2=None, op0=ALU.mult),
                       reads=[vk, ("convw",)], writes=[("convt",)])
                P.emit("dve", lambda hh, vb=vb, m=m: hh.scalar_tensor_tensor(out=convt[:, 0:W], in0=vb[:, 1:W + 1], scalar=convw_t[:, m, 1:2], in1=convt[:, 0:W], op0=ALU.mult, op1=ALU.add),
                       reads=[vk, ("convw",), ("convt",)], writes=[("convt",)])
                P.emit("dve", lambda hh, vb=vb, m=m: hh.scalar_tensor_tensor(out=convt[:, 0:W], in0=vb[:, 2:W + 2], scalar=convw_t[:, m, 2:3], in1=convt[:, 0:W], op0=ALU.mult, op1=ALU.add),
                       reads=[vk, ("convw",), ("convt",)], writes=[("convt",)])
                for si, (c0, n) in enumerate(subs):
                    if c0 < NPT:
                        P.emit("dve", lambda hh, m=m, b=bk_b[si], c0=c0, n=n: hh.tensor_tensor(out=g0[:, m, c0:c0 + n], in0=convt[:, c0:c0 + n], in1=ps[:, b, 0:n], op=ALU.mult),
                               reads=[("convt",), ("ps", bk_b[si])], writes=[("g0", m)])
                    else:
                        cs_ = convt[:, NPT + 2:NPT + 2 + NSS * 6].rearrange("p (s t) -> p s t", t=6)
                        P.emit("dve", lambda hh, m=m, cs_=cs_, b=bk_b[si], c0=c0, n=n: hh.tensor_tensor(
                            out=g0[:, m, c0:c0 + n].rearrange("p (s t) -> p s t", t=4), in0=cs_[:, :, 0:4],
                            in1=ps[:, b, 0:n].rearrange("p (s t) -> p s t", t=4), op=ALU.mult),
                            reads=[("convt",), ("ps", bk_b[si])], writes=[("g0", m)])
            for m in range(NCH):
                slot = load_w(w_out[m])
                banks = next_banks(len(subs))
                mm_group(banks, subs, [(lambda k, slot=slot: wv(slot)[:, k, :], ("w", slot),
                                        lambda k, c0, n: g0[:, k, c0:c0 + n], lambda k: [("g0", k)], 16)])
                h_add_psum(m, banks, subs)

        def ssm_bproj(t0, n, ck, par, seg=None):
            for ri in range(2):
                for g16 in range(4):
                    bb = 6 + ((ri * 4 + g16) % 2)
                    for j in range(16):
                        gp = g16 * 16 + j
                        q = gp // 4
                        P.emit("pe", lambda hh, bb=bb, j=j, q=q, gp=gp, ri=ri: hh.matmul(
                            ps[:, bb, j * TC:j * TC + n], lhsT=bbt[:, gp, ri, :], rhs=xn[:, q, t0:t0 + n],
                            start=True, stop=False),
                            reads=[("bbtp", ri), ("xn", q), ("xnc", q, ck)], writes=[("ps", bb)], signal=False)
                        if seg is None:
                            P.emit("pe", lambda hh, bb=bb, j=j, q=q, gp=gp, ri=ri: hh.matmul(
                                ps[:, bb, j * TC + 1:j * TC + n], lhsT=abt[:, gp, ri, :], rhs=xn[:, q, t0:t0 + n - 1],
                                start=False, stop=True),
                                reads=[("abtp", ri), ("xn", q), ("xnc", q, ck)], writes=[("ps", bb)], signal=(j == 15))
                        else:
                            P.emit("pe", lambda hh, bb=bb, j=j, q=q, gp=gp, ri=ri: hh.matmul(
                                ps[:, bb, j * TC:j * TC + n].rearrange("p (s t) -> p s t", t=seg)[:, :, 1:seg], lhsT=abt[:, gp, ri, :],
                                rhs=xn[:, q, t0:t0 + n].rearrange("p (s t) -> p s t", t=seg)[:, :, 0:seg - 1],
                                start=False, stop=True),
                                reads=[("abtp", ri), ("xn", q), ("xnc", q, ck)], writes=[("ps", bb)], signal=(j == 15))
                    P.emit("act", lambda hh, bb=bb, ri=ri, g16=g16: hh.activation(
                        out=bu[par][:, ri, g16 * 16:g16 * 16 + 16, 0:n], in_=ps[:, bb, 0:16 * TC].rearrange("p (j t) -> p j t", j=16)[:, :, 0:n], func=AF.Copy),
                        reads=[("ps", bb)], writes=[("bu", par, ri, g16)])

        def ssm_recur(n, par, init_ap, init_key, boff=0):
            assert n % 2 == 0
            bk_ = [("bu", par, ri, g16) for ri in range(2) for g16 in range(4)]
            P.emit("dve", lambda hh: hh.memset(traj[:, :, :, 0], 0.0), writes=[("traj",)])
            P.emit("dve", lambda hh: hh.tensor_copy(out=traj[:, :, :, 1], in_=init_ap), reads=[init_key], writes=[("traj",)])
            P.emit("dve", lambda hh: hh.tensor_tensor(out=c1b[:], in0=m1[:], in1=init_ap, op=ALU.mult), reads=[("m1",), init_key], writes=[("cc", 1)])
            P.emit("dve", lambda hh: hh.tensor_tensor(out=c2b[:], in0=m2[:], in1=init_ap[:, ::-1, :], op=ALU.mult), reads=[("m2",), init_key], writes=[("cc", 2)])
            P.emit("dve", lambda hh: hh.tensor_tensor(out=c1b[:], in0=c1b[:], in1=c2b[:], op=ALU.add), reads=[("cc", 1), ("cc", 2)], writes=[("cc", 1)])
            P.emit("dve", lambda hh: hh.tensor_tensor(out=bu[par][:, :, :, boff], in0=c1b[:], in1=bu[par][:, :, :, boff], op=ALU.add),
                   reads=[("cc", 1)] + bk_, writes=[("w", par, "h")])
            for c in range(0, n, 2):
                P.emit("dve", lambda hh, c=c: hh.tensor_tensor(out=t1b[:], in0=m1x[:], in1=traj[:, :, :, c:c + 2], op=ALU.mult), reads=[("m1x",), ("traj",)], writes=[("tt", 1)])
                P.emit("dve", lambda hh, c=c: hh.tensor_tensor(out=t2b[:], in0=m2x[:], in1=traj[:, ::-1, :, c:c + 2], op=ALU.mult), reads=[("m2x",), ("traj",)], writes=[("tt", 2)])
                P.emit("dve", lambda hh, c=c: hh.tensor_tensor(out=t1b[:], in0=t1b[:], in1=bu[par][:, :, :, boff + c:boff + c + 2], op=ALU.add),
                       reads=[("tt", 1), ("w", par, "h")] + (bk_ if (c == 0 or c == n - 2) else []), writes=[("tt", 1)])
                P.emit("dve", lambda hh, c=c: hh.tensor_tensor(out=traj[:, :, :, c + 2:c + 4], in0=t1b[:], in1=t2b[:], op=ALU.add),
                       reads=[("tt", 1), ("tt", 2)], writes=[("traj",)])

        def ssm_trajb(n, boff=0):
            P.emit("act", lambda hh: hh.activation(out=trajb[:, :, :, boff:boff + n], in_=traj[:, :, :, 2:n + 2], func=AF.Copy), reads=[("traj",)], writes=[("trajb",)])

        def ssm_cproj(t0, n, ck, yb):
            for q in range(16):
                P.emit("pe", lambda hh, q=q: hh.matmul(
                    ps[:, yb, q * TC:q * TC + n], lhsT=ddg[:, q, :], rhs=xn[:, q, t0:t0 + n], start=True, stop=False),
                    reads=[("ddg",), ("xn", q), ("xnc", q, ck)], writes=[("ps", yb)], signal=False)
                for g4 in range(4):
                    gp = q * 4 + g4
                    for ri in range(2):
                        lastmm = (g4 == 3 and ri == 1)
                        P.emit("pe", lambda hh, q=q, gp=gp, g4=g4, ri=ri, lastmm=lastmm: hh.matmul(
                            ps[32 * g4:32 * g4 + 32, yb, q * TC:q * TC + n], lhsT=cpad[:, ri, gp * 32:(gp + 1) * 32], rhs=trajb[:, ri, gp, 0:n],
                            start=False, stop=(ri == 1), tile_position=(0, 32 * g4)),
                            reads=[("cpad", ri), ("trajb",)], writes=[("ps", yb)], signal=(q == 15 and lastmm))

        def ssm_gelu(t0, n, ck, yb):
            yv = ps[:, yb, 0:16 * TC]
            P.emit("act", lambda hh: hh.activation(out=gel[0][:], in_=yv, func=AF.Square), reads=[("ps", yb)], writes=[("gel", 0)])
            P.emit("dve", lambda hh: hh.tensor_scalar(out=gel[0][:], in0=gel[0][:], scalar1=0.044715, scalar2=1.0, op0=ALU.mult, op1=ALU.add),
                   reads=[("gel", 0)], writes=[("gel", 0)])
            P.emit("dve", lambda hh: hh.tensor_tensor(out=gel[1][:], in0=gel[0][:], in1=yv, op=ALU.mult), reads=[("gel", 0), ("ps", yb)], writes=[("gel", 1)])
            P.emit("act", lambda hh: hh.activation(out=gel[2][:], in_=gel[1][:], func=AF.Sigmoid, scale=1.5957691216057308), reads=[("gel", 1)], writes=[("gel", 2)])
            P.emit("dve", lambda hh: hh.tensor_tensor(
                out=xn[:, :, t0:t0 + n], in0=gel[2][:].rearrange("p (j t) -> p j t", j=16)[:, :, 0:n],
                in1=yv.rearrange("p (j t) -> p j t", j=16)[:, :, 0:n], op=ALU.mult),
                reads=[("gel", 2), ("ps", yb)], writes=[("xnc", q, ck) for q in range(16)])
            cks.add(ck)

        cks = set()

        def ssm_mixer(subs, TT, has_samp, full, is_last_b):
            cks.clear()
            chunks = []
            t0 = 0
            for n in [16] * 32 + [4]:
                chunks.append((t0, n, "p", None))
                t0 += n
            assert t0 == NPT
            if has_samp:
                for g_ in range(NSS // 4):
                    chunks.append((NPT + 16 * g_, 16, "s", g_))
            nprompt = 33
            ssm_bproj(chunks[0][0], chunks[0][1], 0, 0)
            pend = None
            segof = lambda kind: (4 if kind == "s" else None)
            for ci, (t0, n, kind, s_) in enumerate(chunks):
                par = ci % 2
                if ci + 1 < len(chunks):
                    ssm_bproj(chunks[ci + 1][0], chunks[ci + 1][1], ci + 1, (ci + 1) % 2, segof(chunks[ci + 1][2]))
                if kind == "p":
                    if ci > 0:
                        P.emit("dve", lambda hh, pn=chunks[ci - 1][1]: hh.tensor_copy(out=xcarry[:], in_=traj[:, :, :, pn + 1]), reads=[("traj",)], writes=[("xcarry",)])
                    ssm_recur(n, par, xcarry[:], ("xcarry",))
                    if ci == nprompt - 1:
                        P.emit("dve", lambda hh, n=n: hh.tensor_copy(out=xcarry[:], in_=traj[:, :, :, n + 1]), reads=[("traj",)], writes=[("xcarry",)])
                    if full:
                        ssm_trajb(n)
                else:
                    for l_ in range(4):
                        sq_ = 4 * s_ + l_
                        ssm_recur(4, par, ssms_t[:, sq_, :, :], ("ssms_t",), boff=4 * l_)
                        P.emit("dve", lambda hh, sq_=sq_: hh.tensor_copy(out=ssms_t[:, sq_, :, :], in_=traj[:, :, :, 5]), reads=[("traj",)], writes=[("ssms_t",)])
                        if full:
                            ssm_trajb(4, boff=4 * l_)
                if full:
                    yb = 3 + (ci % 2)
                    ssm_cproj(t0, n, ci, yb)
                    if pend is not None:
                        ssm_gelu(*pend)
                    pend = (t0, n, ci, yb)
            if full and pend is not None:
                ssm_gelu(*pend)
            if not full:
                return
            for m in range(NCH):
                sa = load_w(w_ga[m])
                sb_ = load_w(w_gb[m])
                ckl = sorted(cks)
                rk = lambda k, ckl=ckl: [("xn", k)] + [("xnc", k, c_) for c_ in ckl]
                rf = lambda k, c0, n: xn[:, k, c0:c0 + n]
                bka = next_banks(len(subs))
                mm_group(bka, subs, [(lambda k, s=sa: wv(s)[:, k, :], ("w", sa), rf, rk, 16)])
                bkb = next_banks(len(subs))
                mm_group(bkb, subs, [(lambda k, s=sb_: wv(s)[:, k, :], ("w", sb_), rf, rk, 16)])
                for si, (c0, n) in enumerate(subs):
                    P.emit("act", lambda hh, b=bkb[si], c0=c0, n=n: hh.activation(out=glu_t[0][:, c0:c0 + n], in_=ps[:, b, 0:n], func=AF.Sigmoid),
                           reads=[("ps", bkb[si])], writes=[("glu", 0, si)])
                    P.emit("dve", lambda hh, b=bka[si], c0=c0, n=n: hh.tensor_tensor(out=glu_t[1][:, c0:c0 + n], in0=glu_t[0][:, c0:c0 + n], in1=ps[:, b, 0:n], op=ALU.mult),
                           reads=[("glu", 0, si), ("ps", bka[si])], writes=[("glu", 1, si)])
                    P.emit("dve", lambda hh, m=m, c0=c0, n=n: hh.tensor_tensor(out=h[:, m, c0:c0 + n], in0=h[:, m, c0:c0 + n], in1=glu_t[1][:, c0:c0 + n], op=ALU.add),
                           reads=[("glu", 1, si), ("h", m)], writes=[("h", m)])

        tiles = []
        for ti in range(NTILE_HALF):
            tiles.append(dict(col=ti * NPT, full=False, samp=False, last=False, ocol=None))
        for ti in range(NTILE_HALF):
            lastb = ti == NTILE_HALF - 1
            tiles.append(dict(col=HALF + ti * NPT, full=True, samp=lastb, last=lastb, ocol=ti * NPT))

        for tinfo in tiles:
            full, samp = tinfo["full"], tinfo["samp"]
            TT = NPT + (NSAMP if samp else 0)
            subs = [(0, NPT // 2), (NPT // 2, NPT // 2)] + ([(NPT, NSAMP)] if samp else [])
            for c in range(NCH):
                pass
            P.emit("sp", lambda hh, col=tinfo["col"]: hh.dma_start(out=h[:, :, 0:NPT], in_=xin_v[:, :, col:col + NPT]),
                   writes=[("h", c) for c in range(NCH)], sem=ldx, inc=16)
            if samp:
                P.emit("sp", lambda hh: hh.dma_start(out=h[:, :, NPT:NPT + NSAMP], in_=xin_v[:, :, 2 * HALF:2 * HALF + NSAMP]),
                       writes=[("h", c) for c in range(NCH)], sem=ldx, inc=16)
            rm_switch()
            rmsnorm(subs, TT, 0, lambda c, TT=TT: xn[:, c, 0:TT], "xn")
            conv_mixer(subs, TT, samp, tinfo["last"])
            if stage >= 2:
                rm_switch()
                rmsnorm(subs, TT, 1, lambda c, TT=TT: xn[:, c, 0:TT], "xn")
                mlp(0, subs, TT)
            if stage >= 3:
                rm_switch()
                rmsnorm(subs, TT, 2, lambda c, TT=TT: xn[:, c, 0:TT], "xn")
                ssm_mixer(subs, TT, samp, full and stage >= 4, tinfo["last"])
            if full and stage >= 5:
                rm_switch()
                rmsnorm(subs, TT, 3, lambda c, TT=TT: xn[:, c, 0:TT], "xn")
                mlp(1, subs, TT)
            if full:
                rm_switch()
                if stage >= 6:
                    rmsnorm(subs, TT, 4, lambda c, TT=TT: yfin[:, c, 0:TT], "yfin")
                else:
                    for c in range(NCH):
                        P.emit("dve", lambda hh, c=c, TT=TT: hh.tensor_copy(out=yfin[:, c, 0:TT], in_=h[:, c, 0:TT]), reads=[("h", c)], writes=[("yfin", c)])
                oc = tinfo["ocol"]
                P.emit("sp", lambda hh, oc=oc: hh.dma_start(out=yout_v[:, :, oc:oc + NPT], in_=yfin[:, :, 0:NPT]),
                       reads=[("yfin", c) for c in range(NCH)], sem=st, inc=16)
                if samp:
                    P.emit("sp", lambda hh: hh.dma_start(out=yout_v[:, :, HALF:HALF + NSAMP], in_=yfin[:, :, NPT:NPT + NSAMP]),
                           reads=[("yfin", c) for c in range(NCH)], sem=st, inc=16)
        P.emit("sp", lambda hh: hh.dma_start(out=convp_out, in_=carry_v[:]), reads=[("carry_v",)], sem=st, inc=16)
        P.emit("sp", lambda hh: hh.dma_start(out=ssmp_out, in_=xcarry[:]), reads=[("xcarry",)], sem=st, inc=16)
        P.emit("sp", lambda hh: hh.dma_start(out=convs_out, in_=convs_t[:]), reads=[("convs_t",)], sem=st, inc=16)
        P.emit("sp", lambda hh: hh.dma_start(out=ssms_out, in_=ssms_t[:]), reads=[("ssms_t",)], sem=st, inc=16)
        P.barrier()

        with ExitStack() as es3:
            for s in P.sems:
                s.h = es3.enter_context(nc.semaphore(s.name))
            block = es3.enter_context(nc.Block())

            @block.tensor
            def _(e):
                for f in P.prog["pe"]:
                    f(e)

            @block.scalar
            def _(e):
                for f in P.prog["act"]:
                    f(e)

            @block.vector
            def _(e):
                for f in P.prog["dve"]:
                    f(e)

            @block.gpsimd
            def _(e):
                for f in P.prog["pool"]:
                    f(e)

            @block.sync
            def _(e):
                for f in P.prog["sp"]:
                    f(e)
    return nc


def _slabs(W):
    K, M = W.shape
    a = W.reshape(K // 128, 128, M // 128, 128)
    return np.ascontiguousarray(a.transpose(2, 1, 0, 3)).reshape(M // 128, 128, (K // 128) * 128)


def _fm(x):
    return np.ascontiguousarray(x.T)


def _prep_shared(inp):
    f = lambda k: np.asarray(inp[k], dtype=np.float32)
    sh = {}
    sh["w_in"] = _slabs(f("conv_w_in")[0])
    sh["w_out"] = _slabs(f("conv_w_out")[0])
    sh["w_up"] = np.stack([_slabs(f("mlp_w_up")[l]) for l in range(2)])
    dn = []
    for l in range(2):
        Wd = f("mlp_w_down")[l]
        q = [_slabs(Wd[qd * 2048:(qd + 1) * 2048]) for qd in range(4)]
        dn.append(np.concatenate(q, axis=0))
    sh["w_dn"] = np.stack(dn)
    sh["w_ga"] = _slabs(f("ssm_glu_w_a")[0])
    sh["w_gb"] = _slabs(f("ssm_glu_w_b")[0])
    gam = np.stack([f("norm_mixer")[0], f("norm_mlp")[0], f("norm_mixer")[1], f("norm_mlp")[1], f("norm_final")])
    sh["gam"] = np.ascontiguousarray(gam.reshape(5, 16, 128).transpose(2, 0, 1))
    sh["convw"] = np.ascontiguousarray(f("conv_w")[0].reshape(3, 16, 128).transpose(2, 1, 0))
    lre, lim, ls = f("ssm_lambda_re")[0], f("ssm_lambda_im")[0], f("ssm_log_step")[0]
    lsb = np.broadcast_to(ls[:, None], (128, 64))
    sm = lambda a: np.ascontiguousarray(a.reshape(64, 2, 64).transpose(1, 2, 0).reshape(128, 64))
    sh["lam_s"] = np.ascontiguousarray(np.stack([sm(lre), sm(lim), sm(lsb)], axis=1))
    def cmaj(a):
        t = a.reshape(16, 4, 2, 64)
        t = np.broadcast_to(t[:, :, None, None, :, :], (16, 4, 2, 16, 2, 64))
        return np.ascontiguousarray(t.transpose(1, 2, 3, 0, 4, 5)).reshape(128, 2048)
    tm = lambda a: np.tile(a.reshape(64, 128), (2, 1))
    sh["lam_t"] = np.ascontiguousarray(np.stack([tm(lre), tm(lim), tm(lsb)], axis=1))
    def bmaj(b):
        t = b.reshape(16, 4, 2, 64, 16)
        out = np.zeros((4, 2, 16, 16, 2, 64), np.float32)
        for j in range(2):
            out[:, j, :, :, j, :] = t[:, :, j].transpose(1, 3, 0, 2)
        return out.reshape(128, 2048)
    sh["bp"] = np.ascontiguousarray(np.stack([bmaj(f("ssm_b_re")[0]), bmaj(f("ssm_b_im")[0])], axis=1))
    def cmajp(c):
        t = c.reshape(64, 2, 16, 64)
        out = np.zeros((2, 64, 64, 2, 16), np.float32)
        for j in range(2):
            out[j, :, :, j, :] = t[:, j].transpose(2, 0, 1)
        return out.reshape(128, 64 * 32)
    sh["cp"] = np.ascontiguousarray(np.stack([cmajp(f("ssm_c_re")[0]), cmajp(f("ssm_c_im")[0])], axis=1))
    dv = f("ssm_d")[0].reshape(16, 128)
    ddm = np.zeros((128, 16, 128), np.float32)
    for q in range(16):
        ddm[np.arange(128), q, np.arange(128)] = dv[q]
    sh["dd"] = ddm.reshape(128, 2048)
    return sh


_NC_CACHE = {}


def kernel(**inp):
    sh = _prep_shared(inp)
    xp = np.asarray(inp["x_prompt"], np.float32)
    xs = np.asarray(inp["x_sample"], np.float32)
    meta = np.asarray(inp["meta_tokens"], np.float32)
    sc = np.asarray(inp["state_conv"], np.float32)[0]
    sre = np.asarray(inp["state_ssm_re"], np.float32)[0]
    sim = np.asarray(inp["state_ssm_im"], np.float32)[0]
    in_maps = []
    for c in range(8):
        i, r = c // 2, c % 2
        S = np.concatenate([meta, xp[i]], axis=0)
        A = np.zeros((HALF, D), np.float32) if r == 0 else S[:HALF]
        B = S[:HALF] if r == 0 else S[HALF:]
        smp = xs[NSS * c:NSS * (c + 1)].reshape(NSAMP, D)
        m = dict(sh)
        m["xin"] = _fm(np.concatenate([A, B, smp], axis=0))
        cs = sc[NSS * c:NSS * (c + 1)]
        m["convs_in"] = np.ascontiguousarray(cs.reshape(NSS, 2, 16, 128).transpose(3, 2, 0, 1))
        def st(a):
            return a.reshape(NSS, 64, 2, 64).transpose(2, 3, 0, 1).reshape(128, NSS, 64)
        m["ssms_in"] = np.ascontiguousarray(np.stack([st(sre[NSS * c:NSS * (c + 1)]), st(sim[NSS * c:NSS * (c + 1)])], axis=2))
        in_maps.append(m)
    if "nc" not in _NC_CACHE:
        _NC_CACHE["nc"] = build()
    res = run_bass_kernel_spmd(_NC_CACHE["nc"], in_maps, core_ids=list(range(8)))
    R = res.results
    y_prompt = np.zeros((4, 2048, D), np.float32)
    y_sample = np.zeros((128, 4, D), np.float32)
    conv_p = np.zeros((1, 4, 2, D), np.float32)
    re_p = np.zeros((1, 4, 128, 64), np.float32)
    im_p = np.zeros((1, 4, 128, 64), np.float32)
    conv_s = np.zeros((1, 128, 2, D), np.float32)
    re_s = np.zeros((1, 128, 128, 64), np.float32)
    im_s = np.zeros((1, 128, 128, 64), np.float32)
    for c in range(8):
        i, r = c // 2, c % 2
        yo = np.asarray(R[c]["y_out"]).T
        if r == 0:
            y_prompt[i, 0:HALF - 16] = yo[16:HALF]
        else:
            y_prompt[i, HALF - 16:] = yo[0:HALF]
        y_sample[NSS * c:NSS * (c + 1)] = yo[HALF:].reshape(NSS, 4, D)
        cso = np.asarray(R[c]["convs_out"])
        conv_s[0, NSS * c:NSS * (c + 1)] = cso.transpose(2, 3, 1, 0).reshape(NSS, 2, D)
        sso = np.asarray(R[c]["ssms_out"])
        t = sso.reshape(2, 64, NSS, 2, 64).transpose(2, 3, 4, 0, 1).reshape(NSS, 2, 128, 64)
        re_s[0, NSS * c:NSS * (c + 1)] = t[:, 0]
        im_s[0, NSS * c:NSS * (c + 1)] = t[:, 1]
        if r == 1:
            cpo = np.asarray(R[c]["convp_out"])
            conv_p[0, i] = cpo.transpose(2, 1, 0).reshape(2, D)
            spo = np.asarray(R[c]["ssmp_out"])
            t = spo.reshape(2, 64, 2, 64).transpose(2, 3, 0, 1).reshape(2, 128, 64)
            re_p[0, i] = t[0]
            im_p[0, i] = t[1]
    return (y_prompt, y_sample, conv_p, re_p, im_p, conv_s, re_s, im_s)
```

```python
import math
import os
import numpy as np
import concourse.bass as bass
import concourse.mybir as mybir
from concourse.bass_utils import run_bass_kernel_spmd

F32 = mybir.dt.float32
BF16 = mybir.dt.bfloat16
I32 = mybir.dt.int32
AF = mybir.ActivationFunctionType
ALU = mybir.AluOpType

D = 2048
NCH = 16
NPT = 516
NTILE_HALF = 2
HALF = NPT * NTILE_HALF
NSS = 16
NSAMP = NSS * 4
TTMAX = NPT + NSAMP
VW = 2 + NPT + NSS * 6
TC = 16
NW = 4
EPS = 1e-6
NTOK = 2 * HALF + NSAMP
ENGS = ["pe", "act", "dve", "pool", "sp"]


SEM_LIMIT = 1000


class Sem:
    def __init__(self, name, owner=None):
        self.name = name
        self.h = None
        self.count = 0
        self.owner = owner


class Prog:
    def __init__(self):
        self.prog = {e: [] for e in ENGS}
        self.waited = {e: {} for e in ENGS}
        self.esem = {e: Sem("e_" + e, e) for e in ENGS}
        self.nrot = 0
        self.sems = list(self.esem.values())
        self.lastw = {}
        self.readers = {}
        self.guard = {}

    def new_sem(self, name):
        s = Sem(name)
        self.sems.append(s)
        return s

    def _wait(self, eng, ev):
        if ev is None:
            return
        sem, val = ev
        if eng == "pe" and sem.owner == "pe":
            return
        w = self.waited[eng]
        if w.get(sem.name, 0) >= val:
            return
        w[sem.name] = val
        self.prog[eng].append(lambda h, sem=sem, val=val: h.wait_ge(sem.h, val))

    def emit(self, eng, fn, reads=(), writes=(), sem=None, inc=1, signal=True, extra=()):
        for ev in extra:
            self._wait(eng, ev)
        for b in reads:
            self._wait(eng, self.lastw.get(b))
            for ev in self.guard.get(b[0], ()):
                self._wait(eng, ev)
        for b in writes:
            self._wait(eng, self.lastw.get(b))
            for ev in self.readers.get(b, ()):
                self._wait(eng, ev)
            for ev in self.guard.get(b[0], ()):
                self._wait(eng, ev)
        if sem is None:
            if self.esem[eng].count >= SEM_LIMIT:
                self.nrot += 1
                ns = Sem(f"e_{eng}_{self.nrot}", eng)
                self.sems.append(ns)
                self.esem[eng] = ns
            s = self.esem[eng]
        else:
            s = sem
        if signal:
            s.count += inc
            ev = (s, s.count)
            self.prog[eng].append(lambda h, fn=fn, s=s, inc=inc: fn(h).then_inc(s.h, inc))
        else:
            ev = (s, s.count + inc)
            self.prog[eng].append(lambda h, fn=fn: fn(h))
        for b in reads:
            self.readers.setdefault(b, []).append(ev)
        for b in writes:
            self.lastw[b] = ev
            self.readers[b] = []
        return ev

    def all_events(self):
        return [(s, s.count) for s in self.sems if s.count > 0]

    def barrier(self):
        evs = self.all_events()
        for e in ENGS:
            for ev in evs:
                self._wait(e, ev)


def build(stage=99):
    nc = bass.Bass("TRN2", target_bir_lowering=False)
    P = Prog()

    def din(name, shape, dt=F32):
        return nc.dram_tensor(name, list(shape), dt, kind="ExternalInput").ap()

    def dout(name, shape, dt=F32):
        return nc.dram_tensor(name, list(shape), dt, kind="ExternalOutput").ap()

    xin = din("xin", [D, NTOK])
    convs_in = din("convs_in", [128, NCH, NSS, 2])
    ssms_in = din("ssms_in", [128, NSS, 2, 64])
    gam = din("gam", [128, 5, NCH])
    convw = din("convw", [128, NCH, 3])
    w_in = din("w_in", [48, 128, 2048])
    w_out = din("w_out", [16, 128, 2048])
    w_up = din("w_up", [2, 64, 128, 2048])
    w_dn = din("w_dn", [2, 64, 128, 2048])
    w_ga = din("w_ga", [16, 128, 2048])
    w_gb = din("w_gb", [16, 128, 2048])
    lam_s = din("lam_s", [128, 3, 64])
    lam_t = din("lam_t", [128, 3, 128])
    dscr = nc.dram_tensor("dscr", [64, 4, 128], F32, kind="Internal").ap()
    bp = din("bp", [128, 2, 2048])
    cp = din("cp", [128, 2, 64 * 32])
    dd = din("dd", [128, NCH * 128])

    y_out = dout("y_out", [D, HALF + NSAMP])
    convp_out = dout("convp_out", [128, NCH, 2])
    ssmp_out = dout("ssmp_out", [128, 2, 64])
    convs_out = dout("convs_out", [128, NCH, NSS, 2])
    ssms_out = dout("ssms_out", [128, NSS, 2, 64])

    xin_v = xin.rearrange("(c p) t -> p c t", p=128)
    yout_v = y_out.rearrange("(c p) t -> p c t", p=128)

    RM = 9600
    from contextlib import ExitStack
    with ExitStack() as es:
        def sb(name, shape, dt=F32):
            return es.enter_context(nc.sbuf_tensor(name, list(shape), dt))

        h = sb("h", [128, NCH, TTMAX])
        xn = sb("xn", [128, NCH, TTMAX], BF16)
        wsl = sb("wsl", [128, NW, 2048], BF16)
        rmix = sb("rmix", [128, RM])
        rstd = sb("rstd", [128, TTMAX])
        sqb = sb("sqb", [128, 2, TTMAX], BF16)
        ones = sb("ones", [128, 128], BF16)
        gam_t = sb("gam_t", [128, 5, NCH])
        convw_t = sb("convw_t", [128, NCH, 3])
        convs_t = sb("convs_t", [128, NCH, NSS, 2])
        carry_v = sb("carry_v", [128, NCH, 2])
        xcarry = sb("xcarry", [128, 2, 64])
        ssms_t = sb("ssms_t", [128, NSS, 2, 64])
        m1 = sb("m1", [128, 2, 64])
        m2 = sb("m2", [128, 2, 64])
        bbt = sb("bbt", [128, 64, 2, 128], BF16)
        abt = sb("abt", [128, 64, 2, 128], BF16)
        cpad = sb("cpad", [128, 2, 64 * 32], BF16)
        ddg = sb("ddg", [128, NCH, 128], BF16)
        ps = es.enter_context(nc.psum_tensor("ps", [128, 8, 512], F32))

        def rm_f32(off, n):
            return rmix[:, off:off + n]

        def rm_bf16(off, n):
            return rmix[:, off:off + n].bitcast(BF16)

        vbuf = [rm_f32(0, VW), rm_f32(VW, VW)]
        tmpc = rm_f32(2 * VW, TTMAX)
        convt = rm_f32(2 * VW + TTMAX, VW)
        g0 = rm_bf16(3 * VW + TTMAX, NCH * TTMAX // 2).rearrange("p (c t) -> p c t", c=NCH)
        HQW = NCH * TTMAX // 2
        hq = [rm_bf16(0, HQW).rearrange("p (c t) -> p c t", c=NCH),
              rm_bf16(HQW, HQW).rearrange("p (c t) -> p c t", c=NCH)]
        relu_t = rm_bf16(2 * HQW, TTMAX // 2 + 2)
        glu_t = [rm_f32(0, TTMAX), rm_f32(TTMAX, TTMAX)]
        BUW = 128 * TC
        v4 = lambda off, t_: rm_f32(off, 128 * t_).rearrange("p (r g t) -> p r g t", r=2, g=64)
        bu = [v4(0, TC), v4(BUW, TC)]
        TRW = 128 * (TC + 2)
        traj = v4(2 * BUW, TC + 2)
        trajb = rm_bf16(2 * BUW + TRW, BUW // 2).rearrange("p (r g t) -> p r g t", r=2, g=64)
        o1 = 2 * BUW + TRW + BUW // 2
        t1b = v4(o1, 2)
        t2b = v4(o1 + 256, 2)
        c1b = rm_f32(o1 + 512, 128).rearrange("p (r g) -> p r g", r=2)
        c2b = rm_f32(o1 + 640, 128).rearrange("p (r g) -> p r g", r=2)
        o2 = o1 + 768
        GW = 16 * TC
        gel = [rm_f32(o2 + i * GW, GW) for i in range(3)]
        assert o2 + 3 * GW <= RM and 2 * HQW + TTMAX // 2 + 2 <= RM and 3 * VW + TTMAX + NCH * TTMAX // 2 <= RM
        yfin = rm_f32(0, NCH * TTMAX).rearrange("p (c t) -> p c t", c=NCH) if NCH * TTMAX <= RM else None
        assert yfin is not None

        RMN = ("w", "tmpw", "cc", "g0", "vbuf", "tmpc", "convt", "hq", "relu", "glu", "bu", "traj", "trajb", "tt", "gel", "yfin")

        def rm_switch():
            evs = P.all_events()
            for n in RMN:
                P.guard[n] = evs

        wsem = [P.new_sem(f"w{i}") for i in range(NW)]
        wcount = [0]

        def load_w(src_ap):
            k = wcount[0]
            wcount[0] += 1
            slot = k % NW
            if wsem[slot].count >= SEM_LIMIT:
                wsem[slot] = P.new_sem(f"w{slot}_{k}")
            P.emit("pool", lambda hh, slot=slot, src_ap=src_ap: hh.dma_start(out=wsl[:, slot, :], in_=src_ap),
                   writes=[("w", slot)], sem=wsem[slot], inc=16)
            return slot

        def wv(slot):
            return wsl[:, slot, :].rearrange("p (k m) -> p k m", k=16)

        ldx = P.new_sem("ldx")
        st = P.new_sem("st")

        def load_small(dst, src, key):
            P.emit("sp", lambda hh: hh.dma_start(out=dst, in_=src), writes=[key], sem=P.new_sem("ld_" + key[0]), inc=16)

        load_small(gam_t[:], gam, ("gam",))
        load_small(convw_t[:], convw, ("convw",))
        load_small(convs_t[:], convs_in, ("convs_t",))
        load_small(ssms_t[:], ssms_in, ("ssms_t",))
        P.emit("dve", lambda hh: hh.memset(ones[:], 1.0), writes=[("ones",)])
        P.emit("dve", lambda hh: hh.memset(carry_v[:], 0.0), writes=[("carry_v",)])
        P.emit("dve", lambda hh: hh.memset(xcarry[:], 0.0), writes=[("xcarry",)])
        P.emit("pool", lambda hh: hh.dma_start(out=cpad[:, 0, :], in_=cp[:, 0, :]), writes=[("cpad", 0)], sem=P.new_sem("ldp0"), inc=16)
        P.emit("pool", lambda hh: hh.dma_start(out=cpad[:, 1, :], in_=cp[:, 1, :]), writes=[("cpad", 1)], sem=P.new_sem("ldp1"), inc=16)
        P.emit("pool", lambda hh: hh.dma_start(out=ddg[:].rearrange("p c m -> p (c m)"), in_=dd), writes=[("ddg",)], sem=P.new_sem("ldp2"), inc=16)
        P.emit("act", lambda hh: hh.activation(out=cpad[:, 1, :], in_=cpad[:, 1, :], func=AF.Copy, scale=-1.0),
               reads=[("cpad", 1)], writes=[("cpad", 1)])

        TWO_PI = 2.0 * math.pi

        rm_bump = [0]

        def rm_alloc(n):
            o = rm_bump[0]
            rm_bump[0] += n
            assert rm_bump[0] <= RM
            return rmix[:, o:o + n]

        def cplx_setup(tag, F, want_q):
            T = {}

            def t(n):
                T[n] = rm_alloc(F)
                return T[n]
            lam_t = rm_alloc(3 * F).rearrange("p (a f) -> p a f", a=3)
            lam_sem = P.new_sem("ld_lam_" + tag)
            lr, li, ls = lam_t[:, 0, :], lam_t[:, 1, :], lam_t[:, 2, :]
            K = lambda n: (tag, n)
            dt_ = t("dt"); mag = t("mag"); th = t("th"); kf = t("kf")
            r = t("r"); msk = t("msk"); sn = t("sn"); cs = t("cs"); are = t("are"); aim = t("aim")
            if want_q:
                den = t("den"); nr = t("nr"); qre = t("qre"); qim = t("qim"); tq = t("tq")
            fact = [1.0]
            for i_ in range(1, 20):
                fact.append(fact[-1] * i_)
            EXPC = [1.0 / fact[i_] for i_ in range(11)]
            SINC = [((-1.0) ** i_) / fact[2 * i_ + 1] for i_ in range(8)]
            COSC = [((-1.0) ** i_) / fact[2 * i_] for i_ in range(9)]

            def poly(dst, w, coeffs, kd, kw):
                n_ = len(coeffs) - 1
                P.emit("dve", lambda hh: hh.tensor_scalar(out=dst, in0=w, scalar1=coeffs[n_], scalar2=None, op0=ALU.mult), reads=[K(kw)], writes=[K(kd)])
                for i_ in range(n_ - 1, 0, -1):
                    P.emit("dve", lambda hh, c_=coeffs[i_]: hh.scalar_tensor_tensor(out=dst, in0=dst, scalar=c_, in1=w, op0=ALU.add, op1=ALU.mult),
                           reads=[K(kd), K(kw)], writes=[K(kd)])
                P.emit("dve", lambda hh: hh.tensor_scalar(out=dst, in0=dst, scalar1=coeffs[0], scalar2=None, op0=ALU.add), reads=[K(kd)], writes=[K(kd)])

            def tt(out, a_, b_, op, ko, ka, kb):
                P.emit("dve", lambda hh: hh.tensor_tensor(out=out, in0=a_, in1=b_, op=op), reads=[K(ka), K(kb)], writes=[K(ko)])

            def run(lam_ap):
                P.emit("sp", lambda hh: hh.dma_start(out=lam_t, in_=lam_ap), writes=[K("lam")], sem=lam_sem, inc=16)
                P.emit("dve", lambda hh: hh.tensor_scalar(out=kf, in0=ls, scalar1=1.0 / 16.0, scalar2=None, op0=ALU.mult), reads=[K("lam")], writes=[K("kf")])
                poly(dt_, kf, EXPC, "dt", "kf")
                for _ in range(4):
                    tt(dt_, dt_, dt_, ALU.mult, "dt", "dt", "dt")
                tt(kf, lr, dt_, ALU.mult, "kf", "lam", "dt")
                poly(mag, kf, EXPC[:9], "mag", "kf")
                tt(th, li, dt_, ALU.mult, "th", "lam", "dt")
                P.emit("dve", lambda hh: hh.tensor_scalar(out=th, in0=th, scalar1=1.0 / 16.0, scalar2=None, op0=ALU.mult), reads=[K("th")], writes=[K("th")])
                tt(r, th, th, ALU.mult, "r", "th", "th")
                poly(sn, r, SINC, "sn", "r")
                tt(sn, sn, th, ALU.mult, "sn", "sn", "th")
                poly(cs, r, COSC, "cs", "r")
                for _ in range(4):
                    tt(msk, cs, sn, ALU.mult, "msk", "cs", "sn")
                    tt(cs, cs, cs, ALU.mult, "cs", "cs", "cs")
                    tt(sn, sn, sn, ALU.mult, "sn", "sn", "sn")
                    tt(cs, cs, sn, ALU.subtract, "cs", "cs", "sn")
                    P.emit("dve", lambda hh: hh.tensor_scalar(out=sn, in0=msk, scalar1=2.0, scalar2=None, op0=ALU.mult), reads=[K("msk")], writes=[K("sn")])
                P.emit("dve", lambda hh: hh.tensor_tensor(out=are, in0=mag, in1=cs, op=ALU.mult), reads=[K("mag"), K("cs")], writes=[K("are")])
                P.emit("dve", lambda hh: hh.tensor_tensor(out=aim, in0=mag, in1=sn, op=ALU.mult), reads=[K("mag"), K("sn")], writes=[K("aim")])
                if want_q:
                    P.emit("dve", lambda hh: hh.tensor_tensor(out=den, in0=lr, in1=lr, op=ALU.mult), reads=[K("lam")], writes=[K("den")])
                    P.emit("dve", lambda hh: hh.tensor_tensor(out=tq, in0=li, in1=li, op=ALU.mult), reads=[K("lam")], writes=[K("tq")])
                    P.emit("dve", lambda hh: hh.tensor_tensor(out=den, in0=den, in1=tq, op=ALU.add), reads=[K("den"), K("tq")], writes=[K("den")])
                    P.emit("dve", lambda hh: hh.reciprocal(out=den, in_=den), reads=[K("den")], writes=[K("den")])
                    P.emit("dve", lambda hh: hh.tensor_scalar(out=nr, in0=are, scalar1=-1.0, scalar2=None, op0=ALU.add), reads=[K("are")], writes=[K("nr")])
                    P.emit("dve", lambda hh: hh.tensor_tensor(out=qre, in0=nr, in1=lr, op=ALU.mult), reads=[K("nr"), K("lam")], writes=[K("qre")])
                    P.emit("dve", lambda hh: hh.tensor_tensor(out=tq, in0=aim, in1=li, op=ALU.mult), reads=[K("aim"), K("lam")], writes=[K("tq")])
                    P.emit("dve", lambda hh: hh.tensor_tensor(out=qre, in0=qre, in1=tq, op=ALU.add), reads=[K("qre"), K("tq")], writes=[K("qre")])
                    P.emit("dve", lambda hh: hh.tensor_tensor(out=qre, in0=qre, in1=den, op=ALU.mult), reads=[K("qre"), K("den")], writes=[K("qre")])
                    P.emit("dve", lambda hh: hh.tensor_tensor(out=qim, in0=aim, in1=lr, op=ALU.mult), reads=[K("aim"), K("lam")], writes=[K("qim")])
                    P.emit("dve", lambda hh: hh.tensor_tensor(out=tq, in0=nr, in1=li, op=ALU.mult), reads=[K("nr"), K("lam")], writes=[K("tq")])
                    P.emit("dve", lambda hh: hh.tensor_tensor(out=qim, in0=qim, in1=tq, op=ALU.subtract), reads=[K("qim"), K("tq")], writes=[K("qim")])
                    P.emit("dve", lambda hh: hh.tensor_tensor(out=qim, in0=qim, in1=den, op=ALU.mult), reads=[K("qim"), K("den")], writes=[K("qim")])
            return T, run

        Ts, run_s = cplx_setup("ss", 64, False)
        run_s(lam_s)
        K = lambda n: ("ss", n)
        P.emit("dve", lambda hh: hh.tensor_copy(out=m1[:, 0, :], in_=Ts["are"]), reads=[K("are")], writes=[("m1",)])
        P.emit("dve", lambda hh: hh.tensor_copy(out=m1[:, 1, :], in_=Ts["are"]), reads=[K("are")], writes=[("m1",)])
        P.emit("dve", lambda hh: hh.tensor_copy(out=m2[:, 1, :], in_=Ts["aim"]), reads=[K("aim")], writes=[("m2",)])
        P.emit("dve", lambda hh: hh.tensor_scalar(out=m2[:, 0, :], in0=Ts["aim"], scalar1=-1.0, scalar2=None, op0=ALU.mult), reads=[K("aim")], writes=[("m2",)])
        m1x = sb("m1x", [128, 2, 64, 2])
        m2x = sb("m2x", [128, 2, 64, 2])
        a2re = rm_alloc(64)
        a2im = rm_alloc(64)
        a2t = rm_alloc(64)
        P.emit("dve", lambda hh: hh.tensor_tensor(out=a2re, in0=Ts["are"], in1=Ts["are"], op=ALU.mult), reads=[K("are")], writes=[("a2", 0)])
        P.emit("dve", lambda hh: hh.tensor_tensor(out=a2t, in0=Ts["aim"], in1=Ts["aim"], op=ALU.mult), reads=[K("aim")], writes=[("a2", 2)])
        P.emit("dve", lambda hh: hh.tensor_tensor(out=a2re, in0=a2re, in1=a2t, op=ALU.subtract), reads=[("a2", 0), ("a2", 2)], writes=[("a2", 0)])
        P.emit("dve", lambda hh: hh.tensor_tensor(out=a2im, in0=Ts["are"], in1=Ts["aim"], op=ALU.mult), reads=[K("are"), K("aim")], writes=[("a2", 1)])
        for ri_ in range(2):
            for ph_ in range(2):
                P.emit("dve", lambda hh, ri_=ri_, ph_=ph_: hh.tensor_copy(out=m1x[:, ri_, :, ph_], in_=a2re), reads=[("a2", 0)], writes=[("m1x",)])
                P.emit("dve", lambda hh, ri_=ri_, ph_=ph_: hh.tensor_scalar(out=m2x[:, ri_, :, ph_], in0=a2im, scalar1=(-2.0 if ri_ == 0 else 2.0), scalar2=None, op0=ALU.mult),
                       reads=[("a2", 1)], writes=[("m2x",)])
        FB = 256
        NQB = FB // 128
        bbk = rm_alloc(NQB * 2 * 128).rearrange("p (c r m) -> p c r m", c=NQB, r=2)
        abk = rm_alloc(NQB * 2 * 128).rearrange("p (c r m) -> p c r m", c=NQB, r=2)
        P.emit("dve", lambda hh: hh.memset(bbt[:].rearrange("p a r m -> p (a r m)"), 0.0), writes=[("bbtp", 0), ("bbtp", 1)])
        P.emit("dve", lambda hh: hh.memset(abt[:].rearrange("p a r m -> p (a r m)"), 0.0), writes=[("abtp", 0), ("abtp", 1)])
        Tt_, run_t = cplx_setup("st", 128, True)
        run_t(lam_t)
        t4 = rm_alloc(4 * 128).rearrange("p (k m) -> p k m", k=4)
        for k_, nm_ in enumerate(("are", "aim", "qre", "qim")):
            P.emit("dve", lambda hh, k_=k_, nm_=nm_: hh.tensor_copy(out=t4[:, k_, :], in_=Tt_[nm_]), reads=[("st", nm_)], writes=[("t4",)])
        P.emit("sp", lambda hh: hh.dma_start(out=dscr, in_=t4[0:64, :, :]), reads=[("t4",)], writes=[("dscr",)], sem=P.new_sem("st_dscr"), inc=16)
        cm = rm_alloc(4 * FB).rearrange("p (k f) -> p k f", k=4)
        cm_sem = P.new_sem("ld_cm")
        dscr_v = dscr.rearrange("(q f) k m -> f k q m", f=4)
        bp_t = rm_alloc(2 * FB).rearrange("p (a f) -> p a f", a=2)
        bp_sem = P.new_sem("ld_bp")
        K = lambda n: ("sc", n)
        u1 = rm_alloc(FB); u2 = rm_alloc(FB)
        qre, qim = cm[:, 2, :], cm[:, 3, :]
        arc, aic = cm[:, 0, :], cm[:, 1, :]
        v3 = lambda a: a.rearrange("p (c m) -> p c m", m=128)
        bbt_v = bbt[:].rearrange("p (q f) r m -> p q f r m", f=4)
        abt_v = abt[:].rearrange("p (q f) r m -> p q f r m", f=4)
        cmk = [("cm", g4_, k__) for g4_ in range(4) for k__ in range(4)]
        for blk in range(2048 // FB):
            f0 = blk * FB
            q0 = f0 // 128
            for g4 in range(4):
                for k_ in range(4):
                    P.emit("sp", lambda hh, g4=g4, q0=q0, k_=k_: hh.dma_start(
                        out=cm[32 * g4:32 * g4 + 32, k_, :].rearrange("p (q m) -> p q m", m=128),
                        in_=dscr_v[g4, k_, q0:q0 + NQB, :].partition_broadcast(32)),
                        reads=[("dscr",)], writes=[("cm", g4, k_)], sem=cm_sem, inc=16)
            P.emit("sp", lambda hh, f0=f0: hh.dma_start(out=bp_t, in_=bp[:, :, f0:f0 + FB]), writes=[("bp_t",)], sem=bp_sem, inc=16)
            b_re, b_im = bbk[:, :, 0, :], bbk[:, :, 1, :]
            a_re, a_im = abk[:, :, 0, :], abk[:, :, 1, :]
            P.emit("dve", lambda hh: hh.tensor_tensor(out=u1, in0=qre, in1=bp_t[:, 0, :], op=ALU.mult), reads=[*cmk, ("bp_t",), K("r")], writes=[K("r")])
            P.emit("dve", lambda hh: hh.tensor_tensor(out=u2, in0=qim, in1=bp_t[:, 1, :], op=ALU.mult), reads=[*cmk, ("bp_t",), K("msk")], writes=[K("msk")])
            P.emit("dve", lambda hh: hh.tensor_tensor(out=b_re, in0=v3(u1), in1=v3(u2), op=ALU.subtract), reads=[K("r"), K("msk")], writes=[("bbk", 0)])
            P.emit("dve", lambda hh: hh.tensor_tensor(out=u1, in0=qre, in1=bp_t[:, 1, :], op=ALU.mult), reads=[*cmk, ("bp_t",), K("r")], writes=[K("r")])
            P.emit("dve", lambda hh: hh.tensor_tensor(out=u2, in0=qim, in1=bp_t[:, 0, :], op=ALU.mult), reads=[*cmk, ("bp_t",), K("msk")], writes=[K("msk")])
            P.emit("dve", lambda hh: hh.tensor_tensor(out=b_im, in0=v3(u1), in1=v3(u2), op=ALU.add), reads=[K("r"), K("msk")], writes=[("bbk", 1)])
            P.emit("dve", lambda hh: hh.tensor_tensor(out=v3(u1), in0=v3(arc), in1=b_re, op=ALU.mult), reads=[*cmk, ("bbk", 0), K("r")], writes=[K("r")])
            P.emit("dve", lambda hh: hh.tensor_tensor(out=v3(u2), in0=v3(aic), in1=b_im, op=ALU.mult), reads=[*cmk, ("bbk", 1), K("msk")], writes=[K("msk")])
            P.emit("dve", lambda hh: hh.tensor_tensor(out=a_re, in0=v3(u1), in1=v3(u2), op=ALU.subtract), reads=[K("r"), K("msk")], writes=[("abk", 0)])
            P.emit("dve", lambda hh: hh.tensor_tensor(out=v3(u1), in0=v3(arc), in1=b_im, op=ALU.mult), reads=[*cmk, ("bbk", 1), K("r")], writes=[K("r")])
            P.emit("dve", lambda hh: hh.tensor_tensor(out=v3(u2), in0=v3(aic), in1=b_re, op=ALU.mult), reads=[*cmk, ("bbk", 0), K("msk")], writes=[K("msk")])
            P.emit("dve", lambda hh: hh.tensor_tensor(out=a_im, in0=v3(u1), in1=v3(u2), op=ALU.add), reads=[K("r"), K("msk")], writes=[("abk", 1)])
            for g4 in range(4):
                for ri in range(2):
                    P.emit("dve", lambda hh, g4=g4, ri=ri, q0=q0: hh.tensor_copy(out=bbt_v[32 * g4:32 * g4 + 32, q0:q0 + NQB, g4, ri, :], in_=bbk[32 * g4:32 * g4 + 32, :, ri, :]),
                           reads=[("bbk", ri)], writes=[("bbtp", ri)])
                    P.emit("dve", lambda hh, g4=g4, ri=ri, q0=q0: hh.tensor_copy(out=abt_v[32 * g4:32 * g4 + 32, q0:q0 + NQB, g4, ri, :], in_=abk[32 * g4:32 * g4 + 32, :, ri, :]),
                           reads=[("abk", ri)], writes=[("abtp", ri)])
        P.barrier()

        bank_rr = [0]

        def next_banks(n):
            s = bank_rr[0] % 2
            bank_rr[0] += 1
            return [3 * s + i for i in range(n)]

        def mm_group(banks, subs, slots_rhs, extra_reads=()):
            total = sum(x[4] for x in slots_rhs)
            idx = 0
            last = None
            for (lf, wkey, rf, rkeys, nk) in slots_rhs:
                for k in range(nk):
                    for si, (c0, n) in enumerate(subs):
                        is_last = (idx == total - 1) and (si == len(subs) - 1)
                        last = P.emit("pe", lambda hh, b=banks[si], n=n, c0=c0, k=k, lf=lf, rf=rf, st_=(idx == 0), sp_=(idx == total - 1):
                                      hh.matmul(ps[:, b, 0:n], lhsT=lf(k), rhs=rf(k, c0, n), start=st_, stop=sp_),
                                      reads=[wkey] + list(rkeys(k)) + list(extra_reads), writes=[("ps", banks[si])], signal=is_last)
                    idx += 1
            return last

        def rmsnorm(subs, TT, gi, dst_fn, dst_key):
            bnk = [6, 7, 6][:len(subs)]
            for c in range(NCH):
                sl = c % 2
                P.emit("act", lambda hh, c=c, sl=sl: hh.activation(out=sqb[:, sl, 0:TT], in_=h[:, c, 0:TT], func=AF.Square),
                       reads=[("h", c)], writes=[("sqb", sl)])
                for si, (c0, n) in enumerate(subs):
                    bb = 6 + (si % 2) if len(subs) <= 2 else [6, 7, 5][si]
                    P.emit("pe", lambda hh, bb=bb, c=c, sl=sl, c0=c0, n=n: hh.matmul(ps[:, bb, 0:n], lhsT=ones[:], rhs=sqb[:, sl, c0:c0 + n],
                                                                                 start=(c == 0), stop=(c == NCH - 1)),
                           reads=[("ones",), ("sqb", sl)], writes=[("ps", bb)], signal=True)
            for si, (c0, n) in enumerate(subs):
                bb = 6 + (si % 2) if len(subs) <= 2 else [6, 7, 5][si]
                P.emit("act", lambda hh, bb=bb, c0=c0, n=n: hh.activation(out=rstd[:, c0:c0 + n], in_=ps[:, bb, 0:n], func=AF.Sqrt, bias=eps_t[:, 0:1], scale=1.0 / D),
                       reads=[("ps", bb), ("eps",)], writes=[("rstd", si)])
                P.emit("dve", lambda hh, c0=c0, n=n: hh.reciprocal(out=rstd[:, c0:c0 + n], in_=rstd[:, c0:c0 + n]),
                       reads=[("rstd", si)], writes=[("rstd", si)])
            for c in range(NCH):
                P.emit("dve", lambda hh, c=c: hh.scalar_tensor_tensor(out=dst_fn(c), in0=h[:, c, 0:TT], scalar=gam_t[:, gi, c:c + 1], in1=rstd[:, 0:TT],
                                                                    op0=ALU.mult, op1=ALU.mult),
                       reads=[("h", c), ("gam",)] + [("rstd", si) for si in range(len(subs))], writes=[(dst_key, c)])

        eps_t = sb("eps_t", [128, 1])
        P.emit("dve", lambda hh: hh.memset(eps_t[:], EPS), writes=[("eps",)])

        def h_add_psum(m, banks, subs):
            for si, (c0, n) in enumerate(subs):
                P.emit("dve", lambda hh, m=m, b=banks[si], c0=c0, n=n: hh.tensor_tensor(out=h[:, m, c0:c0 + n], in0=h[:, m, c0:c0 + n], in1=ps[:, b, 0:n], op=ALU.add),
                       reads=[("ps", banks[si]), ("h", m)], writes=[("h", m)])

        def mlp(layer, subs, TT):
            def up(qd):
                hb = hq[qd % 2]
                for m16 in range(16):
                    f = qd * 16 + m16
                    slot = load_w(w_up[layer, f])
                    banks = next_banks(len(subs))
                    mm_group(banks, subs, [(lambda k, slot=slot: wv(slot)[:, k, :], ("w", slot),
                                            lambda k, c0, n: xn[:, k, c0:c0 + n], lambda k: [("xn", k)], 16)])
                    for si, (c0, n) in enumerate(subs):
                        P.emit("act", lambda hh, b=banks[si], c0=c0, n=n: hh.activation(out=relu_t[:, 0:n], in_=ps[:, b, 0:n], func=AF.Relu),
                               reads=[("ps", banks[si])], writes=[("relu",)])
                        P.emit("dve", lambda hh, hb=hb, m16=m16, c0=c0, n=n: hh.tensor_tensor(out=hb[:, m16, c0:c0 + n], in0=relu_t[:, 0:n], in1=relu_t[:, 0:n], op=ALU.mult),
                               reads=[("relu",)], writes=[("hq", qd % 2, m16)])

            def down(qd):
                hb = hq[qd % 2]
                for m in range(16):
                    slot = load_w(w_dn[layer, qd * 16 + m])
                    banks = next_banks(len(subs))
                    mm_group(banks, subs, [(lambda k, slot=slot: wv(slot)[:, k, :], ("w", slot),
                                            lambda k, c0, n, hb=hb: hb[:, k, c0:c0 + n], lambda k, qd=qd: [("hq", qd % 2, k)], 16)])
                    h_add_psum(m, banks, subs)
            up(0)
            for qd in range(1, 4):
                up(qd)
                down(qd - 1)
            down(3)

        def conv_mixer(subs, TT, has_samp, tile_is_last_b):
            npv = 2 + NPT + (NSS * 6 if has_samp else 0)
            W = npv - 2
            for m in range(NCH):
                vb = vbuf[m % 2]
                vk = ("vbuf", m % 2)
                s_c = load_w(w_in[16 + m])
                s_h = load_w(w_in[32 + m])
                s_b = load_w(w_in[m])
                rk = lambda k: [("xn", k)]
                rf = lambda k, c0, n: xn[:, k, c0:c0 + n]
                bk_c = next_banks(len(subs))
                mm_group(bk_c, subs, [(lambda k, s=s_c: wv(s)[:, k, :], ("w", s_c), rf, rk, 16)])
                bk_h = next_banks(len(subs))
                mm_group(bk_h, subs, [(lambda k, s=s_h: wv(s)[:, k, :], ("w", s_h), rf, rk, 16)])
                P.emit("dve", lambda hh, vb=vb, m=m: hh.tensor_copy(out=vb[:, 0:2], in_=carry_v[:, m, :]), reads=[("carry_v",)], writes=[vk])
                if has_samp:
                    vs = vb[:, 2 + NPT:2 + NPT + NSS * 6].rearrange("p (s t) -> p s t", t=6)
                    P.emit("dve", lambda hh, vs=vs, m=m: hh.tensor_copy(out=vs[:, :, 0:2], in_=convs_t[:, m, :, :]), reads=[("convs_t",)], writes=[vk])
                for si, (c0, n) in enumerate(subs):
                    P.emit("act", lambda hh, b=bk_c[si], c0=c0, n=n: hh.activation(out=tmpc[:, c0:c0 + n], in_=ps[:, b, 0:n], func=AF.Copy),
                           reads=[("ps", bk_c[si])], writes=[("tmpc", si)])
                    if c0 < NPT:
                        P.emit("dve", lambda hh, vb=vb, b=bk_h[si], c0=c0, n=n: hh.tensor_tensor(out=vb[:, 2 + c0:2 + c0 + n], in0=tmpc[:, c0:c0 + n], in1=ps[:, b, 0:n], op=ALU.mult),
                               reads=[("tmpc", si), ("ps", bk_h[si])], writes=[vk])
                    else:
                        vs = vb[:, 2 + NPT:2 + NPT + NSS * 6].rearrange("p (s t) -> p s t", t=6)
                        P.emit("dve", lambda hh, vs=vs, b=bk_h[si], c0=c0, n=n: hh.tensor_tensor(
                            out=vs[:, :, 2:6], in0=tmpc[:, c0:c0 + n].rearrange("p (s t) -> p s t", t=4),
                            in1=ps[:, b, 0:n].rearrange("p (s t) -> p s t", t=4), op=ALU.mult),
                            reads=[("tmpc", si), ("ps", bk_h[si])], writes=[vk])
                bk_b = next_banks(len(subs))
                mm_group(bk_b, subs, [(lambda k, s=s_b: wv(s)[:, k, :], ("w", s_b), rf, rk, 16)])
                P.emit("dve", lambda hh, vb=vb, m=m: hh.tensor_copy(out=carry_v[:, m, :], in_=vb[:, NPT:NPT + 2]), reads=[vk], writes=[("carry_v",)])
                if has_samp:
                    vs = vb[:, 2 + NPT:2 + NPT + NSS * 6].rearrange("p (s t) -> p s t", t=6)
                    P.emit("dve", lambda hh, vs=vs, m=m: hh.tensor_copy(out=convs_t[:, m, :, :], in_=vs[:, :, 4:6]), reads=[vk], writes=[("convs_t",)])
                P.emit("dve", lambda hh, vb=vb, m=m: hh.tensor_scalar(out=convt[:, 0:W], in0=vb[:, 0:W], scalar1=convw_t[:, m, 0:1], scalar2=None, op0=ALU.mult),
                       reads=[vk, ("convw",)], writes=[("convt",)])
                P.emit("dve", lambda hh, vb=vb, m=m: hh.scalar_tensor_tensor(out=convt[:, 0:W], in0=vb[:, 1:W + 1], scalar=convw_t[:, m, 1:2], in1=convt[:, 0:W], op0=ALU.mult, op1=ALU.add),
                       reads=[vk, ("convw",), ("convt",)], writes=[("convt",)])
                P.emit("dve", lambda hh, vb=vb, m=m: hh.scalar_tensor_tensor(out=convt[:, 0:W], in0=vb[:, 2:W + 2], scalar=convw_t[:, m, 2:3], in1=convt[:, 0:W], op0=ALU.mult, op1=ALU.add),
                       reads=[vk, ("convw",), ("convt",)], writes=[("convt",)])
                for si, (c0, n) in enumerate(subs):
                    if c0 < NPT:
                        P.emit("dve", lambda hh, m=m, b=bk_b[si], c0=c0, n=n: hh.tensor_tensor(out=g0[:, m, c0:c0 + n], in0=convt[:, c0:c0 + n], in1=ps[:, b, 0:n], op=ALU.mult),
                               reads=[("convt",), ("ps", bk_b[si])], writes=[("g0", m)])
                    else:
                        cs_ = convt[:, NPT + 2:NPT + 2 + NSS * 6].rearrange("p (s t) -> p s t", t=6)
                        P.emit("dve", lambda hh, m=m, cs_=cs_, b=bk_b[si], c0=c0, n=n: hh.tensor_tensor(
                            out=g0[:, m, c0:c0 + n].rearrange("p (s t) -> p s t", t=4), in0=cs_[:, :, 0:4],
                            in1=ps[:, b, 0:n].rearrange("p (s t) -> p s t", t=4), op=ALU.mult),
                            reads=[("convt",), ("ps", bk_b[si])], writes=[("g0", m)])
            for m in range(NCH):
                slot = load_w(w_out[m])
                banks = next_banks(len(subs))
                mm_group(banks, subs, [(lambda k, slot=slot: wv(slot)[:, k, :], ("w", slot),
                                        lambda k, c0, n: g0[:, k, c0:c0 + n], lambda k: [("g0", k)], 16)])
                h_add_psum(m, banks, subs)

        def ssm_bproj(t0, n, ck, par, seg=None):
            for ri in range(2):
                for g16 in range(4):
                    bb = 6 + ((ri * 4 + g16) % 2)
                    for j in range(16):
                        gp = g16 * 16 + j
                        q = gp // 4
                        P.emit("pe", lambda hh, bb=bb, j=j, q=q, gp=gp, ri=ri: hh.matmul(
                            ps[:, bb, j * TC:j * TC + n], lhsT=bbt[:, gp, ri, :], rhs=xn[:, q, t0:t0 + n],
                            start=True, stop=False),
                            reads=[("bbtp", ri), ("xn", q), ("xnc", q, ck)], writes=[("ps", bb)], signal=False)
                        if seg is None:
                            P.emit("pe", lambda hh, bb=bb, j=j, q=q, gp=gp, ri=ri: hh.matmul(
                                ps[:, bb, j * TC + 1:j * TC + n], lhsT=abt[:, gp, ri, :], rhs=xn[:, q, t0:t0 + n - 1],
                                start=False, stop=True),
                                reads=[("abtp", ri), ("xn", q), ("xnc", q, ck)], writes=[("ps", bb)], signal=(j == 15))
                        else:
                            P.emit("pe", lambda hh, bb=bb, j=j, q=q, gp=gp, ri=ri: hh.matmul(
                                ps[:, bb, j * TC:j * TC + n].rearrange("p (s t) -> p s t", t=seg)[:, :, 1:seg], lhsT=abt[:, gp, ri, :],
                                rhs=xn[:, q, t0:t0 + n].rearrange("p (s t) -> p s t", t=seg)[:, :, 0:seg - 1],
                                start=False, stop=True),
                                reads=[("abtp", ri), ("xn", q), ("xnc", q, ck)], writes=[("ps", bb)], signal=(j == 15))
                    P.emit("act", lambda hh, bb=bb, ri=ri, g16=g16: hh.activation(
                        out=bu[par][:, ri, g16 * 16:g16 * 16 + 16, 0:n], in_=ps[:, bb, 0:16 * TC].rearrange("p (j t) -> p j t", j=16)[:, :, 0:n], func=AF.Copy),
                        reads=[("ps", bb)], writes=[("bu", par, ri, g16)])

        def ssm_recur(n, par, init_ap, init_key, boff=0):
            assert n % 2 == 0
            bk_ = [("bu", par, ri, g16) for ri in range(2) for g16 in range(4)]
            P.emit("dve", lambda hh: hh.memset(traj[:, :, :, 0], 0.0), writes=[("traj",)])
            P.emit("dve", lambda hh: hh.tensor_copy(out=traj[:, :, :, 1], in_=init_ap), reads=[init_key], writes=[("traj",)])
            P.emit("dve", lambda hh: hh.tensor_tensor(out=c1b[:], in0=m1[:], in1=init_ap, op=ALU.mult), reads=[("m1",), init_key], writes=[("cc", 1)])
            P.emit("dve", lambda hh: hh.tensor_tensor(out=c2b[:], in0=m2[:], in1=init_ap[:, ::-1, :], op=ALU.mult), reads=[("m2",), init_key], writes=[("cc", 2)])
            P.emit("dve", lambda hh: hh.tensor_tensor(out=c1b[:], in0=c1b[:], in1=c2b[:], op=ALU.add), reads=[("cc", 1), ("cc", 2)], writes=[("cc", 1)])
            P.emit("dve", lambda hh: hh.tensor_tensor(out=bu[par][:, :, :, boff], in0=c1b[:], in1=bu[par][:, :, :, boff], op=ALU.add),
                   reads=[("cc", 1)] + bk_, writes=[("w", par, "h")])
            for c in range(0, n, 2):
                P.emit("dve", lambda hh, c=c: hh.tensor_tensor(out=t1b[:], in0=m1x[:], in1=traj[:, :, :, c:c + 2], op=ALU.mult), reads=[("m1x",), ("traj",)], writes=[("tt", 1)])
                P.emit("dve", lambda hh, c=c: hh.tensor_tensor(out=t2b[:], in0=m2x[:], in1=traj[:, ::-1, :, c:c + 2], op=ALU.mult), reads=[("m2x",), ("traj",)], writes=[("tt", 2)])
                P.emit("dve", lambda hh, c=c: hh.tensor_tensor(out=t1b[:], in0=t1b[:], in1=bu[par][:, :, :, boff + c:boff + c + 2], op=ALU.add),
                       reads=[("tt", 1), ("w", par, "h")] + (bk_ if (c == 0 or c == n - 2) else []), writes=[("tt", 1)])
                P.emit("dve", lambda hh, c=c: hh.tensor_tensor(out=traj[:, :, :, c + 2:c + 4], in0=t1b[:], in1=t2b[:], op=ALU.add),
                       reads=[("tt", 1), ("tt", 2)], writes=[("traj",)])

        def ssm_trajb(n, boff=0):
            P.emit("act", lambda hh: hh.activation(out=trajb[:, :, :, boff:boff + n], in_=traj[:, :, :, 2:n + 2], func=AF.Copy), reads=[("traj",)], writes=[("trajb",)])

        def ssm_cproj(t0, n, ck, yb):
            for q in range(16):
                P.emit("pe", lambda hh, q=q: hh.matmul(
                    ps[:, yb, q * TC:q * TC + n], lhsT=ddg[:, q, :], rhs=xn[:, q, t0:t0 + n], start=True, stop=False),
                    reads=[("ddg",), ("xn", q), ("xnc", q, ck)], writes=[("ps", yb)], signal=False)
                for g4 in range(4):
                    gp = q * 4 + g4
                    for ri in range(2):
                        lastmm = (g4 == 3 and ri == 1)
                        P.emit("pe", lambda hh, q=q, gp=gp, g4=g4, ri=ri, lastmm=lastmm: hh.matmul(
                            ps[32 * g4:32 * g4 + 32, yb, q * TC:q * TC + n], lhsT=cpad[:, ri, gp * 32:(gp + 1) * 32], rhs=trajb[:, ri, gp, 0:n],
                            start=False, stop=(ri == 1), tile_position=(0, 32 * g4)),
                            reads=[("cpad", ri), ("trajb",)], writes=[("ps", yb)], signal=(q == 15 and lastmm))

        def ssm_gelu(t0, n, ck, yb):
            yv = ps[:, yb, 0:16 * TC]
            P.emit("act", lambda hh: hh.activation(out=gel[0][:], in_=yv, func=AF.Square), reads=[("ps", yb)], writes=[("gel", 0)])
            P.emit("dve", lambda hh: hh.tensor_scalar(out=gel[0][:], in0=gel[0][:], scalar1=0.044715, scalar2=1.0, op0=ALU.mult, op1=ALU.add),
                   reads=[("gel", 0)], writes=[("gel", 0)])
            P.emit("dve", lambda hh: hh.tensor_tensor(out=gel[1][:], in0=gel[0][:], in1=yv, op=ALU.mult), reads=[("gel", 0), ("ps", yb)], writes=[("gel", 1)])
            P.emit("act", lambda hh: hh.activation(out=gel[2][:], in_=gel[1][:], func=AF.Sigmoid, scale=1.5957691216057308), reads=[("gel", 1)], writes=[("gel", 2)])
            P.emit("dve", lambda hh: hh.tensor_tensor(
                out=xn[:, :, t0:t0 + n], in0=gel[2][:].rearrange("p (j t) -> p j t", j=16)[:, :, 0:n],
                in1=yv.rearrange("p (j t) -> p j t", j=16)[:, :, 0:n], op=ALU.mult),
                reads=[("gel", 2), ("ps", yb)], writes=[("xnc", q, ck) for q in range(16)])
            cks.add(ck)

        cks = set()

        def ssm_mixer(subs, TT, has_samp, full, is_last_b):
            cks.clear()
            chunks = []
            t0 = 0
            for n in [16] * 32 + [4]:
                chunks.append((t0, n, "p", None))
                t0 += n
            assert t0 == NPT
            if has_samp:
                for g_ in range(NSS // 4):
                    chunks.append((NPT + 16 * g_, 16, "s", g_))
            nprompt = 33
            ssm_bproj(chunks[0][0], chunks[0][1], 0, 0)
            pend = None
            segof = lambda kind: (4 if kind == "s" else None)
            for ci, (t0, n, kind, s_) in enumerate(chunks):
                par = ci % 2
                if ci + 1 < len(chunks):
                    ssm_bproj(chunks[ci + 1][0], chunks[ci + 1][1], ci + 1, (ci + 1) % 2, segof(chunks[ci + 1][2]))
                if kind == "p":
                    if ci > 0:
                        P.emit("dve", lambda hh, pn=chunks[ci - 1][1]: hh.tensor_copy(out=xcarry[:], in_=traj[:, :, :, pn + 1]), reads=[("traj",)], writes=[("xcarry",)])
                    ssm_recur(n, par, xcarry[:], ("xcarry",))
                    if ci == nprompt - 1:
                        P.emit("dve", lambda hh, n=n: hh.tensor_copy(out=xcarry[:], in_=traj[:, :, :, n + 1]), reads=[("traj",)], writes=[("xcarry",)])
                    if full:
                        ssm_trajb(n)
                else:
                    for l_ in range(4):
                        sq_ = 4 * s_ + l_
                        ssm_recur(4, par, ssms_t[:, sq_, :, :], ("ssms_t",), boff=4 * l_)
                        P.emit("dve", lambda hh, sq_=sq_: hh.tensor_copy(out=ssms_t[:, sq_, :, :], in_=traj[:, :, :, 5]), reads=[("traj",)], writes=[("ssms_t",)])
                        if full:
                            ssm_trajb(4, boff=4 * l_)
                if full:
                    yb = 3 + (ci % 2)
                    ssm_cproj(t0, n, ci, yb)
                    if pend is not None:
                        ssm_gelu(*pend)
                    pend = (t0, n, ci, yb)
            if full and pend is not None:
                ssm_gelu(*pend)
            if not full:
                return
            for m in range(NCH):
                sa = load_w(w_ga[m])
                sb_ = load_w(w_gb[m])
                ckl = sorted(cks)
                rk = lambda k, ckl=ckl: [("xn", k)] + [("xnc", k, c_) for c_ in ckl]
                rf = lambda k, c0, n: xn[:, k, c0:c0 + n]
                bka = next_banks(len(subs))
                mm_group(bka, subs, [(lambda k, s=sa: wv(s)[:, k, :], ("w", sa), rf, rk, 16)])
                bkb = next_banks(len(subs))
                mm_group(bkb, subs, [(lambda k, s=sb_: wv(s)[:, k, :], ("w", sb_), rf, rk, 16)])
                for si, (c0, n) in enumerate(subs):
                    P.emit("act", lambda hh, b=bkb[si], c0=c0, n=n: hh.activation(out=glu_t[0][:, c0:c0 + n], in_=ps[:, b, 0:n], func=AF.Sigmoid),
                           reads=[("ps", bkb[si])], writes=[("glu", 0, si)])
                    P.emit("dve", lambda hh, b=bka[si], c0=c0, n=n: hh.tensor_tensor(out=glu_t[1][:, c0:c0 + n], in0=glu_t[0][:, c0:c0 + n], in1=ps[:, b, 0:n], op=ALU.mult),
                           reads=[("glu", 0, si), ("ps", bka[si])], writes=[("glu", 1, si)])
                    P.emit("dve", lambda hh, m=m, c0=c0, n=n: hh.tensor_tensor(out=h[:, m, c0:c0 + n], in0=h[:, m, c0:c0 + n], in1=glu_t[1][:, c0:c0 + n], op=ALU.add),
                           reads=[("glu", 1, si), ("h", m)], writes=[("h", m)])

        tiles = []
        for ti in range(NTILE_HALF):
            tiles.append(dict(col=ti * NPT, full=False, samp=False, last=False, ocol=None))
        for ti in range(NTILE_HALF):
            lastb = ti == NTILE_HALF - 1
            tiles.append(dict(col=HALF + ti * NPT, full=True, samp=lastb, last=lastb, ocol=ti * NPT))

        for tinfo in tiles:
            full, samp = tinfo["full"], tinfo["samp"]
            TT = NPT + (NSAMP if samp else 0)
            subs = [(0, NPT // 2), (NPT // 2, NPT // 2)] + ([(NPT, NSAMP)] if samp else [])
            for c in range(NCH):
                pass
            P.emit("sp", lambda hh, col=tinfo["col"]: hh.dma_start(out=h[:, :, 0:NPT], in_=xin_v[:, :, col:col + NPT]),
                   writes=[("h", c) for c in range(NCH)], sem=ldx, inc=16)
            if samp:
                P.emit("sp", lambda hh: hh.dma_start(out=h[:, :, NPT:NPT + NSAMP], in_=xin_v[:, :, 2 * HALF:2 * HALF + NSAMP]),
                       writes=[("h", c) for c in range(NCH)], sem=ldx, inc=16)
            rm_switch()
            rmsnorm(subs, TT, 0, lambda c, TT=TT: xn[:, c, 0:TT], "xn")
            conv_mixer(subs, TT, samp, tinfo["last"])
            if stage >= 2:
                rm_switch()
                rmsnorm(subs, TT, 1, lambda c, TT=TT: xn[:, c, 0:TT], "xn")
                mlp(0, subs, TT)
            if stage >= 3:
                rm_switch()
                rmsnorm(subs, TT, 2, lambda c, TT=TT: xn[:, c, 0:TT], "xn")
                ssm_mixer(subs, TT, samp, full and stage >= 4, tinfo["last"])
            if full and stage >= 5:
                rm_switch()
                rmsnorm(subs, TT, 3, lambda c, TT=TT: xn[:, c, 0:TT], "xn")
                mlp(1, subs, TT)
            if full:
                rm_switch()
                if stage >= 6:
                    rmsnorm(subs, TT, 4, lambda c, TT=TT: yfin[:, c, 0:TT], "yfin")
                else:
                    for c in range(NCH):
                        P.emit("dve", lambda hh, c=c, TT=TT: hh.tensor_copy(out=yfin[:, c, 0:TT], in_=h[:, c, 0:TT]), reads=[("h", c)], writes=[("yfin", c)])
                oc = tinfo["ocol"]
                P.emit("sp", lambda hh, oc=oc: hh.dma_start(out=yout_v[:, :, oc:oc + NPT], in_=yfin[:, :, 0:NPT]),
                       reads=[("yfin", c) for c in range(NCH)], sem=st, inc=16)
                if samp:
                    P.emit("sp", lambda hh: hh.dma_start(out=yout_v[:, :, HALF:HALF + NSAMP], in_=yfin[:, :, NPT:NPT + NSAMP]),
                           reads=[("yfin", c) for c in range(NCH)], sem=st, inc=16)
        P.emit("sp", lambda hh: hh.dma_start(out=convp_out, in_=carry_v[:]), reads=[("carry_v",)], sem=st, inc=16)
        P.emit("sp", lambda hh: hh.dma_start(out=ssmp_out, in_=xcarry[:]), reads=[("xcarry",)], sem=st, inc=16)
        P.emit("sp", lambda hh: hh.dma_start(out=convs_out, in_=convs_t[:]), reads=[("convs_t",)], sem=st, inc=16)
        P.emit("sp", lambda hh: hh.dma_start(out=ssms_out, in_=ssms_t[:]), reads=[("ssms_t",)], sem=st, inc=16)
        P.barrier()

        with ExitStack() as es3:
            for s in P.sems:
                s.h = es3.enter_context(nc.semaphore(s.name))
            block = es3.enter_context(nc.Block())

            @block.tensor
            def _(e):
                for f in P.prog["pe"]:
                    f(e)

            @block.scalar
            def _(e):
                for f in P.prog["act"]:
                    f(e)

            @block.vector
            def _(e):
                for f in P.prog["dve"]:
                    f(e)

            @block.gpsimd
            def _(e):
                for f in P.prog["pool"]:
                    f(e)

            @block.sync
            def _(e):
                for f in P.prog["sp"]:
                    f(e)
    return nc


def _slabs(W):
    K, M = W.shape
    a = W.reshape(K // 128, 128, M // 128, 128)
    return np.ascontiguousarray(a.transpose(2, 1, 0, 3)).reshape(M // 128, 128, (K // 128) * 128)


def _fm(x):
    return np.ascontiguousarray(x.T)


def _prep_shared(inp):
    f = lambda k: np.asarray(inp[k], dtype=np.float32)
    sh = {}
    sh["w_in"] = _slabs(f("conv_w_in")[0])
    sh["w_out"] = _slabs(f("conv_w_out")[0])
    sh["w_up"] = np.stack([_slabs(f("mlp_w_up")[l]) for l in range(2)])
    dn = []
    for l in range(2):
        Wd = f("mlp_w_down")[l]
        q = [_slabs(Wd[qd * 2048:(qd + 1) * 2048]) for qd in range(4)]
        dn.append(np.concatenate(q, axis=0))
    sh["w_dn"] = np.stack(dn)
    sh["w_ga"] = _slabs(f("ssm_glu_w_a")[0])
    sh["w_gb"] = _slabs(f("ssm_glu_w_b")[0])
    gam = np.stack([f("norm_mixer")[0], f("norm_mlp")[0], f("norm_mixer")[1], f("norm_mlp")[1], f("norm_final")])
    sh["gam"] = np.ascontiguousarray(gam.reshape(5, 16, 128).transpose(2, 0, 1))
    sh["convw"] = np.ascontiguousarray(f("conv_w")[0].reshape(3, 16, 128).transpose(2, 1, 0))
    lre, lim, ls = f("ssm_lambda_re")[0], f("ssm_lambda_im")[0], f("ssm_log_step")[0]
    lsb = np.broadcast_to(ls[:, None], (128, 64))
    sm = lambda a: np.ascontiguousarray(a.reshape(64, 2, 64).transpose(1, 2, 0).reshape(128, 64))
    sh["lam_s"] = np.ascontiguousarray(np.stack([sm(lre), sm(lim), sm(lsb)], axis=1))
    def cmaj(a):
        t = a.reshape(16, 4, 2, 64)
        t = np.broadcast_to(t[:, :, None, None, :, :], (16, 4, 2, 16, 2, 64))
        return np.ascontiguousarray(t.transpose(1, 2, 3, 0, 4, 5)).reshape(128, 2048)
    tm = lambda a: np.tile(a.reshape(64, 128), (2, 1))
    sh["lam_t"] = np.ascontiguousarray(np.stack([tm(lre), tm(lim), tm(lsb)], axis=1))
    def bmaj(b):
        t = b.reshape(16, 4, 2, 64, 16)
        out = np.zeros((4, 2, 16, 16, 2, 64), np.float32)
        for j in range(2):
            out[:, j, :, :, j, :] = t[:, :, j].transpose(1, 3, 0, 2)
        return out.reshape(128, 2048)
    sh["bp"] = np.ascontiguousarray(np.stack([bmaj(f("ssm_b_re")[0]), bmaj(f("ssm_b_im")[0])], axis=1))
    def cmajp(c):
        t = c.reshape(64, 2, 16, 64)
        out = np.zeros((2, 64, 64, 2, 16), np.float32)
        for j in range(2):
            out[j, :, :, j, :] = t[:, j].transpose(2, 0, 1)
        return out.reshape(128, 64 * 32)
    sh["cp"] = np.ascontiguousarray(np.stack([cmajp(f("ssm_c_re")[0]), cmajp(f("ssm_c_im")[0])], axis=1))
    dv = f("ssm_d")[0].reshape(16, 128)
    ddm = np.zeros((128, 16, 128), np.float32)
    for q in range(16):
        ddm[np.arange(128), q, np.arange(128)] = dv[q]
    sh["dd"] = ddm.reshape(128, 2048)
    return sh


_NC_CACHE = {}


def kernel(**inp):
    sh = _prep_shared(inp)
    xp = np.asarray(inp["x_prompt"], np.float32)
    xs = np.asarray(inp["x_sample"], np.float32)
    meta = np.asarray(inp["meta_tokens"], np.float32)
    sc = np.asarray(inp["state_conv"], np.float32)[0]
    sre = np.asarray(inp["state_ssm_re"], np.float32)[0]
    sim = np.asarray(inp["state_ssm_im"], np.float32)[0]
    in_maps = []
    for c in range(8):
        i, r = c // 2, c % 2
        S = np.concatenate([meta, xp[i]], axis=0)
        A = np.zeros((HALF, D), np.float32) if r == 0 else S[:HALF]
        B = S[:HALF] if r == 0 else S[HALF:]
        smp = xs[NSS * c:NSS * (c + 1)].reshape(NSAMP, D)
        m = dict(sh)
        m["xin"] = _fm(np.concatenate([A, B, smp], axis=0))
        cs = sc[NSS * c:NSS * (c + 1)]
        m["convs_in"] = np.ascontiguousarray(cs.reshape(NSS, 2, 16, 128).transpose(3, 2, 0, 1))
        def st(a):
            return a.reshape(NSS, 64, 2, 64).transpose(2, 3, 0, 1).reshape(128, NSS, 64)
        m["ssms_in"] = np.ascontiguousarray(np.stack([st(sre[NSS * c:NSS * (c + 1)]), st(sim[NSS * c:NSS * (c + 1)])], axis=2))
        in_maps.append(m)
    if "nc" not in _NC_CACHE:
        _NC_CACHE["nc"] = build()
    res = run_bass_kernel_spmd(_NC_CACHE["nc"], in_maps, core_ids=list(range(8)))
    R = res.results
    y_prompt = np.zeros((4, 2048, D), np.float32)
    y_sample = np.zeros((128, 4, D), np.float32)
    conv_p = np.zeros((1, 4, 2, D), np.float32)
    re_p = np.zeros((1, 4, 128, 64), np.float32)
    im_p = np.zeros((1, 4, 128, 64), np.float32)
    conv_s = np.zeros((1, 128, 2, D), np.float32)
    re_s = np.zeros((1, 128, 128, 64), np.float32)
    im_s = np.zeros((1, 128, 128, 64), np.float32)
    for c in range(8):
        i, r = c // 2, c % 2
        yo = np.asarray(R[c]["y_out"]).T
        if r == 0:
            y_prompt[i, 0:HALF - 16] = yo[16:HALF]
        else:
            y_prompt[i, HALF - 16:] = yo[0:HALF]
        y_sample[NSS * c:NSS * (c + 1)] = yo[HALF:].reshape(NSS, 4, D)
        cso = np.asarray(R[c]["convs_out"])
        conv_s[0, NSS * c:NSS * (c + 1)] = cso.transpose(2, 3, 1, 0).reshape(NSS, 2, D)
        sso = np.asarray(R[c]["ssms_out"])
        t = sso.reshape(2, 64, NSS, 2, 64).transpose(2, 3, 4, 0, 1).reshape(NSS, 2, 128, 64)
        re_s[0, NSS * c:NSS * (c + 1)] = t[:, 0]
        im_s[0, NSS * c:NSS * (c + 1)] = t[:, 1]
        if r == 1:
            cpo = np.asarray(R[c]["convp_out"])
            conv_p[0, i] = cpo.transpose(2, 1, 0).reshape(2, D)
            spo = np.asarray(R[c]["ssmp_out"])
            t = spo.reshape(2, 64, 2, 64).transpose(2, 3, 0, 1).reshape(2, 128, 64)
            re_p[0, i] = t[0]
            im_p[0, i] = t[1]
    return (y_prompt, y_sample, conv_p, re_p, im_p, conv_s, re_s, im_s)
```

```python
import math
import os
import numpy as np
import concourse.bass as bass
import concourse.mybir as mybir
from concourse.bass_utils import run_bass_kernel_spmd

F32 = mybir.dt.float32
BF16 = mybir.dt.bfloat16
I32 = mybir.dt.int32
AF = mybir.ActivationFunctionType
ALU = mybir.AluOpType

D = 2048
NCH = 16
NPT = 516
NTILE_HALF = 2
HALF = NPT * NTILE_HALF
NSS = 16
NSAMP = NSS * 4
TTMAX = NPT + NSAMP
VW = 2 + NPT + NSS * 6
TC = 16
NW = 4
EPS = 1e-6
NTOK = 2 * HALF + NSAMP
ENGS = ["pe", "act", "dve", "pool", "sp"]


SEM_LIMIT = 1000


class Sem:
    def __init__(self, name, owner=None):
        self.name = name
        self.h = None
        self.count = 0
        self.owner = owner


class Prog:
    def __init__(self):
        self.prog = {e: [] for e in ENGS}
        self.waited = {e: {} for e in ENGS}
        self.esem = {e: Sem("e_" + e, e) for e in ENGS}
        self.nrot = 0
        self.sems = list(self.esem.values())
        self.lastw = {}
        self.readers = {}
        self.guard = {}

    def new_sem(self, name):
        s = Sem(name)
        self.sems.append(s)
        return s

    def _wait(self, eng, ev):
        if ev is None:
            return
        sem, val = ev
        if eng == "pe" and sem.owner == "pe":
            return
        w = self.waited[eng]
        if w.get(sem.name, 0) >= val:
            return
        w[sem.name] = val
        self.prog[eng].append(lambda h, sem=sem, val=val: h.wait_ge(sem.h, val))

    def emit(self, eng, fn, reads=(), writes=(), sem=None, inc=1, signal=True, extra=()):
        for ev in extra:
            self._wait(eng, ev)
        for b in reads:
            self._wait(eng, self.lastw.get(b))
            for ev in self.guard.get(b[0], ()):
                self._wait(eng, ev)
        for b in writes:
            self._wait(eng, self.lastw.get(b))
            for ev in self.readers.get(b, ()):
                self._wait(eng, ev)
            for ev in self.guard.get(b[0], ()):
                self._wait(eng, ev)
        if sem is None:
            if self.esem[eng].count >= SEM_LIMIT:
                self.nrot += 1
                ns = Sem(f"e_{eng}_{self.nrot}", eng)
                self.sems.append(ns)
                self.esem[eng] = ns
            s = self.esem[eng]
        else:
            s = sem
        if signal:
            s.count += inc
            ev = (s, s.count)
            self.prog[eng].append(lambda h, fn=fn, s=s, inc=inc: fn(h).then_inc(s.h, inc))
        else:
            ev = (s, s.count + inc)
            self.prog[eng].append(lambda h, fn=fn: fn(h))
        for b in reads:
            self.readers.setdefault(b, []).append(ev)
        for b in writes:
            self.lastw[b] = ev
            self.readers[b] = []
        return ev

    def all_events(self):
        return [(s, s.count) for s in self.sems if s.count > 0]

    def barrier(self):
        evs = self.all_events()
        for e in ENGS:
            for ev in evs:
                self._wait(e, ev)


def build(stage=99):
    nc = bass.Bass("TRN2", target_bir_lowering=False)
    P = Prog()

    def din(name, shape, dt=F32):
        return nc.dram_tensor(name, list(shape), dt, kind="ExternalInput").ap()

    def dout(name, shape, dt=F32):
        return nc.dram_tensor(name, list(shape), dt, kind="ExternalOutput").ap()

    xin = din("xin", [D, NTOK])
    convs_in = din("convs_in", [128, NCH, NSS, 2])
    ssms_in = din("ssms_in", [128, NSS, 2, 64])
    gam = din("gam", [128, 5, NCH])
    convw = din("convw", [128, NCH, 3])
    w_in = din("w_in", [48, 128, 2048])
    w_out = din("w_out", [16, 128, 2048])
    w_up = din("w_up", [2, 64, 128, 2048])
    w_dn = din("w_dn", [2, 64, 128, 2048])
    w_ga = din("w_ga", [16, 128, 2048])
    w_gb = din("w_gb", [16, 128, 2048])
    lam_s = din("lam_s", [128, 3, 64])
    lam_t = din("lam_t", [128, 3, 128])
    dscr = nc.dram_tensor("dscr", [64, 4, 128], F32, kind="Internal").ap()
    bp = din("bp", [128, 2, 2048])
    cp = din("cp", [128, 2, 64 * 32])
    dd = din("dd", [128, NCH * 128])

    y_out = dout("y_out", [D, HALF + NSAMP])
    convp_out = dout("convp_out", [128, NCH, 2])
    ssmp_out = dout("ssmp_out", [128, 2, 64])
    convs_out = dout("convs_out", [128, NCH, NSS, 2])
    ssms_out = dout("ssms_out", [128, NSS, 2, 64])

    xin_v = xin.rearrange("(c p) t -> p c t", p=128)
    yout_v = y_out.rearrange("(c p) t -> p c t", p=128)

    RM = 9600
    from contextlib import ExitStack
    with ExitStack() as es:
        def sb(name, shape, dt=F32):
            return es.enter_context(nc.sbuf_tensor(name, list(shape), dt))

        h = sb("h", [128, NCH, TTMAX])
        xn = sb("xn", [128, NCH, TTMAX], BF16)
        wsl = sb("wsl", [128, NW, 2048], BF16)
        rmix = sb("rmix", [128, RM])
        rstd = sb("rstd", [128, TTMAX])
        sqb = sb("sqb", [128, 2, TTMAX], BF16)
        ones = sb("ones", [128, 128], BF16)
        gam_t = sb("gam_t", [128, 5, NCH])
        convw_t = sb("convw_t", [128, NCH, 3])
        convs_t = sb("convs_t", [128, NCH, NSS, 2])
        carry_v = sb("carry_v", [128, NCH, 2])
        xcarry = sb("xcarry", [128, 2, 64])
        ssms_t = sb("ssms_t", [128, NSS, 2, 64])
        m1 = sb("m1", [128, 2, 64])
        m2 = sb("m2", [128, 2, 64])
        bbt = sb("bbt", [128, 64, 2, 128], BF16)
        abt = sb("abt", [128, 64, 2, 128], BF16)
        cpad = sb("cpad", [128, 2, 64 * 32], BF16)
        ddg = sb("ddg", [128, NCH, 128], BF16)
        ps = es.enter_context(nc.psum_tensor("ps", [128, 8, 512], F32))

        def rm_f32(off, n):
            return rmix[:, off:off + n]

        def rm_bf16(off, n):
            return rmix[:, off:off + n].bitcast(BF16)

        vbuf = [rm_f32(0, VW), rm_f32(VW, VW)]
        tmpc = rm_f32(2 * VW, TTMAX)
        convt = rm_f32(2 * VW + TTMAX, VW)
        g0 = rm_bf16(3 * VW + TTMAX, NCH * TTMAX // 2).rearrange("p (c t) -> p c t", c=NCH)
        HQW = NCH * TTMAX // 2
        hq = [rm_bf16(0, HQW).rearrange("p (c t) -> p c t", c=NCH),
              rm_bf16(HQW, HQW).rearrange("p (c t) -> p c t", c=NCH)]
        relu_t = rm_bf16(2 * HQW, TTMAX // 2 + 2)
        glu_t = [rm_f32(0, TTMAX), rm_f32(TTMAX, TTMAX)]
        BUW = 128 * TC
        v4 = lambda off, t_: rm_f32(off, 128 * t_).rearrange("p (r g t) -> p r g t", r=2, g=64)
        bu = [v4(0, TC), v4(BUW, TC)]
        TRW = 128 * (TC + 2)
        traj = v4(2 * BUW, TC + 2)
        trajb = rm_bf16(2 * BUW + TRW, BUW // 2).rearrange("p (r g t) -> p r g t", r=2, g=64)
        o1 = 2 * BUW + TRW + BUW // 2
        t1b = v4(o1, 2)
        t2b = v4(o1 + 256, 2)
        c1b = rm_f32(o1 + 512, 128).rearrange("p (r g) -> p r g", r=2)
        c2b = rm_f32(o1 + 640, 128).rearrange("p (r g) -> p r g", r=2)
        o2 = o1 + 768
        GW = 16 * TC
        gel = [rm_f32(o2 + i * GW, GW) for i in range(3)]
        assert o2 + 3 * GW <= RM and 2 * HQW + TTMAX // 2 + 2 <= RM and 3 * VW + TTMAX + NCH * TTMAX // 2 <= RM
        yfin = rm_f32(0, NCH * TTMAX).rearrange("p (c t) -> p c t", c=NCH) if NCH * TTMAX <= RM else None
        assert yfin is not None

        RMN = ("w", "tmpw", "cc", "g0", "vbuf", "tmpc", "convt", "hq", "relu", "glu", "bu", "traj", "trajb", "tt", "gel", "yfin")

        def rm_switch():
            evs = P.all_events()
            for n in RMN:
                P.guard[n] = evs

        wsem = [P.new_sem(f"w{i}") for i in range(NW)]
        wcount = [0]

        def load_w(src_ap):
            k = wcount[0]
            wcount[0] += 1
            slot = k % NW
            if wsem[slot].count >= SEM_LIMIT:
                wsem[slot] = P.new_sem(f"w{slot}_{k}")
            P.emit("pool", lambda hh, slot=slot, src_ap=src_ap: hh.dma_start(out=wsl[:, slot, :], in_=src_ap),
                   writes=[("w", slot)], sem=wsem[slot], inc=16)
            return slot

        def wv(slot):
            return wsl[:, slot, :].rearrange("p (k m) -> p k m", k=16)

        ldx = P.new_sem("ldx")
        st = P.new_sem("st")

        def load_small(dst, src, key):
            P.emit("sp", lambda hh: hh.dma_start(out=dst, in_=src), writes=[key], sem=P.new_sem("ld_" + key[0]), inc=16)

        load_small(gam_t[:], gam, ("gam",))
        load_small(convw_t[:], convw, ("convw",))
        load_small(convs_t[:], convs_in, ("convs_t",))
        load_small(ssms_t[:], ssms_in, ("ssms_t",))
        P.emit("dve", lambda hh: hh.memset(ones[:], 1.0), writes=[("ones",)])
        P.emit("dve", lambda hh: hh.memset(carry_v[:], 0.0), writes=[("carry_v",)])
        P.emit("dve", lambda hh: hh.memset(xcarry[:], 0.0), writes=[("xcarry",)])
        P.emit("pool", lambda hh: hh.dma_start(out=cpad[:, 0, :], in_=cp[:, 0, :]), writes=[("cpad", 0)], sem=P.new_sem("ldp0"), inc=16)
        P.emit("pool", lambda hh: hh.dma_start(out=cpad[:, 1, :], in_=cp[:, 1, :]), writes=[("cpad", 1)], sem=P.new_sem("ldp1"), inc=16)
        P.emit("pool", lambda hh: hh.dma_start(out=ddg[:].rearrange("p c m -> p (c m)"), in_=dd), writes=[("ddg",)], sem=P.new_sem("ldp2"), inc=16)
        P.emit("act", lambda hh: hh.activation(out=cpad[:, 1, :], in_=cpad[:, 1, :], func=AF.Copy, scale=-1.0),
               reads=[("cpad", 1)], writes=[("cpad", 1)])

        TWO_PI = 2.0 * math.pi

        rm_bump = [0]

        def rm_alloc(n):
            o = rm_bump[0]
            rm_bump[0] += n
            assert rm_bump[0] <= RM
            return rmix[:, o:o + n]

        def cplx_setup(tag, F, want_q):
            T = {}

            def t(n):
                T[n] = rm_alloc(F)
                return T[n]
            lam_t = rm_alloc(3 * F).rearrange("p (a f) -> p a f", a=3)
            lam_sem = P.new_sem("ld_lam_" + tag)
            lr, li, ls = lam_t[:, 0, :], lam_t[:, 1, :], lam_t[:, 2, :]
            K = lambda n: (tag, n)
            dt_ = t("dt"); mag = t("mag"); th = t("th"); kf = t("kf")
            r = t("r"); msk = t("msk"); sn = t("sn"); cs = t("cs"); are = t("are"); aim = t("aim")
            if want_q:
                den = t("den"); nr = t("nr"); qre = t("qre"); qim = t("qim"); tq = t("tq")
            fact = [1.0]
            for i_ in range(1, 20):
                fact.append(fact[-1] * i_)
            EXPC = [1.0 / fact[i_] for i_ in range(11)]
            SINC = [((-1.0) ** i_) / fact[2 * i_ + 1] for i_ in range(8)]
            COSC = [((-1.0) ** i_) / fact[2 * i_] for i_ in range(9)]

            def poly(dst, w, coeffs, kd, kw):
                n_ = len(coeffs) - 1
                P.emit("dve", lambda hh: hh.tensor_scalar(out=dst, in0=w, scalar1=coeffs[n_], scalar2=None, op0=ALU.mult), reads=[K(kw)], writes=[K(kd)])
                for i_ in range(n_ - 1, 0, -1):
                    P.emit("dve", lambda hh, c_=coeffs[i_]: hh.scalar_tensor_tensor(out=dst, in0=dst, scalar=c_, in1=w, op0=ALU.add, op1=ALU.mult),
                           reads=[K(kd), K(kw)], writes=[K(kd)])
                P.emit("dve", lambda hh: hh.tensor_scalar(out=dst, in0=dst, scalar1=coeffs[0], scalar2=None, op0=ALU.add), reads=[K(kd)], writes=[K(kd)])

            def tt(out, a_, b_, op, ko, ka, kb):
                P.emit("dve", lambda hh: hh.tensor_tensor(out=out, in0=a_, in1=b_, op=op), reads=[K(ka), K(kb)], writes=[K(ko)])

            def run(lam_ap):
                P.emit("sp", lambda hh: hh.dma_start(out=lam_t, in_=lam_ap), writes=[K("lam")], sem=lam_sem, inc=16)
                P.emit("dve", lambda hh: hh.tensor_scalar(out=kf, in0=ls, scalar1=1.0 / 16.0, scalar2=None, op0=ALU.mult), reads=[K("lam")], writes=[K("kf")])
                poly(dt_, kf, EXPC, "dt", "kf")
                for _ in range(4):
                    tt(dt_, dt_, dt_, ALU.mult, "dt", "dt", "dt")
                tt(kf, lr, dt_, ALU.mult, "kf", "lam", "dt")
                poly(mag, kf, EXPC[:9], "mag", "kf")
                tt(th, li, dt_, ALU.mult, "th", "lam", "dt")
                P.emit("dve", lambda hh: hh.tensor_scalar(out=th, in0=th, scalar1=1.0 / 16.0, scalar2=None, op0=ALU.mult), reads=[K("th")], writes=[K("th")])
                tt(r, th, th, ALU.mult, "r", "th", "th")
                poly(sn, r, SINC, "sn", "r")
                tt(sn, sn, th, ALU.mult, "sn", "sn", "th")
                poly(cs, r, COSC, "cs", "r")
                for _ in range(4):
                    tt(msk, cs, sn, ALU.mult, "msk", "cs", "sn")
                    tt(cs, cs, cs, ALU.mult, "cs", "cs", "cs")
                    tt(sn, sn, sn, ALU.mult, "sn", "sn", "sn")
                    tt(cs, cs, sn, ALU.subtract, "cs", "cs", "sn")
                    P.emit("dve", lambda hh: hh.tensor_scalar(out=sn, in0=msk, scalar1=2.0, scalar2=None, op0=ALU.mult), reads=[K("msk")], writes=[K("sn")])
                P.emit("dve", lambda hh: hh.tensor_tensor(out=are, in0=mag, in1=cs, op=ALU.mult), reads=[K("mag"), K("cs")], writes=[K("are")])
                P.emit("dve", lambda hh: hh.tensor_tensor(out=aim, in0=mag, in1=sn, op=ALU.mult), reads=[K("mag"), K("sn")], writes=[K("aim")])
                if want_q:
                    P.emit("dve", lambda hh: hh.tensor_tensor(out=den, in0=lr, in1=lr, op=ALU.mult), reads=[K("lam")], writes=[K("den")])
                    P.emit("dve", lambda hh: hh.tensor_tensor(out=tq, in0=li, in1=li, op=ALU.mult), reads=[K("lam")], writes=[K("tq")])
                    P.emit("dve", lambda hh: hh.tensor_tensor(out=den, in0=den, in1=tq, op=ALU.add), reads=[K("den"), K("tq")], writes=[K("den")])
                    P.emit("dve", lambda hh: hh.reciprocal(out=den, in_=den), reads=[K("den")], writes=[K("den")])
                    P.emit("dve", lambda hh: hh.tensor_scalar(out=nr, in0=are, scalar1=-1.0, scalar2=None, op0=ALU.add), reads=[K("are")], writes=[K("nr")])
                    P.emit("dve", lambda hh: hh.tensor_tensor(out=qre, in0=nr, in1=lr, op=ALU.mult), reads=[K("nr"), K("lam")], writes=[K("qre")])
                    P.emit("dve", lambda hh: hh.tensor_tensor(out=tq, in0=aim, in1=li, op=ALU.mult), reads=[K("aim"), K("lam")], writes=[K("tq")])
                    P.emit("dve", lambda hh: hh.tensor_tensor(out=qre, in0=qre, in1=tq, op=ALU.add), reads=[K("qre"), K("tq")], writes=[K("qre")])
                    P.emit("dve", lambda hh: hh.tensor_tensor(out=qre, in0=qre, in1=den, op=ALU.mult), reads=[K("qre"), K("den")], writes=[K("qre")])
                    P.emit("dve", lambda hh: hh.tensor_tensor(out=qim, in0=aim, in1=lr, op=ALU.mult), reads=[K("aim"), K("lam")], writes=[K("qim")])
                    P.emit("dve", lambda hh: hh.tensor_tensor(out=tq, in0=nr, in1=li, op=ALU.mult), reads=[K("nr"), K("lam")], writes=[K("tq")])
                    P.emit("dve", lambda hh: hh.tensor_tensor(out=qim, in0=qim, in1=tq, op=ALU.subtract), reads=[K("qim"), K("tq")], writes=[K("qim")])
                    P.emit("dve", lambda hh: hh.tensor_tensor(out=qim, in0=qim, in1=den, op=ALU.mult), reads=[K("qim"), K("den")], writes=[K("qim")])
            return T, run

        Ts, run_s = cplx_setup("ss", 64, False)
        run_s(lam_s)
        K = lambda n: ("ss", n)
        P.emit("dve", lambda hh: hh.tensor_copy(out=m1[:, 0, :], in_=Ts["are"]), reads=[K("are")], writes=[("m1",)])
        P.emit("dve", lambda hh: hh.tensor_copy(out=m1[:, 1, :], in_=Ts["are"]), reads=[K("are")], writes=[("m1",)])
        P.emit("dve", lambda hh: hh.tensor_copy(out=m2[:, 1, :], in_=Ts["aim"]), reads=[K("aim")], writes=[("m2",)])
        P.emit("dve", lambda hh: hh.tensor_scalar(out=m2[:, 0, :], in0=Ts["aim"], scalar1=-1.0, scalar2=None, op0=ALU.mult), reads=[K("aim")], writes=[("m2",)])
        m1x = sb("m1x", [128, 2, 64, 2])
        m2x = sb("m2x", [128, 2, 64, 2])
        a2re = rm_alloc(64)
        a2im = rm_alloc(64)
        a2t = rm_alloc(64)
        P.emit("dve", lambda hh: hh.tensor_tensor(out=a2re, in0=Ts["are"], in1=Ts["are"], op=ALU.mult), reads=[K("are")], writes=[("a2", 0)])
        P.emit("dve", lambda hh: hh.tensor_tensor(out=a2t, in0=Ts["aim"], in1=Ts["aim"], op=ALU.mult), reads=[K("aim")], writes=[("a2", 2)])
        P.emit("dve", lambda hh: hh.tensor_tensor(out=a2re, in0=a2re, in1=a2t, op=ALU.subtract), reads=[("a2", 0), ("a2", 2)], writes=[("a2", 0)])
        P.emit("dve", lambda hh: hh.tensor_tensor(out=a2im, in0=Ts["are"], in1=Ts["aim"], op=ALU.mult), reads=[K("are"), K("aim")], writes=[("a2", 1)])
        for ri_ in range(2):
            for ph_ in range(2):
                P.emit("dve", lambda hh, ri_=ri_, ph_=ph_: hh.tensor_copy(out=m1x[:, ri_, :, ph_], in_=a2re), reads=[("a2", 0)], writes=[("m1x",)])
                P.emit("dve", lambda hh, ri_=ri_, ph_=ph_: hh.tensor_scalar(out=m2x[:, ri_, :, ph_], in0=a2im, scalar1=(-2.0 if ri_ == 0 else 2.0), scalar2=None, op0=ALU.mult),
                       reads=[("a2", 1)], writes=[("m2x",)])
        FB = 256
        NQB = FB // 128
        bbk = rm_alloc(NQB * 2 * 128).rearrange("p (c r m) -> p c r m", c=NQB, r=2)
        abk = rm_alloc(NQB * 2 * 128).rearrange("p (c r m) -> p c r m", c=NQB, r=2)
        P.emit("dve", lambda hh: hh.memset(bbt[:].rearrange("p a r m -> p (a r m)"), 0.0), writes=[("bbtp", 0), ("bbtp", 1)])
        P.emit("dve", lambda hh: hh.memset(abt[:].rearrange("p a r m -> p (a r m)"), 0.0), writes=[("abtp", 0), ("abtp", 1)])
        Tt_, run_t = cplx_setup("st", 128, True)
        run_t(lam_t)
        t4 = rm_alloc(4 * 128).rearrange("p (k m) -> p k m", k=4)
        for k_, nm_ in enumerate(("are", "aim", "qre", "qim")):
            P.emit("dve", lambda hh, k_=k_, nm_=nm_: hh.tensor_copy(out=t4[:, k_, :], in_=Tt_[nm_]), reads=[("st", nm_)], writes=[("t4",)])
        P.emit("sp", lambda hh: hh.dma_start(out=dscr, in_=t4[0:64, :, :]), reads=[("t4",)], writes=[("dscr",)], sem=P.new_sem("st_dscr"), inc=16)
        cm2 = [rm_alloc(4 * FB).rearrange("p (k f) -> p k f", k=4) for _ in range(2)]
        cm_sem2 = [P.new_sem("ld_cm0"), P.new_sem("ld_cm1")]
        dscr_v = dscr.rearrange("(q f) k m -> f k q m", f=4)
        bp2 = [rm_alloc(2 * FB).rearrange("p (a f) -> p a f", a=2) for _ in range(2)]
        bp_sem2 = [P.new_sem("ld_bp0"), P.new_sem("ld_bp1")]
        K = lambda n: ("sc", n)
        u1 = rm_alloc(FB); u2 = rm_alloc(FB)
        v3 = lambda a: a.rearrange("p (c m) -> p c m", m=128)
        bbt_v = bbt[:].rearrange("p (q f) r m -> p q f r m", f=4)
        abt_v = abt[:].rearrange("p (q f) r m -> p q f r m", f=4)

        def blk_dma(blk):
            par = blk % 2
            cm, bp_t = cm2[par], bp2[par]
            f0 = blk * FB
            q0 = f0 // 128
            for g4 in range(4):
                for k_ in range(4):
                    P.emit("sp", lambda hh, g4=g4, q0=q0, k_=k_, cm=cm: hh.dma_start(
                        out=cm[32 * g4:32 * g4 + 32, k_, :].rearrange("p (q m) -> p q m", m=128),
                        in_=dscr_v[g4, k_, q0:q0 + NQB, :].partition_broadcast(32)),
                        reads=[("dscr",)], writes=[("cm", par, g4, k_)], sem=cm_sem2[par], inc=16)
            P.emit("sp", lambda hh, f0=f0, bp_t=bp_t: hh.dma_start(out=bp_t, in_=bp[:, :, f0:f0 + FB]), writes=[("bp_t", par)], sem=bp_sem2[par], inc=16)

        def blk_compute(blk):
            par = blk % 2
            cm, bp_t = cm2[par], bp2[par]
            q0 = (blk * FB) // 128
            cmk = [("cm", par, g4_, k__) for g4_ in range(4) for k__ in range(4)]
            bpk = ("bp_t", par)
            qre, qim = cm[:, 2, :], cm[:, 3, :]
            arc, aic = cm[:, 0, :], cm[:, 1, :]
            b_re, b_im = bbk[:, :, 0, :], bbk[:, :, 1, :]
            a_re, a_im = abk[:, :, 0, :], abk[:, :, 1, :]
            P.emit("dve", lambda hh: hh.tensor_tensor(out=u1, in0=qre, in1=bp_t[:, 0, :], op=ALU.mult), reads=[*cmk, bpk, K("r")], writes=[K("r")])
            P.emit("dve", lambda hh: hh.tensor_tensor(out=u2, in0=qim, in1=bp_t[:, 1, :], op=ALU.mult), reads=[*cmk, bpk, K("msk")], writes=[K("msk")])
            P.emit("dve", lambda hh: hh.tensor_tensor(out=b_re, in0=v3(u1), in1=v3(u2), op=ALU.subtract), reads=[K("r"), K("msk")], writes=[("bbk", 0)])
            P.emit("dve", lambda hh: hh.tensor_tensor(out=u1, in0=qre, in1=bp_t[:, 1, :], op=ALU.mult), reads=[*cmk, bpk, K("r")], writes=[K("r")])
            P.emit("dve", lambda hh: hh.tensor_tensor(out=u2, in0=qim, in1=bp_t[:, 0, :], op=ALU.mult), reads=[*cmk, bpk, K("msk")], writes=[K("msk")])
            P.emit("dve", lambda hh: hh.tensor_tensor(out=b_im, in0=v3(u1), in1=v3(u2), op=ALU.add), reads=[K("r"), K("msk")], writes=[("bbk", 1)])
            P.emit("dve", lambda hh: hh.tensor_tensor(out=v3(u1), in0=v3(arc), in1=b_re, op=ALU.mult), reads=[*cmk, ("bbk", 0), K("r")], writes=[K("r")])
            P.emit("dve", lambda hh: hh.tensor_tensor(out=v3(u2), in0=v3(aic), in1=b_im, op=ALU.mult), reads=[*cmk, ("bbk", 1), K("msk")], writes=[K("msk")])
            P.emit("dve", lambda hh: hh.tensor_tensor(out=a_re, in0=v3(u1), in1=v3(u2), op=ALU.subtract), reads=[K("r"), K("msk")], writes=[("abk", 0)])
            P.emit("dve", lambda hh: hh.tensor_tensor(out=v3(u1), in0=v3(arc), in1=b_im, op=ALU.mult), reads=[*cmk, ("bbk", 1), K("r")], writes=[K("r")])
            P.emit("dve", lambda hh: hh.tensor_tensor(out=v3(u2), in0=v3(aic), in1=b_re, op=ALU.mult), reads=[*cmk, ("bbk", 0), K("msk")], writes=[K("msk")])
            P.emit("dve", lambda hh: hh.tensor_tensor(out=a_im, in0=v3(u1), in1=v3(u2), op=ALU.add), reads=[K("r"), K("msk")], writes=[("abk", 1)])
            for g4 in range(4):
                for ri in range(2):
                    P.emit("dve", lambda hh, g4=g4, ri=ri, q0=q0: hh.tensor_copy(out=bbt_v[32 * g4:32 * g4 + 32, q0:q0 + NQB, g4, ri, :], in_=bbk[32 * g4:32 * g4 + 32, :, ri, :]),
                           reads=[("bbk", ri)], writes=[("bbtp", ri)])
                    P.emit("dve", lambda hh, g4=g4, ri=ri, q0=q0: hh.tensor_copy(out=abt_v[32 * g4:32 * g4 + 32, q0:q0 + NQB, g4, ri, :], in_=abk[32 * g4:32 * g4 + 32, :, ri, :]),
                           reads=[("abk", ri)], writes=[("abtp", ri)])

        NBLK = 2048 // FB
        blk_dma(0)
        for blk in range(NBLK):
            if blk + 1 < NBLK:
                blk_dma(blk + 1)
            blk_compute(blk)
        P.barrier()

        bank_rr = [0]

        def next_banks(n):
            s = bank_rr[0] % 2
            bank_rr[0] += 1
            return [3 * s + i for i in range(n)]

        def mm_group(banks, subs, slots_rhs, extra_reads=()):
            total = sum(x[4] for x in slots_rhs)
            idx = 0
            last = None
            for (lf, wkey, rf, rkeys, nk) in slots_rhs:
                for k in range(nk):
                    for si, (c0, n) in enumerate(subs):
                        is_last = (idx == total - 1) and (si == len(subs) - 1)
                        last = P.emit("pe", lambda hh, b=banks[si], n=n, c0=c0, k=k, lf=lf, rf=rf, st_=(idx == 0), sp_=(idx == total - 1):
                                      hh.matmul(ps[:, b, 0:n], lhsT=lf(k), rhs=rf(k, c0, n), start=st_, stop=sp_),
                                      reads=[wkey] + list(rkeys(k)) + list(extra_reads), writes=[("ps", banks[si])], signal=is_last)
                    idx += 1
            return last

        def rmsnorm(subs, TT, gi, dst_fn, dst_key):
            bnk = [6, 7, 6][:len(subs)]
            for c in range(NCH):
                sl = c % 2
                P.emit("act", lambda hh, c=c, sl=sl: hh.activation(out=sqb[:, sl, 0:TT], in_=h[:, c, 0:TT], func=AF.Square),
                       reads=[("h", c)], writes=[("sqb", sl)])
                for si, (c0, n) in enumerate(subs):
                    bb = 6 + (si % 2) if len(subs) <= 2 else [6, 7, 5][si]
                    P.emit("pe", lambda hh, bb=bb, c=c, sl=sl, c0=c0, n=n: hh.matmul(ps[:, bb, 0:n], lhsT=ones[:], rhs=sqb[:, sl, c0:c0 + n],
                                                                                 start=(c == 0), stop=(c == NCH - 1)),
                           reads=[("ones",), ("sqb", sl)], writes=[("ps", bb)], signal=True)
            for si, (c0, n) in enumerate(subs):
                bb = 6 + (si % 2) if len(subs) <= 2 else [6, 7, 5][si]
                P.emit("act", lambda hh, bb=bb, c0=c0, n=n: hh.activation(out=rstd[:, c0:c0 + n], in_=ps[:, bb, 0:n], func=AF.Sqrt, bias=eps_t[:, 0:1], scale=1.0 / D),
                       reads=[("ps", bb), ("eps",)], writes=[("rstd", si)])
                P.emit("dve", lambda hh, c0=c0, n=n: hh.reciprocal(out=rstd[:, c0:c0 + n], in_=rstd[:, c0:c0 + n]),
                       reads=[("rstd", si)], writes=[("rstd", si)])
            for c in range(NCH):
                P.emit("dve", lambda hh, c=c: hh.scalar_tensor_tensor(out=dst_fn(c), in0=h[:, c, 0:TT], scalar=gam_t[:, gi, c:c + 1], in1=rstd[:, 0:TT],
                                                                    op0=ALU.mult, op1=ALU.mult),
                       reads=[("h", c), ("gam",)] + [("rstd", si) for si in range(len(subs))], writes=[(dst_key, c)])

        eps_t = sb("eps_t", [128, 1])
        P.emit("dve", lambda hh: hh.memset(eps_t[:], EPS), writes=[("eps",)])

        def h_add_psum(m, banks, subs):
            for si, (c0, n) in enumerate(subs):
                P.emit("dve", lambda hh, m=m, b=banks[si], c0=c0, n=n: hh.tensor_tensor(out=h[:, m, c0:c0 + n], in0=h[:, m, c0:c0 + n], in1=ps[:, b, 0:n], op=ALU.add),
                       reads=[("ps", banks[si]), ("h", m)], writes=[("h", m)])

        def mlp(layer, subs, TT):
            def up(qd):
                hb = hq[qd % 2]
                for m16 in range(16):
                    f = qd * 16 + m16
                    slot = load_w(w_up[layer, f])
                    banks = next_banks(len(subs))
                    mm_group(banks, subs, [(lambda k, slot=slot: wv(slot)[:, k, :], ("w", slot),
                                            lambda k, c0, n: xn[:, k, c0:c0 + n], lambda k: [("xn", k)], 16)])
                    for si, (c0, n) in enumerate(subs):
                        P.emit("act", lambda hh, b=banks[si], c0=c0, n=n: hh.activation(out=relu_t[:, 0:n], in_=ps[:, b, 0:n], func=AF.Relu),
                               reads=[("ps", banks[si])], writes=[("relu",)])
                        P.emit("dve", lambda hh, hb=hb, m16=m16, c0=c0, n=n: hh.tensor_tensor(out=hb[:, m16, c0:c0 + n], in0=relu_t[:, 0:n], in1=relu_t[:, 0:n], op=ALU.mult),
                               reads=[("relu",)], writes=[("hq", qd % 2, m16)])

            def down(qd):
                hb = hq[qd % 2]
                for m in range(16):
                    slot = load_w(w_dn[layer, qd * 16 + m])
                    banks = next_banks(len(subs))
                    mm_group(banks, subs, [(lambda k, slot=slot: wv(slot)[:, k, :], ("w", slot),
                                            lambda k, c0, n, hb=hb: hb[:, k, c0:c0 + n], lambda k, qd=qd: [("hq", qd % 2, k)], 16)])
                    h_add_psum(m, banks, subs)
            up(0)
            for qd in range(1, 4):
                up(qd)
                down(qd - 1)
            down(3)

        def conv_mixer(subs, TT, has_samp, tile_is_last_b):
            npv = 2 + NPT + (NSS * 6 if has_samp else 0)
            W = npv - 2
            for m in range(NCH):
                vb = vbuf[m % 2]
                vk = ("vbuf", m % 2)
                s_c = load_w(w_in[16 + m])
                s_h = load_w(w_in[32 + m])
                s_b = load_w(w_in[m])
                rk = lambda k: [("xn", k)]
                rf = lambda k, c0, n: xn[:, k, c0:c0 + n]
                bk_c = next_banks(len(subs))
                mm_group(bk_c, subs, [(lambda k, s=s_c: wv(s)[:, k, :], ("w", s_c), rf, rk, 16)])
                bk_h = next_banks(len(subs))
                mm_group(bk_h, subs, [(lambda k, s=s_h: wv(s)[:, k, :], ("w", s_h), rf, rk, 16)])
                P.emit("dve", lambda hh, vb=vb, m=m: hh.tensor_copy(out=vb[:, 0:2], in_=carry_v[:, m, :]), reads=[("carry_v",)], writes=[vk])
                if has_samp:
                    vs = vb[:, 2 + NPT:2 + NPT + NSS * 6].rearrange("p (s t) -> p s t", t=6)
                    P.emit("dve", lambda hh, vs=vs, m=m: hh.tensor_copy(out=vs[:, :, 0:2], in_=convs_t[:, m, :, :]), reads=[("convs_t",)], writes=[vk])
                for si, (c0, n) in enumerate(subs):
                    P.emit("act", lambda hh, b=bk_c[si], c0=c0, n=n: hh.activation(out=tmpc[:, c0:c0 + n], in_=ps[:, b, 0:n], func=AF.Copy),
                           reads=[("ps", bk_c[si])], writes=[("tmpc", si)])
                    if c0 < NPT:
                        P.emit("dve", lambda hh, vb=vb, b=bk_h[si], c0=c0, n=n: hh.tensor_tensor(out=vb[:, 2 + c0:2 + c0 + n], in0=tmpc[:, c0:c0 + n], in1=ps[:, b, 0:n], op=ALU.mult),
                               reads=[("tmpc", si), ("ps", bk_h[si])], writes=[vk])
                    else:
                        vs = vb[:, 2 + NPT:2 + NPT + NSS * 6].rearrange("p (s t) -> p s t", t=6)
                        P.emit("dve", lambda hh, vs=vs, b=bk_h[si], c0=c0, n=n: hh.tensor_tensor(
                            out=vs[:, :, 2:6], in0=tmpc[:, c0:c0 + n].rearrange("p (s t) -> p s t", t=4),
                            in1=ps[:, b, 0:n].rearrange("p (s t) -> p s t", t=4), op=ALU.mult),
                            reads=[("tmpc", si), ("ps", bk_h[si])], writes=[vk])
                bk_b = next_banks(len(subs))
                mm_group(bk_b, subs, [(lambda k, s=s_b: wv(s)[:, k, :], ("w", s_b), rf, rk, 16)])
                P.emit("dve", lambda hh, vb=vb, m=m: hh.tensor_copy(out=carry_v[:, m, :], in_=vb[:, NPT:NPT + 2]), reads=[vk], writes=[("carry_v",)])
                if has_samp:
                    vs = vb[:, 2 + NPT:2 + NPT + NSS * 6].rearrange("p (s t) -> p s t", t=6)
                    P.emit("dve", lambda hh, vs=vs, m=m: hh.tensor_copy(out=convs_t[:, m, :, :], in_=vs[:, :, 4:6]), reads=[vk], writes=[("convs_t",)])
                P.emit("dve", lambda hh, vb=vb, m=m: hh.tensor_scalar(out=convt[:, 0:W], in0=vb[:, 0:W], scalar1=convw_t[:, m, 0:1], scalar2=None, op0=ALU.mult),
                       reads=[vk, ("convw",)], writes=[("convt",)])
                P.emit("dve", lambda hh, vb=vb, m=m: hh.scalar_tensor_tensor(out=convt[:, 0:W], in0=vb[:, 1:W + 1], scalar=convw_t[:, m, 1:2], in1=convt[:, 0:W], op0=ALU.mult, op1=ALU.add),
                       reads=[vk, ("convw",), ("convt",)], writes=[("convt",)])
                P.emit("dve", lambda hh, vb=vb, m=m: hh.scalar_tensor_tensor(out=convt[:, 0:W], in0=vb[:, 2:W + 2], scalar=convw_t[:, m, 2:3], in1=convt[:, 0:W], op0=ALU.mult, op1=ALU.add),
                       reads=[vk, ("convw",), ("convt",)], writes=[("convt",)])
                for si, (c0, n) in enumerate(subs):
                    if c0 < NPT:
                        P.emit("dve", lambda hh, m=m, b=bk_b[si], c0=c0, n=n: hh.tensor_tensor(out=g0[:, m, c0:c0 + n], in0=convt[:, c0:c0 + n], in1=ps[:, b, 0:n], op=ALU.mult),
                               reads=[("convt",), ("ps", bk_b[si])], writes=[("g0", m)])
                    else:
                        cs_ = convt[:, NPT + 2:NPT + 2 + NSS * 6].rearrange("p (s t) -> p s t", t=6)
                        P.emit("dve", lambda hh, m=m, cs_=cs_, b=bk_b[si], c0=c0, n=n: hh.tensor_tensor(
                            out=g0[:, m, c0:c0 + n].rearrange("p (s t) -> p s t", t=4), in0=cs_[:, :, 0:4],
                            in1=ps[:, b, 0:n].rearrange("p (s t) -> p s t", t=4), op=ALU.mult),
                            reads=[("convt",), ("ps", bk_b[si])], writes=[("g0", m)])
            for m in range(NCH):
                slot = load_w(w_out[m])
                banks = next_banks(len(subs))
                mm_group(banks, subs, [(lambda k, slot=slot: wv(slot)[:, k, :], ("w", slot),
                                        lambda k, c0, n: g0[:, k, c0:c0 + n], lambda k: [("g0", k)], 16)])
                h_add_psum(m, banks, subs)

        def ssm_bproj(t0, n, ck, par, seg=None):
            for ri in range(2):
                for g16 in range(4):
                    bb = 6 + ((ri * 4 + g16) % 2)
                    for j in range(16):
                        gp = g16 * 16 + j
                        q = gp // 4
                        P.emit("pe", lambda hh, bb=bb, j=j, q=q, gp=gp, ri=ri: hh.matmul(
                            ps[:, bb, j * TC:j * TC + n], lhsT=bbt[:, gp, ri, :], rhs=xn[:, q, t0:t0 + n],
                            start=True, stop=False),
                            reads=[("bbtp", ri), ("xn", q), ("xnc", q, ck)], writes=[("ps", bb)], signal=False)
                        if seg is None:
                            P.emit("pe", lambda hh, bb=bb, j=j, q=q, gp=gp, ri=ri: hh.matmul(
                                ps[:, bb, j * TC + 1:j * TC + n], lhsT=abt[:, gp, ri, :], rhs=xn[:, q, t0:t0 + n - 1],
                                start=False, stop=True),
                                reads=[("abtp", ri), ("xn", q), ("xnc", q, ck)], writes=[("ps", bb)], signal=(j == 15))
                        else:
                            P.emit("pe", lambda hh, bb=bb, j=j, q=q, gp=gp, ri=ri: hh.matmul(
                                ps[:, bb, j * TC:j * TC + n].rearrange("p (s t) -> p s t", t=seg)[:, :, 1:seg], lhsT=abt[:, gp, ri, :],
                                rhs=xn[:, q, t0:t0 + n].rearrange("p (s t) -> p s t", t=seg)[:, :, 0:seg - 1],
                                start=False, stop=True),
                                reads=[("abtp", ri), ("xn", q), ("xnc", q, ck)], writes=[("ps", bb)], signal=(j == 15))
                    P.emit("act", lambda hh, bb=bb, ri=ri, g16=g16: hh.activation(
                        out=bu[par][:, ri, g16 * 16:g16 * 16 + 16, 0:n], in_=ps[:, bb, 0:16 * TC].rearrange("p (j t) -> p j t", j=16)[:, :, 0:n], func=AF.Copy),
                        reads=[("ps", bb)], writes=[("bu", par, ri, g16)])

        def ssm_recur(n, par, init_ap, init_key, boff=0):
            assert n % 2 == 0
            bk_ = [("bu", par, ri, g16) for ri in range(2) for g16 in range(4)]
            P.emit("dve", lambda hh: hh.memset(traj[:, :, :, 0], 0.0), writes=[("traj",)])
            P.emit("dve", lambda hh: hh.tensor_copy(out=traj[:, :, :, 1], in_=init_ap), reads=[init_key], writes=[("traj",)])
            P.emit("dve", lambda hh: hh.tensor_tensor(out=c1b[:], in0=m1[:], in1=init_ap, op=ALU.mult), reads=[("m1",), init_key], writes=[("cc", 1)])
            P.emit("dve", lambda hh: hh.tensor_tensor(out=c2b[:], in0=m2[:], in1=init_ap[:, ::-1, :], op=ALU.mult), reads=[("m2",), init_key], writes=[("cc", 2)])
            P.emit("dve", lambda hh: hh.tensor_tensor(out=c1b[:], in0=c1b[:], in1=c2b[:], op=ALU.add), reads=[("cc", 1), ("cc", 2)], writes=[("cc", 1)])
            P.emit("dve", lambda hh: hh.tensor_tensor(out=bu[par][:, :, :, boff], in0=c1b[:], in1=bu[par][:, :, :, boff], op=ALU.add),
                   reads=[("cc", 1)] + bk_, writes=[("w", par, "h")])
            for c in range(0, n, 2):
                P.emit("dve", lambda hh, c=c: hh.tensor_tensor(out=t1b[:], in0=m1x[:], in1=traj[:, :, :, c:c + 2], op=ALU.mult), reads=[("m1x",), ("traj",)], writes=[("tt", 1)])
                P.emit("dve", lambda hh, c=c: hh.tensor_tensor(out=t2b[:], in0=m2x[:], in1=traj[:, ::-1, :, c:c + 2], op=ALU.mult), reads=[("m2x",), ("traj",)], writes=[("tt", 2)])
                P.emit("dve", lambda hh, c=c: hh.tensor_tensor(out=t1b[:], in0=t1b[:], in1=bu[par][:, :, :, boff + c:boff + c + 2], op=ALU.add),
                       reads=[("tt", 1), ("w", par, "h")] + (bk_ if (c == 0 or c == n - 2) else []), writes=[("tt", 1)])
                P.emit("dve", lambda hh, c=c: hh.tensor_tensor(out=traj[:, :, :, c + 2:c + 4], in0=t1b[:], in1=t2b[:], op=ALU.add),
                       reads=[("tt", 1), ("tt", 2)], writes=[("traj",)])

        def ssm_trajb(n, boff=0):
            P.emit("act", lambda hh: hh.activation(out=trajb[:, :, :, boff:boff + n], in_=traj[:, :, :, 2:n + 2], func=AF.Copy), reads=[("traj",)], writes=[("trajb",)])

        def ssm_cproj(t0, n, ck, yb):
            for q in range(16):
                P.emit("pe", lambda hh, q=q: hh.matmul(
                    ps[:, yb, q * TC:q * TC + n], lhsT=ddg[:, q, :], rhs=xn[:, q, t0:t0 + n], start=True, stop=False),
                    reads=[("ddg",), ("xn", q), ("xnc", q, ck)], writes=[("ps", yb)], signal=False)
                for g4 in range(4):
                    gp = q * 4 + g4
                    for ri in range(2):
                        lastmm = (g4 == 3 and ri == 1)
                        P.emit("pe", lambda hh, q=q, gp=gp, g4=g4, ri=ri, lastmm=lastmm: hh.matmul(
                            ps[32 * g4:32 * g4 + 32, yb, q * TC:q * TC + n], lhsT=cpad[:, ri, gp * 32:(gp + 1) * 32], rhs=trajb[:, ri, gp, 0:n],
                            start=False, stop=(ri == 1), tile_position=(0, 32 * g4)),
                            reads=[("cpad", ri), ("trajb",)], writes=[("ps", yb)], signal=(q == 15 and lastmm))

        def ssm_gelu(t0, n, ck, yb):
            yv = ps[:, yb, 0:16 * TC]
            P.emit("act", lambda hh: hh.activation(out=gel[0][:], in_=yv, func=AF.Square), reads=[("ps", yb)], writes=[("gel", 0)])
            P.emit("dve", lambda hh: hh.tensor_scalar(out=gel[0][:], in0=gel[0][:], scalar1=0.044715, scalar2=1.0, op0=ALU.mult, op1=ALU.add),
                   reads=[("gel", 0)], writes=[("gel", 0)])
            P.emit("dve", lambda hh: hh.tensor_tensor(out=gel[1][:], in0=gel[0][:], in1=yv, op=ALU.mult), reads=[("gel", 0), ("ps", yb)], writes=[("gel", 1)])
            P.emit("act", lambda hh: hh.activation(out=gel[2][:], in_=gel[1][:], func=AF.Sigmoid, scale=1.5957691216057308), reads=[("gel", 1)], writes=[("gel", 2)])
            P.emit("dve", lambda hh: hh.tensor_tensor(
                out=xn[:, :, t0:t0 + n], in0=gel[2][:].rearrange("p (j t) -> p j t", j=16)[:, :, 0:n],
                in1=yv.rearrange("p (j t) -> p j t", j=16)[:, :, 0:n], op=ALU.mult),
                reads=[("gel", 2), ("ps", yb)], writes=[("xnc", q, ck) for q in range(16)])
            cks.add(ck)

        cks = set()

        def ssm_mixer(subs, TT, has_samp, full, is_last_b):
            cks.clear()
            chunks = []
            t0 = 0
            for n in [16] * 32 + [4]:
                chunks.append((t0, n, "p", None))
                t0 += n
            assert t0 == NPT
            if has_samp:
                for g_ in range(NSS // 4):
                    chunks.append((NPT + 16 * g_, 16, "s", g_))
            nprompt = 33
            ssm_bproj(chunks[0][0], chunks[0][1], 0, 0)
            pend = None
            segof = lambda kind: (4 if kind == "s" else None)
            for ci, (t0, n, kind, s_) in enumerate(chunks):
                par = ci % 2
                if ci + 1 < len(chunks):
                    ssm_bproj(chunks[ci + 1][0], chunks[ci + 1][1], ci + 1, (ci + 1) % 2, segof(chunks[ci + 1][2]))
                if kind == "p":
                    if ci > 0:
                        P.emit("dve", lambda hh, pn=chunks[ci - 1][1]: hh.tensor_copy(out=xcarry[:], in_=traj[:, :, :, pn + 1]), reads=[("traj",)], writes=[("xcarry",)])
                    ssm_recur(n, par, xcarry[:], ("xcarry",))
                    if ci == nprompt - 1:
                        P.emit("dve", lambda hh, n=n: hh.tensor_copy(out=xcarry[:], in_=traj[:, :, :, n + 1]), reads=[("traj",)], writes=[("xcarry",)])
                    if full:
                        ssm_trajb(n)
                else:
                    for l_ in range(4):
                        sq_ = 4 * s_ + l_
                        ssm_recur(4, par, ssms_t[:, sq_, :, :], ("ssms_t",), boff=4 * l_)
                        P.emit("dve", lambda hh, sq_=sq_: hh.tensor_copy(out=ssms_t[:, sq_, :, :], in_=traj[:, :, :, 5]), reads=[("traj",)], writes=[("ssms_t",)])
                        if full:
                            ssm_trajb(4, boff=4 * l_)
                if full:
                    yb = 3 + (ci % 2)
                    ssm_cproj(t0, n, ci, yb)
                    if pend is not None:
                        ssm_gelu(*pend)
                    pend = (t0, n, ci, yb)
            if full and pend is not None:
                ssm_gelu(*pend)
            if not full:
                return
            for m in range(NCH):
                sa = load_w(w_ga[m])
                sb_ = load_w(w_gb[m])
                ckl = sorted(cks)
                rk = lambda k, ckl=ckl: [("xn", k)] + [("xnc", k, c_) for c_ in ckl]
                rf = lambda k, c0, n: xn[:, k, c0:c0 + n]
                bka = next_banks(len(subs))
                mm_group(bka, subs, [(lambda k, s=sa: wv(s)[:, k, :], ("w", sa), rf, rk, 16)])
                bkb = next_banks(len(subs))
                mm_group(bkb, subs, [(lambda k, s=sb_: wv(s)[:, k, :], ("w", sb_), rf, rk, 16)])
                for si, (c0, n) in enumerate(subs):
                    P.emit("act", lambda hh, b=bkb[si], c0=c0, n=n: hh.activation(out=glu_t[0][:, c0:c0 + n], in_=ps[:, b, 0:n], func=AF.Sigmoid),
                           reads=[("ps", bkb[si])], writes=[("glu", 0, si)])
                    P.emit("dve", lambda hh, b=bka[si], c0=c0, n=n: hh.tensor_tensor(out=glu_t[1][:, c0:c0 + n], in0=glu_t[0][:, c0:c0 + n], in1=ps[:, b, 0:n], op=ALU.mult),
                           reads=[("glu", 0, si), ("ps", bka[si])], writes=[("glu", 1, si)])
                    P.emit("dve", lambda hh, m=m, c0=c0, n=n: hh.tensor_tensor(out=h[:, m, c0:c0 + n], in0=h[:, m, c0:c0 + n], in1=glu_t[1][:, c0:c0 + n], op=ALU.add),
                           reads=[("glu", 1, si), ("h", m)], writes=[("h", m)])

        tiles = []
        for ti in range(NTILE_HALF):
            tiles.append(dict(col=ti * NPT, full=False, samp=False, last=False, ocol=None))
        for ti in range(NTILE_HALF):
            lastb = ti == NTILE_HALF - 1
            tiles.append(dict(col=HALF + ti * NPT, full=True, samp=lastb, last=lastb, ocol=ti * NPT))

        for tinfo in tiles:
            full, samp = tinfo["full"], tinfo["samp"]
            TT = NPT + (NSAMP if samp else 0)
            subs = [(0, NPT // 2), (NPT // 2, NPT // 2)] + ([(NPT, NSAMP)] if samp else [])
            for c in range(NCH):
                pass
            P.emit("sp", lambda hh, col=tinfo["col"]: hh.dma_start(out=h[:, :, 0:NPT], in_=xin_v[:, :, col:col + NPT]),
                   writes=[("h", c) for c in range(NCH)], sem=ldx, inc=16)
            if samp:
                P.emit("sp", lambda hh: hh.dma_start(out=h[:, :, NPT:NPT + NSAMP], in_=xin_v[:, :, 2 * HALF:2 * HALF + NSAMP]),
                       writes=[("h", c) for c in range(NCH)], sem=ldx, inc=16)
            rm_switch()
            rmsnorm(subs, TT, 0, lambda c, TT=TT: xn[:, c, 0:TT], "xn")
            conv_mixer(subs, TT, samp, tinfo["last"])
            if stage >= 2:
                rm_switch()
                rmsnorm(subs, TT, 1, lambda c, TT=TT: xn[:, c, 0:TT], "xn")
                mlp(0, subs, TT)
            if stage >= 3:
                rm_switch()
                rmsnorm(subs, TT, 2, lambda c, TT=TT: xn[:, c, 0:TT], "xn")
                ssm_mixer(subs, TT, samp, full and stage >= 4, tinfo["last"])
            if full and stage >= 5:
                rm_switch()
                rmsnorm(subs, TT, 3, lambda c, TT=TT: xn[:, c, 0:TT], "xn")
                mlp(1, subs, TT)
            if full:
                rm_switch()
                if stage >= 6:
                    rmsnorm(subs, TT, 4, lambda c, TT=TT: yfin[:, c, 0:TT], "yfin")
                else:
                    for c in range(NCH):
                        P.emit("dve", lambda hh, c=c, TT=TT: hh.tensor_copy(out=yfin[:, c, 0:TT], in_=h[:, c, 0:TT]), reads=[("h", c)], writes=[("yfin", c)])
                oc = tinfo["ocol"]
                P.emit("sp", lambda hh, oc=oc: hh.dma_start(out=yout_v[:, :, oc:oc + NPT], in_=yfin[:, :, 0:NPT]),
                       reads=[("yfin", c) for c in range(NCH)], sem=st, inc=16)
                if samp:
                    P.emit("sp", lambda hh: hh.dma_start(out=yout_v[:, :, HALF:HALF + NSAMP], in_=yfin[:, :, NPT:NPT + NSAMP]),
                           reads=[("yfin", c) for c in range(NCH)], sem=st, inc=16)
        P.emit("sp", lambda hh: hh.dma_start(out=convp_out, in_=carry_v[:]), reads=[("carry_v",)], sem=st, inc=16)
        P.emit("sp", lambda hh: hh.dma_start(out=ssmp_out, in_=xcarry[:]), reads=[("xcarry",)], sem=st, inc=16)
        P.emit("sp", lambda hh: hh.dma_start(out=convs_out, in_=convs_t[:]), reads=[("convs_t",)], sem=st, inc=16)
        P.emit("sp", lambda hh: hh.dma_start(out=ssms_out, in_=ssms_t[:]), reads=[("ssms_t",)], sem=st, inc=16)
        P.barrier()

        with ExitStack() as es3:
            for s in P.sems:
                s.h = es3.enter_context(nc.semaphore(s.name))
            block = es3.enter_context(nc.Block())

            @block.tensor
            def _(e):
                for f in P.prog["pe"]:
                    f(e)

            @block.scalar
            def _(e):
                for f in P.prog["act"]:
                    f(e)

            @block.vector
            def _(e):
                for f in P.prog["dve"]:
                    f(e)

            @block.gpsimd
            def _(e):
                for f in P.prog["pool"]:
                    f(e)

            @block.sync
            def _(e):
                for f in P.prog["sp"]:
                    f(e)
    return nc


def _slabs(W):
    K, M = W.shape
    a = W.reshape(K // 128, 128, M // 128, 128)
    return np.ascontiguousarray(a.transpose(2, 1, 0, 3)).reshape(M // 128, 128, (K // 128) * 128)


def _fm(x):
    return np.ascontiguousarray(x.T)


def _prep_shared(inp):
    f = lambda k: np.asarray(inp[k], dtype=np.float32)
    sh = {}
    sh["w_in"] = _slabs(f("conv_w_in")[0])
    sh["w_out"] = _slabs(f("conv_w_out")[0])
    sh["w_up"] = np.stack([_slabs(f("mlp_w_up")[l]) for l in range(2)])
    dn = []
    for l in range(2):
        Wd = f("mlp_w_down")[l]
        q = [_slabs(Wd[qd * 2048:(qd + 1) * 2048]) for qd in range(4)]
        dn.append(np.concatenate(q, axis=0))
    sh["w_dn"] = np.stack(dn)
    sh["w_ga"] = _slabs(f("ssm_glu_w_a")[0])
    sh["w_gb"] = _slabs(f("ssm_glu_w_b")[0])
    gam = np.stack([f("norm_mixer")[0], f("norm_mlp")[0], f("norm_mixer")[1], f("norm_mlp")[1], f("norm_final")])
    sh["gam"] = np.ascontiguousarray(gam.reshape(5, 16, 128).transpose(2, 0, 1))
    sh["convw"] = np.ascontiguousarray(f("conv_w")[0].reshape(3, 16, 128).transpose(2, 1, 0))
    lre, lim, ls = f("ssm_lambda_re")[0], f("ssm_lambda_im")[0], f("ssm_log_step")[0]
    lsb = np.broadcast_to(ls[:, None], (128, 64))
    sm = lambda a: np.ascontiguousarray(a.reshape(64, 2, 64).transpose(1, 2, 0).reshape(128, 64))
    sh["lam_s"] = np.ascontiguousarray(np.stack([sm(lre), sm(lim), sm(lsb)], axis=1))
    def cmaj(a):
        t = a.reshape(16, 4, 2, 64)
        t = np.broadcast_to(t[:, :, None, None, :, :], (16, 4, 2, 16, 2, 64))
        return np.ascontiguousarray(t.transpose(1, 2, 3, 0, 4, 5)).reshape(128, 2048)
    tm = lambda a: np.tile(a.reshape(64, 128), (2, 1))
    sh["lam_t"] = np.ascontiguousarray(np.stack([tm(lre), tm(lim), tm(lsb)], axis=1))
    def bmaj(b):
        t = b.reshape(16, 4, 2, 64, 16)
        out = np.zeros((4, 2, 16, 16, 2, 64), np.float32)
        for j in range(2):
            out[:, j, :, :, j, :] = t[:, :, j].transpose(1, 3, 0, 2)
        return out.reshape(128, 2048)
    sh["bp"] = np.ascontiguousarray(np.stack([bmaj(f("ssm_b_re")[0]), bmaj(f("ssm_b_im")[0])], axis=1))
    def cmajp(c):
        t = c.reshape(64, 2, 16, 64)
        out = np.zeros((2, 64, 64, 2, 16), np.float32)
        for j in range(2):
            out[j, :, :, j, :] = t[:, j].transpose(2, 0, 1)
        return out.reshape(128, 64 * 32)
    sh["cp"] = np.ascontiguousarray(np.stack([cmajp(f("ssm_c_re")[0]), cmajp(f("ssm_c_im")[0])], axis=1))
    dv = f("ssm_d")[0].reshape(16, 128)
    ddm = np.zeros((128, 16, 128), np.float32)
    for q in range(16):
        ddm[np.arange(128), q, np.arange(128)] = dv[q]
    sh["dd"] = ddm.reshape(128, 2048)
    return sh


_NC_CACHE = {}


def kernel(**inp):
    sh = _prep_shared(inp)
    xp = np.asarray(inp["x_prompt"], np.float32)
    xs = np.asarray(inp["x_sample"], np.float32)
    meta = np.asarray(inp["meta_tokens"], np.float32)
    sc = np.asarray(inp["state_conv"], np.float32)[0]
    sre = np.asarray(inp["state_ssm_re"], np.float32)[0]
    sim = np.asarray(inp["state_ssm_im"], np.float32)[0]
    in_maps = []
    for c in range(8):
        i, r = c // 2, c % 2
        S = np.concatenate([meta, xp[i]], axis=0)
        A = np.zeros((HALF, D), np.float32) if r == 0 else S[:HALF]
        B = S[:HALF] if r == 0 else S[HALF:]
        smp = xs[NSS * c:NSS * (c + 1)].reshape(NSAMP, D)
        m = dict(sh)
        m["xin"] = _fm(np.concatenate([A, B, smp], axis=0))
        cs = sc[NSS * c:NSS * (c + 1)]
        m["convs_in"] = np.ascontiguousarray(cs.reshape(NSS, 2, 16, 128).transpose(3, 2, 0, 1))
        def st(a):
            return a.reshape(NSS, 64, 2, 64).transpose(2, 3, 0, 1).reshape(128, NSS, 64)
        m["ssms_in"] = np.ascontiguousarray(np.stack([st(sre[NSS * c:NSS * (c + 1)]), st(sim[NSS * c:NSS * (c + 1)])], axis=2))
        in_maps.append(m)
    if "nc" not in _NC_CACHE:
        _NC_CACHE["nc"] = build()
    res = run_bass_kernel_spmd(_NC_CACHE["nc"], in_maps, core_ids=list(range(8)))
    R = res.results
    y_prompt = np.zeros((4, 2048, D), np.float32)
    y_sample = np.zeros((128, 4, D), np.float32)
    conv_p = np.zeros((1, 4, 2, D), np.float32)
    re_p = np.zeros((1, 4, 128, 64), np.float32)
    im_p = np.zeros((1, 4, 128, 64), np.float32)
    conv_s = np.zeros((1, 128, 2, D), np.float32)
    re_s = np.zeros((1, 128, 128, 64), np.float32)
    im_s = np.zeros((1, 128, 128, 64), np.float32)
    for c in range(8):
        i, r = c // 2, c % 2
        yo = np.asarray(R[c]["y_out"]).T
        if r == 0:
            y_prompt[i, 0:HALF - 16] = yo[16:HALF]
        else:
            y_prompt[i, HALF - 16:] = yo[0:HALF]
        y_sample[NSS * c:NSS * (c + 1)] = yo[HALF:].reshape(NSS, 4, D)
        cso = np.asarray(R[c]["convs_out"])
        conv_s[0, NSS * c:NSS * (c + 1)] = cso.transpose(2, 3, 1, 0).reshape(NSS, 2, D)
        sso = np.asarray(R[c]["ssms_out"])
        t = sso.reshape(2, 64, NSS, 2, 64).transpose(2, 3, 4, 0, 1).reshape(NSS, 2, 128, 64)
        re_s[0, NSS * c:NSS * (c + 1)] = t[:, 0]
        im_s[0, NSS * c:NSS * (c + 1)] = t[:, 1]
        if r == 1:
            cpo = np.asarray(R[c]["convp_out"])
            conv_p[0, i] = cpo.transpose(2, 1, 0).reshape(2, D)
            spo = np.asarray(R[c]["ssmp_out"])
            t = spo.reshape(2, 64, 2, 64).transpose(2, 3, 0, 1).reshape(2, 128, 64)
            re_p[0, i] = t[0]
            im_p[0, i] = t[1]
    return (y_prompt, y_sample, conv_p, re_p, im_p, conv_s, re_s, im_s)
```
